# Optimizing a Trainium2 kernel written in Bass

```python
import jax, jax.numpy as jnp
from jax import lax
import numpy as np

D_MODEL = 1024
BATCH = 4
SEQ = 4096
DEPTH = 1
DEC_BATCH = 2
DEC_SEQ = 8192
PAST_LEN = 128

HEAD_DIM = 64
D_MIX = D_MODEL
D_RWKV = D_MIX // 2
D_ATTN = D_MIX - D_RWKV
N_RWKV_HEADS = D_RWKV // HEAD_DIM
N_Q_HEADS = D_ATTN // HEAD_DIM
N_KV_HEADS = 2
D_KV = N_KV_HEADS * HEAD_DIM
DECAY_LORA = 64
AAA_LORA = 64
GATE_LORA = 128
N_DIR = 2
RWKV_GN_EPS = 64e-5
SHIFT_SIZES = (D_RWKV, D_RWKV, D_RWKV, N_DIR * DECAY_LORA, N_DIR * AAA_LORA, GATE_LORA)
ATTN_SIZES = (D_ATTN, D_KV, D_KV)
D_SHIFT = 3 * D_RWKV + N_DIR * (DECAY_LORA + AAA_LORA) + GATE_LORA
D_IN = D_SHIFT + D_ATTN + 2 * D_KV
GRID_W = 64
ROPE_THETA = 10000.0
ROPE_AXIS_DIM = HEAD_DIM // 2
Q_BLOCK = 128
N_KEYS = 128
N_EXPERTS = N_KEYS * N_KEYS
PEER_HEADS = 8
PEER_TOPK = 16
PEER_DKEY = 256
PEER_BLOCK = 128
NORM_EPS = 1e-6

kernel_name = "hymba_rwkv7_axialgqa_peer_encoder"


def rms_norm(x, g):
    xf = x.astype(jnp.float32)
    y = xf * lax.rsqrt(jnp.mean(xf * xf, axis=-1, keepdims=True) + NORM_EPS)
    return (y * g.astype(jnp.float32)).astype(x.dtype)


def split_cols(z, sizes):
    outs, o = [], 0
    for s in sizes:
        outs.append(z[..., o:o + s])
        o += s
    return outs


def centred_shift(z):
    prev = jnp.pad(z[:, :-1], ((0, 0), (1, 0), (0, 0)))
    nxt = jnp.pad(z[:, 1:], ((0, 0), (0, 1), (0, 0)))
    return 0.5 * (prev + nxt) - z


def rwkv_scan(r, w, k, v, kk, a):
    def step(S, inp):
        r_t, w_t, k_t, v_t, kk_t, a_t = inp
        sa = jnp.einsum('dbhvk,dbhk->dbhv', S, -kk_t)
        S = S * w_t[..., None, :] + sa[..., None] * (kk_t * a_t)[..., None, :] + v_t[..., None] * k_t[..., None, :]
        y = jnp.einsum('dbhvk,dbhk->dbhv', S, r_t)
        return S, y
    T, Dd, B, H, N = r.shape
    S0 = jnp.zeros((Dd, B, H, N, N), jnp.float32)
    _, ys = lax.scan(step, S0, (r, w, k, v, kk, a))
    return ys


def rwkv7_bidir(p_r, p_k, p_v, p_w, p_a, p_g, w0, w2, a0, a2, g2, k_k, k_a, r_k, ln_w, ln_b):
    B, T, C = p_r.shape
    H, N = N_RWKV_HEADS, HEAD_DIM
    f32 = jnp.float32
    r = p_r.astype(f32)
    k = p_k.astype(f32)
    v = p_v.astype(f32)
    hw = jnp.tanh(p_w.astype(f32).reshape(B, T, N_DIR, DECAY_LORA))
    w_log = w0.astype(f32) + jnp.einsum('btdl,dlc->btdc', hw, w2.astype(f32))
    decay = jnp.exp(-jnp.exp(-jax.nn.softplus(-w_log) - 0.5))
    ha = p_a.astype(f32).reshape(B, T, N_DIR, AAA_LORA)
    a = jax.nn.sigmoid(a0.astype(f32) + jnp.einsum('btdl,dlc->btdc', ha, a2.astype(f32)))
    g = jax.nn.sigmoid(p_g.astype(f32)) @ g2.astype(f32)
    kk = (k * k_k.astype(f32)).reshape(B, T, H, N)
    kk = (kk / jnp.maximum(jnp.linalg.norm(kk, axis=-1, keepdims=True), 1e-12)).reshape(B, T, C)
    k_dir = k[:, :, None, :] * (1.0 + (a - 1.0) * k_a.astype(f32))

    def to_scan(z):
        if z.ndim == 3:
            z = jnp.broadcast_to(z[:, :, None, :], (B, T, N_DIR, C))
        z = jnp.stack([z[:, :, 0], jnp.flip(z[:, :, 1], axis=1)], axis=0)
        return z.reshape(N_DIR, B, T, H, N).transpose(2, 0, 1, 3, 4)

    ys = rwkv_scan(to_scan(r), to_scan(decay), to_scan(k_dir), to_scan(v), to_scan(kk), to_scan(a))
    y = (ys[:, 0] + jnp.flip(ys[:, 1], axis=0)).transpose(1, 0, 2, 3)
    mu = jnp.mean(y, axis=-1, keepdims=True)
    var = jnp.mean(jnp.square(y - mu), axis=-1, keepdims=True)
    y = ((y - mu) * lax.rsqrt(var + RWKV_GN_EPS)).reshape(B, T, C) * ln_w.astype(f32) + ln_b.astype(f32)
    k_b = jnp.mean(k_dir, axis=2).reshape(B, T, H, N)
    bonus = jnp.sum(r.reshape(B, T, H, N) * k_b * r_k.astype(f32), axis=-1, keepdims=True) * v.reshape(B, T, H, N)
    return (y + bonus.reshape(B, T, C)) * g


def rope_1d(x, ang):
    cos = jnp.cos(ang)[None, :, None, :].astype(x.dtype)
    sin = jnp.sin(ang)[None, :, None, :].astype(x.dtype)
    x1, x2 = jnp.split(x, 2, axis=-1)
    return jnp.concatenate([x1 * cos - x2 * sin, x1 * sin + x2 * cos], axis=-1)


def axial_rope(x, n_rows):
    inv_freq = ROPE_THETA ** (-jnp.arange(0, ROPE_AXIS_DIM, 2, dtype=jnp.float32) / ROPE_AXIS_DIM)
    rows = jnp.repeat(jnp.arange(n_rows, dtype=jnp.float32), GRID_W)
    cols = jnp.tile(jnp.arange(GRID_W, dtype=jnp.float32), n_rows)
    xr, xc = x[..., :ROPE_AXIS_DIM], x[..., ROPE_AXIS_DIM:]
    return jnp.concatenate([rope_1d(xr, rows[:, None] * inv_freq), rope_1d(xc, cols[:, None] * inv_freq)], axis=-1)


def gqa_axial(q, k, v, q_g, k_g, out_g):
    B, T, _ = q.shape
    n_rows = T // GRID_W
    G = N_Q_HEADS // N_KV_HEADS
    nblk = T // Q_BLOCK
    scale = HEAD_DIM ** -0.5
    q = axial_rope(rms_norm(q.reshape(B, T, N_Q_HEADS, HEAD_DIM), q_g), n_rows)
    k = axial_rope(rms_norm(k.reshape(B, T, N_KV_HEADS, HEAD_DIM), k_g), n_rows)
    v = v.reshape(B, T, N_KV_HEADS, HEAD_DIM)
    qb = q.reshape(B, nblk, Q_BLOCK, N_KV_HEADS, G, HEAD_DIM).transpose(1, 0, 3, 4, 2, 5)
    kt = k.transpose(0, 2, 1, 3)
    vt = v.transpose(0, 2, 1, 3)

    def block(qx):
        s = jnp.einsum('bkgqd,bkld->bkgql', qx, kt).astype(jnp.float32) * scale
        p = jax.nn.softmax(s, axis=-1).astype(vt.dtype)
        return jnp.einsum('bkgql,bkld->bkgqd', p, vt)

    o = lax.map(block, qb)
    o = o.transpose(1, 0, 4, 2, 3, 5).reshape(B, T, N_Q_HEADS, HEAD_DIM)
    o = rms_norm(o, out_g.reshape(N_Q_HEADS, HEAD_DIM))
    return o.reshape(B, T, D_ATTN)


def peer(h, wq, sub_keys, u_tab, v_tab):
    B, T, D = h.shape
    nb = T // PEER_BLOCK
    hb = h.reshape(B, nb, PEER_BLOCK, D).transpose(1, 0, 2, 3)

    def block(hx):
        q = (hx @ wq).reshape(B, PEER_BLOCK, PEER_HEADS, 2, PEER_DKEY // 2)
        s = jnp.einsum('bqhpd,pnd->bqhpn', q, sub_keys).astype(jnp.float32)
        sv, si = lax.top_k(s, PEER_TOPK)
        cand = (sv[..., 0, :, None] + sv[..., 1, None, :]).reshape(B, PEER_BLOCK, PEER_HEADS, PEER_TOPK * PEER_TOPK)
        cidx = (si[..., 0, :, None] * N_KEYS + si[..., 1, None, :]).reshape(B, PEER_BLOCK, PEER_HEADS, PEER_TOPK * PEER_TOPK)
        tv, ti = lax.top_k(cand, PEER_TOPK)
        eidx = jnp.take_along_axis(cidx, ti, axis=-1).reshape(B, PEER_BLOCK, PEER_HEADS * PEER_TOPK)
        gate = jax.nn.softmax(tv, axis=-1).reshape(B, PEER_BLOCK, PEER_HEADS * PEER_TOPK)
        u = u_tab[eidx]
        act = jax.nn.gelu(jnp.einsum('bqed,bqd->bqe', u, hx).astype(jnp.float32), approximate=False)
        vv = v_tab[eidx]
        return jnp.einsum('bqe,bqed->bqd', (gate * act).astype(hx.dtype), vv)

    out = lax.map(block, hb)
    return out.transpose(1, 0, 2, 3).reshape(B, T, D)


def encoder_layer(x, c, ada_w, ada_b, norm1_g, norm2_g, w_in, mu_shift, w0, w2, a0, a2, g2, k_k, k_a, r_k,
                  ln_x_w, ln_x_b, q_norm_g, k_norm_g, attn_out_g, w_out, peer_wq, peer_sub_keys, peer_u, peer_v):
    mod = (jax.nn.silu(c.astype(jnp.float32)) @ ada_w.astype(jnp.float32) + ada_b.astype(jnp.float32)).astype(x.dtype)
    sh1, sc1, gt1, sh2, sc2, gt2 = [m[:, None, :] for m in jnp.split(mod, 6, axis=-1)]
    h = rms_norm(x, norm1_g) * (1 + sc1) + sh1
    proj = h @ w_in
    p_shift, p_attn = proj[..., :D_SHIFT], proj[..., D_SHIFT:]
    p_shift = p_shift + mu_shift * centred_shift(p_shift)
    p_r, p_k, p_v, p_w, p_a, p_g = split_cols(p_shift, SHIFT_SIZES)
    q, k, v = split_cols(p_attn, ATTN_SIZES)
    y_rwkv = rwkv7_bidir(p_r, p_k, p_v, p_w, p_a, p_g, w0, w2, a0, a2, g2, k_k, k_a, r_k, ln_x_w, ln_x_b)
    y_attn = gqa_axial(q, k, v, q_norm_g, k_norm_g, attn_out_g)
    mix = jnp.concatenate([y_rwkv.astype(x.dtype), y_attn], axis=-1) @ w_out
    x = x + gt1 * mix
    h2 = rms_norm(x, norm2_g) * (1 + sc2) + sh2
    x = x + gt2 * peer(h2, peer_wq, peer_sub_keys, peer_u, peer_v)
    return x


def setup_inputs(seed: int = 0) -> dict:
    key = jax.random.key(seed)
    ks = iter(jax.random.split(key, 40))
    f32 = jnp.float32

    def nrm(shape, s):
        return jax.random.normal(next(ks), shape, f32) * s

    return {
        "x_prompt": nrm((BATCH, SEQ, D_MODEL), 1.0),
        "x_sample": nrm((DEC_BATCH, DEC_SEQ, D_MODEL), 1.0),
        "c_prompt": nrm((BATCH, D_MODEL), 1.0),
        "c_sample": nrm((DEC_BATCH, D_MODEL), 1.0),
        "ada_w": nrm((DEPTH, D_MODEL, 6 * D_MODEL), 0.2 * D_MODEL ** -0.5),
        "ada_b": nrm((DEPTH, 6 * D_MODEL), 0.01),
        "norm1_g": 1.0 + nrm((DEPTH, D_MODEL), 0.02),
        "norm2_g": 1.0 + nrm((DEPTH, D_MODEL), 0.02),
        "w_in": nrm((DEPTH, D_MODEL, D_IN), D_MODEL ** -0.5),
        "mu_shift": jax.random.uniform(next(ks), (DEPTH, D_SHIFT), f32),
        "w0": jax.random.uniform(next(ks), (DEPTH, N_DIR, D_RWKV), f32, -5.0, 1.0),
        "w2": nrm((DEPTH, N_DIR, DECAY_LORA, D_RWKV), 0.1 * DECAY_LORA ** -0.5),
        "a0": nrm((DEPTH, N_DIR, D_RWKV), 0.1),
        "a2": nrm((DEPTH, N_DIR, AAA_LORA, D_RWKV), 0.1 * AAA_LORA ** -0.5),
        "g2": nrm((DEPTH, GATE_LORA, D_RWKV), GATE_LORA ** -0.5),
        "k_k": 0.85 + nrm((DEPTH, D_RWKV), 0.02),
        "k_a": 1.0 + nrm((DEPTH, D_RWKV), 0.02),
        "r_k": nrm((DEPTH, N_RWKV_HEADS, HEAD_DIM), 0.1),
        "ln_x_w": 1.0 + nrm((DEPTH, D_RWKV), 0.02),
        "ln_x_b": nrm((DEPTH, D_RWKV), 0.01),
        "q_norm_g": 1.0 + nrm((DEPTH, HEAD_DIM), 0.02),
        "k_norm_g": 1.0 + nrm((DEPTH, HEAD_DIM), 0.02),
        "attn_out_g": 1.0 + nrm((DEPTH, D_ATTN), 0.02),
        "w_out": nrm((DEPTH, D_MIX, D_MODEL), D_MIX ** -0.5),
        "peer_wq": nrm((DEPTH, D_MODEL, PEER_HEADS * PEER_DKEY), D_MODEL ** -0.5),
        "peer_sub_keys": nrm((DEPTH, 2, N_KEYS, PEER_DKEY // 2), (PEER_DKEY // 2) ** -0.5),
        "peer_u": nrm((DEPTH, N_EXPERTS, D_MODEL), D_MODEL ** -0.5),
        "peer_v": nrm((DEPTH, N_EXPERTS, D_MODEL), PEER_HEADS ** -0.5),
    }


def reference(x_prompt, x_sample, c_prompt, c_sample, ada_w, ada_b, norm1_g, norm2_g, w_in, mu_shift, w0, w2, a0, a2,
              g2, k_k, k_a, r_k, ln_x_w, ln_x_b, q_norm_g, k_norm_g, attn_out_g, w_out, peer_wq, peer_sub_keys,
              peer_u, peer_v):
    y_p, y_s = x_prompt, x_sample
    for l in range(DEPTH):
        params_l = (ada_w[l], ada_b[l], norm1_g[l], norm2_g[l], w_in[l], mu_shift[l], w0[l], w2[l], a0[l], a2[l],
                    g2[l], k_k[l], k_a[l], r_k[l], ln_x_w[l], ln_x_b[l], q_norm_g[l], k_norm_g[l], attn_out_g[l],
                    w_out[l], peer_wq[l], peer_sub_keys[l], peer_u[l], peer_v[l])
        y_p = encoder_layer(y_p, c_prompt, *params_l)
        y_s = encoder_layer(y_s, c_sample, *params_l)
    return (y_p, y_s)
```

```python
import numpy as np
from contextlib import ExitStack
import concourse.bass as bass
import concourse.mybir as mybir
from concourse.bass_utils import run_bass_kernel_spmd

F32 = mybir.dt.float32
BF16 = mybir.dt.bfloat16
I32 = mybir.dt.int32
U32 = mybir.dt.uint32
ALU = mybir.AluOpType
AF = mybir.ActivationFunctionType
AX = mybir.AxisListType

class Buf:
    __slots__ = ("t", "lw", "rd", "name", "psum")

    def __init__(self, t, name=""):
        self.t = t
        self.lw = None
        self.rd = []
        self.name = name
        self.psum = False

    def __getitem__(self, k):
        return self.t[k]


class Eng:
    def __init__(self, P, name, eng, kind):
        self.P = P
        self.name = name
        self.eng = eng
        self.kind = kind
        self.seen = {}
        self.sem = P.newsem(name)
        self.cnt = 0
        self.pool = []
        self.ndma = 0


class Op:
    __slots__ = ("id", "eng", "fn", "kind", "cost", "preds", "tag")


class Prog:
    K = 12
    HOP = 2000.0
    SELF = 150.0

    def __init__(self, nc, ctx):
        self.nc = nc
        self.ctx = ctx
        self.sems = {}
        self.pe = Eng(self, "pe", nc.tensor, "c")
        self.dve = Eng(self, "dve", nc.vector, "c")
        self.act = Eng(self, "act", nc.scalar, "c")
        self.pool = Eng(self, "pool", nc.gpsimd, "c")
        self.sp = Eng(self, "sp", nc.sync, "c")
        self.engs = (self.pe, self.dve, self.act, self.pool, self.sp)
        for e in (self.sp, self.act, self.pool):
            e.pool = [self.newsem(f"{e.name}_d{i}") for i in range(self.K)]
        self.nins = 0
        self.ops = []
        self.nid = 0
        self.base = 0
        self.reorder = REORDER

    def newsem(self, name):
        s = self.ctx.enter_context(self.nc.semaphore(name))
        self.sems[name] = s
        return name

    def sb(self, name, shape, dt=F32):
        self._uid = getattr(self, "_uid", 0) + 1
        t = self.ctx.enter_context(self.nc.sbuf_tensor(f"s{self._uid}_" + name, list(shape), dt))
        return Buf(t, name)

    def ps(self, name, shape, dt=F32):
        t = self.ctx.enter_context(self.nc.psum_tensor("p_" + name, list(shape), dt))
        b = Buf(t, name)
        b.psum = True
        return b

    def dram(self, name, shape, dt=F32, kind="Internal"):
        t = self.nc.dram_tensor(name, list(shape), dt, kind=kind)
        return Buf(t, name)

    def _record(self, E, fn, R, W, kind, cost):
        W = list(W) + [b for b in R if b.psum]
        R = [b for b in R if not b.psum]
        op = Op()
        op.id = self.nid
        self.nid += 1
        op.eng, op.fn, op.kind, op.cost, op.tag = E, fn, kind, cost, None
        preds = set()
        base = self.base
        for b in R:
            if b.lw is not None and b.lw >= base:
                preds.add(b.lw)
        for b in W:
            if b.lw is not None and b.lw >= base:
                preds.add(b.lw)
            for r in b.rd:
                if r >= base:
                    preds.add(r)
        op.preds = preds
        for b in W:
            b.lw = op.id
            b.rd = []
        for b in R:
            b.rd.append(op.id)
        self.ops.append(op)
        self.nins += 1

    def I(self, E, fn, R=(), W=(), cost=250.0):
        self._record(E, fn, R, W, "I", cost)

    def D(self, Q, fn, R=(), W=(), cost=3000.0):
        self._record(Q, fn, R, W, "D", cost)

    def _schedule(self, ops):
        import heapq
        n = len(ops)
        base = self.base
        succs = [[] for _ in range(n)]
        indeg = [0] * n
        for k, op in enumerate(ops):
            for p in op.preds:
                succs[p - base].append(k)
                indeg[k] += 1
        if not self.reorder:
            return list(range(n))
        future = {e.name: [] for e in self.engs}
        avail = {e.name: [] for e in self.engs}
        free = {e.name: 0.0 for e in self.engs}
        finish = [0.0] * n
        rtime = [0.0] * n
        bl = [0.0] * n
        for k in range(n - 1, -1, -1):
            op = ops[k]
            m_ = 0.0
            for s_ in succs[k]:
                lat = self.SELF if ops[s_].eng is op.eng else self.HOP
                v = lat + bl[s_]
                if v > m_:
                    m_ = v
            bl[k] = m_ + (120.0 if op.kind == "D" else op.cost)
        PRI = PRIORITY
        for k in range(n):
            if indeg[k] == 0:
                heapq.heappush(avail[ops[k].eng.name], ((-bl[k], k) if PRI else (k, k)))
        order = []
        while len(order) < n:
            best = None
            for e in self.engs:
                nm = e.name
                fu, av = future[nm], avail[nm]
                while fu and fu[0][0] <= free[nm]:
                    k_ = heapq.heappop(fu)[1]
                    heapq.heappush(av, ((-bl[k_], k_) if PRI else (k_, k_)))
                if av:
                    cand = (free[nm], av[0][1], nm, True)
                elif fu:
                    cand = (fu[0][0], fu[0][1], nm, False)
                else:
                    continue
                if best is None or cand[:2] < best[:2]:
                    best = cand
            start, k, nm, from_av = best
            if from_av:
                heapq.heappop(avail[nm])
            else:
                heapq.heappop(future[nm])
            op = ops[k]
            if op.kind == "D":
                free[nm] = start + 120.0
                finish[k] = start + op.cost
            else:
                free[nm] = start + op.cost
                finish[k] = free[nm]
            order.append(k)
            for s_ in succs[k]:
                lat = self.SELF if ops[s_].eng is op.eng else self.HOP
                t_ = finish[k] + lat
                if t_ > rtime[s_]:
                    rtime[s_] = t_
                indeg[s_] -= 1
                if indeg[s_] == 0:
                    heapq.heappush(future[ops[s_].eng.name], (rtime[s_], s_))
        return order

    def flush(self):
        ops = self.ops
        if not ops:
            return
        order = self._schedule(ops)
        base = self.base
        for k in order:
            op = ops[k]
            E = op.eng
            need = {}
            for p in op.preds:
                po = ops[p - base]
                if po.eng is E and E is self.pe:
                    continue
                key, val = po.tag
                if need.get(key, 0) < val:
                    need[key] = val
            for key, val in need.items():
                if E.seen.get(key, 0) < val:
                    E.eng.wait_ge(self.sems[key], val)
                    E.seen[key] = val
            if op.kind == "I":
                ins = op.fn(E.eng)
                E.cnt += 1
                ins.then_inc(self.sems[E.sem], 1)
                op.tag = (E.sem, E.cnt)
            else:
                j = E.ndma
                sname = E.pool[j % self.K]
                prev = 16 * (j // self.K)
                if prev > 0 and E.seen.get(sname, 0) < prev:
                    E.eng.wait_ge(self.sems[sname], prev)
                    E.seen[sname] = prev
                ins = op.fn(E.eng)
                ins.then_inc(self.sems[sname], 16)
                E.ndma += 1
                op.tag = (sname, prev + 16)
            op.fn = None
        self.base = self.nid
        self.ops = []

    def push(self):
        self._saved = getattr(self, "_saved", [])
        self._saved.append(self.ctx)
        self.ctx = ExitStack()
        self.ctx.__enter__()

    def barrier(self):
        self.flush()
        engs = self.engs
        for E in engs:
            for X in engs:
                if X is not E and X.cnt > 0 and E.seen.get(X.sem, 0) < X.cnt:
                    E.eng.wait_ge(self.sems[X.sem], X.cnt)
                    E.seen[X.sem] = X.cnt
            for Q in (self.sp, self.act, self.pool):
                for i, sname in enumerate(Q.pool):
                    n = (Q.ndma - i + self.K - 1) // self.K if Q.ndma > i else 0
                    if n > 0 and E.seen.get(sname, 0) < 16 * n:
                        E.eng.wait_ge(self.sems[sname], 16 * n)
                        E.seen[sname] = 16 * n

    def pop(self):
        self.barrier()
        self.ctx.__exit__(None, None, None)
        self.ctx = self._saved.pop()

    def finish(self, bufs):
        self.barrier()


T_OWN = 4096
REORDER = True
PRIORITY = True
NT = 32
D = 1024
LOGC = -0.6065306597126334
NEG = -100.0


def build_program(dbg=(), stages=(), small=False):
    nc = bass.Bass("TRN2", target_bir_lowering=False)

    def din(name, shape, dt=F32):
        return nc.dram_tensor(name, list(shape), dt, kind="ExternalInput").ap()

    xs = din("xs", [3, 33 * 128, D])
    flags_d = din("flags", [128, 16])
    ccol_d = din("ccol", [128, 8])
    ada_w = din("ada_w", [D, 6 * D])
    ada_b = din("ada_b", [1, 6 * D])
    g1col_d = din("g1col", [128, 8])
    g2col_d = din("g2col", [128, 8])
    g2row_d = din("g2row", [1, D])
    NCOL = [2688, 1664, 1408]
    wst_d = [din(f"wst{s}", [D, NCOL[s]]) for s in range(3)]
    mu_d = [din(f"mu{s}", [1, NCOL[s]]) for s in range(3)]
    w2a2_d = din("w2a2", [4, 2, 128, 512])
    w0a0_d = din("w0a0", [4, 2, 512])
    g2lora_d = din("g2lora", [128, 512])
    vecs_d = din("vecs", [8, 512])
    rope_d = din("rope", [3, 4096, 64])
    w_out_d = din("w_out", [D, D])
    wq_d = din("wq", [D, 2048])
    skT_d = din("skT", [2, 128, 128])
    NEXP = 512 if small else 16384
    u_tab = din("u_tab", [NEXP, D])
    v_tab = din("v_tab", [NEXP, D])
    consts_d = din("consts", [128, 2048])

    y_out = nc.dram_tensor("y_out", [T_OWN, D], F32, kind="ExternalOutput").ap()
    dbg_t = {}
    for name, shape in dbg:
        dbg_t[name] = nc.dram_tensor("dbg_" + name, list(shape), F32, kind="ExternalOutput").ap()

    ctx = ExitStack()
    with ctx:
        P = Prog(nc, ctx)
        L2 = _emit(P, nc, locals())
        for kv in stages:
            if isinstance(kv, tuple):
                L2[kv[0]] = kv[1]
        if "stop0" not in stages:
            _emit_streams(P, nc, L2)
        if "attn" in stages or not stages:
            _emit_attention(P, nc, L2)
        if "final" in stages or not stages:
            _emit_final(P, nc, L2)
            P.finish([L2["yo"]])
        else:
            P.finish([])
    return nc


def _emit(P, nc, L):
    xs, flags_d, ccol_d, ada_w, ada_b = L["xs"], L["flags_d"], L["ccol_d"], L["ada_w"], L["ada_b"]
    dbg_t = L["dbg_t"]
    pe, dve, act, pool, sp = P.pe, P.dve, P.act, P.pool, P.sp

    def fsz(ap):
        try:
            return float(ap.free_size())
        except Exception:
            return 256.0

    def mm(ob, o, lb, l, rb, r, start=True, stop=True):
        c = 64.0 + fsz(r) / 1.4
        if l.dtype == F32:
            c *= 4
        return P.I(pe, lambda e: e.matmul(o, lhsT=l, rhs=r, start=start, stop=stop), R=[lb, rb], W=[ob], cost=c)

    def tr(ob, o, ib, i, idb, idap):
        c = 64.0 + fsz(i) / 1.4
        if i.dtype == F32:
            c *= 4
        return P.I(pe, lambda e: e.transpose(o, i, idap), R=[ib, idb], W=[ob], cost=c)

    def vcost(E, o):
        if E is pool:
            return 150.0 + fsz(o) / 0.7
        return 80.0 + fsz(o) / 0.96

    def tt(E, ob, o, ab, a, bb, b, op):
        return P.I(E, lambda e: e.tensor_tensor(out=o, in0=a, in1=b, op=op), R=[ab, bb], W=[ob], cost=vcost(E, o))

    def ts(E, ob, o, ab, a, s1, s2, op0, op1=None, sb=()):
        if op1 is None:
            return P.I(E, lambda e: e.tensor_scalar(out=o, in0=a, scalar1=s1, scalar2=None, op0=op0), R=[ab, *sb], W=[ob], cost=vcost(E, o))
        return P.I(E, lambda e: e.tensor_scalar(out=o, in0=a, scalar1=s1, scalar2=s2, op0=op0, op1=op1), R=[ab, *sb], W=[ob], cost=vcost(E, o))

    def stt(ob, o, ab, a, sc, bb, b, op0, op1, sb=()):
        return P.I(dve, lambda e: e.scalar_tensor_tensor(out=o, in0=a, scalar=sc, in1=b, op0=op0, op1=op1), R=[ab, bb, *sb], W=[ob],
                   cost=vcost(dve, o))

    def actf(ob, o, ib, i, func, bias=0.0, scale=1.0, sb=(), accum=None, accb=()):
        def f(e):
            kw = dict(out=o, in_=i, func=func, bias=bias, scale=scale)
            if accum is not None:
                kw["accum_out"] = accum
            return e.activation(**kw)
        return P.I(act, f, R=[ib, *sb], W=[ob, *accb], cost=220.0 + fsz(o) / 1.4)

    def rsqrt(ob, o, ib, i):
        P.I(act, lambda e: e.activation(out=o, in_=i, func=AF.Sqrt), R=[ib], W=[ob], cost=220.0 + fsz(o) / 1.4)
        P.I(dve, lambda e: e.reciprocal(out=o, in_=o), R=[ob], W=[ob], cost=vcost(dve, o))

    def cp(E, ob, o, ib, i):
        if E is act:
            return P.I(E, lambda e: e.copy(out=o, in_=i), R=[ib], W=[ob], cost=220.0 + fsz(o) / 1.4)
        return P.I(E, lambda e: e.tensor_copy(out=o, in_=i), R=[ib], W=[ob], cost=vcost(E, o))

    def dma(Q, ob, o, ib, i):
        return P.D(Q, lambda e: e.dma_start(out=o, in_=i), R=[ib] if ib is not None else [], W=[ob] if ob is not None else [],
                   cost=2500.0 + fsz(o) * 128 * 4 / 150.0)

    DR = Buf(None, "dram_in")

    def dump(name, buf, ap):
        if name in dbg_t:
            dma(sp, Buf(None), dbg_t[name], buf, ap)

    cst = P.sb("cst", [128, 2048])
    dma(sp, cst, cst[:], None, L["consts_d"])
    identf = cst[:, 0:128]
    identb_t = P.sb("identb", [128, 256], BF16)
    cp(dve, identb_t, identb_t[:], cst, cst[:, 0:256])
    identb = identb_t[:, 0:128]
    flags = P.sb("flags", [128, 16])
    dma(sp, flags, flags[:], None, flags_d)
    ones1 = P.sb("ones1", [1, 128])
    P.I(dve, lambda e: e.memset(ones1[:], 1.0), W=[ones1])

    pps = [P.ps(f"pp{i}", [128, 512]) for i in range(7)]
    ppi = [0]

    npool = [7]

    def nps():
        b = pps[ppi[0] % npool[0]]
        ppi[0] += 1
        return b
    tpb = P.ps("tpb", [128, 1024], BF16)

    gs = P.sb("gs", [128, 16])
    modcol = P.sb("modcol", [128, 32])
    rep_d = P.dram("rep_d", [4, 128, D])
    P.push()
    cT = P.sb("cT", [128, 8])
    dma(sp, cT, cT[:], None, ccol_d)
    sT = P.sb("sT", [128, 8])
    actf(sT, sT[:], cT, cT[:], AF.Silu)
    adab = P.sb("adab", [1, 6 * D])
    dma(sp, adab, adab[:], None, ada_b)
    modrow = P.sb("modrow", [1, 6 * D])
    awv = ada_w.rearrange("(j p) n -> p j n", p=128)
    stg = [P.sb(f"stg{i}", [128, 4096]) for i in range(2)]
    for g in range(12):
        b = stg[g % 2]
        bv = b[:].rearrange("p (j n) -> p j n", j=8)
        dma(sp, b, bv, None, awv[:, :, g * 512:(g + 1) * 512])
        ps = nps()
        for j in range(8):
            mm(ps, ps[0:1, :], sT, sT[:, j:j + 1], b, bv[:, j, :], start=(j == 0), stop=(j == 7))
        tt(dve, modrow, modrow[0:1, g * 512:(g + 1) * 512], ps, ps[0:1, :], adab, adab[0:1, g * 512:(g + 1) * 512], ALU.add)
    dump("modrow", modrow, modrow[:])
    ps = nps()
    for pi, off in enumerate((0, 1024, 3072, 4096)):
        for j in range(8):
            mm(ps, ps[:, pi * 8 + j:pi * 8 + j + 1], modrow, modrow[0:1, off + j * 128:off + (j + 1) * 128],
               ones1, ones1[0:1, 0:1])
    cp(dve, modcol, modcol[:], ps, ps[:, 0:32])
    gcol = P.sb("gcol", [128, 16])
    dma(sp, gcol, gcol[:, 0:8], None, L["g1col_d"])
    dma(sp, gcol, gcol[:, 8:16], None, L["g2col_d"])
    stt(gs, gs[:, 0:8], modcol, modcol[:, 8:16], 1.0, gcol, gcol[:, 0:8], ALU.add, ALU.mult)
    stt(gs, gs[:, 8:16], modcol, modcol[:, 24:32], 1.0, gcol, gcol[:, 8:16], ALU.add, ALU.mult)
    g2rep = P.sb("g2rep", [128, D])
    dma(sp, g2rep, g2rep[:], None, L["g2row_d"].partition_broadcast(128))
    for ri, (name, off) in enumerate((("gt1", 2048), ("gt2", 5120), ("sc2", 4096), ("sh2", 3072))):
        t = stg[ri % 2]
        for hf in range(2):
            ps = nps()
            mm(ps, ps[:, :], ones1, ones1[0:1, :], modrow, modrow[0:1, off + hf * 512:off + (hf + 1) * 512])
            cp(act, t, t[:, hf * 512:(hf + 1) * 512], ps, ps[:, :])
        if name == "sc2":
            stt(t, t[:, 0:D], t, t[:, 0:D], 1.0, g2rep, g2rep[:], ALU.add, ALU.mult)
        dma(sp, rep_d, rep_d[ri], t, t[:, 0:D])
    dump("gs", gs, gs[:])
    P.pop()
    L2 = dict(L)
    L2.update(locals())
    return L2


def _consts():
    c = np.zeros((128, 2048), np.float32)
    i = np.arange(128)
    c[:, 0:128] = np.eye(128)
    c[:, 128:256] = np.eye(128)[::-1]
    same = (i[:, None] // 64) == (i[None, :] // 64)
    s_le_t = same & (i[:, None] <= i[None, :])
    s_lt_t = same & (i[:, None] < i[None, :])
    s_gt_t = same & (i[:, None] > i[None, :])
    c[:, 256:384] = LOGC * s_le_t
    c[:, 384:512] = LOGC * s_gt_t
    c[:, 512:640] = s_lt_t
    c[:, 640:768] = s_le_t
    c[:, 768:896] = s_gt_t
    c[:, 896:1024] = s_lt_t
    c[:, 1024:1152] = s_le_t
    c[:, 1152] = LOGC
    c[:, 1153] = 1.0
    c[:, 1280:1296] = np.arange(16)[None, :]
    return c


def host_layout(inp):
    f = np.float32
    w_in = inp["w_in"][0]
    mu = inp["mu_shift"][0]
    sl = lambda a, b: list(range(a, b))
    r_, k_, v_ = sl(0, 512), sl(512, 1024), sl(1024, 1536)
    wl = [sl(1536, 1600), sl(1600, 1664)]
    al = [sl(1664, 1728), sl(1728, 1792)]
    gl, q_, ak, av = sl(1792, 1920), sl(1920, 2432), sl(2432, 2560), sl(2560, 2688)
    mu_ext = np.concatenate([mu, np.zeros(768, f)])
    cols_a = r_ + k_ + v_ + wl[0] + al[0] + wl[1] + al[1] + gl + q_ + ak + av
    cols_b = r_ + k_ + v_ + wl[1] + al[1]
    w2, a2, w0, a0 = inp["w2"][0], inp["a2"][0], inp["w0"][0], inp["a0"][0]
    vecs = np.zeros((8, 512), f)
    vecs[0] = inp["k_k"][0]; vecs[1] = inp["k_a"][0]; vecs[2] = inp["ln_x_w"][0]; vecs[3] = inp["ln_x_b"][0]
    vecs[4] = inp["r_k"][0].reshape(-1); vecs[5] = inp["attn_out_g"][0]
    vecs[6] = np.tile(inp["q_norm_g"][0], 8); vecs[7] = np.tile(inp["k_norm_g"][0], 8)
    inv_freq = (10000.0 ** (-np.arange(0, 32, 2, dtype=f) / 32.0)).astype(f)

    def rope_tab(pos):
        rows = (pos // 64).astype(f); colsp = (pos % 64).astype(f)
        ar = rows[:, None] * inv_freq[None, :]; ac = colsp[:, None] * inv_freq[None, :]
        return np.concatenate([np.cos(ar), np.cos(ac), np.sin(ar), np.sin(ac)], axis=1).astype(f)

    consts = _consts()
    shared = dict(
        ada_w=inp["ada_w"][0], ada_b=inp["ada_b"], g1col=inp["norm1_g"][0].reshape(8, 128).T.copy(),
        g2col=inp["norm2_g"][0].reshape(8, 128).T.copy(), g2row=inp["norm2_g"],
        g2lora=inp["g2"][0], vecs=vecs, w_out=inp["w_out"][0], wq=inp["peer_wq"][0],
        skT=np.ascontiguousarray(inp["peer_sub_keys"][0].transpose(0, 2, 1)),
        u_tab=inp["peer_u"][0], v_tab=inp["peer_v"][0], consts=consts,
        wst0=np.ascontiguousarray(w_in[:, cols_a]), mu0=mu_ext[cols_a][None, :].copy(),
        wst1=np.ascontiguousarray(w_in[:, cols_b]), mu1=mu_ext[cols_b][None, :].copy(),
    )
    wc_cache = {}
    maps = []
    for core in range(8):
        xs = np.zeros((3, 33 * 128, D), f)
        fl = np.zeros((128, 16), f)
        if core < 4:
            xfull = inp["x_prompt"][core]; base = 0; c = inp["c_prompt"][core]; dc = 0
            own = xfull
            fl[:, 8] = NEG
            pos_c = np.zeros(4096, np.int64)
            xs[2, :4096] = xfull
        else:
            s, half = (core - 4) // 2, (core - 4) % 2
            xfull = inp["x_sample"][s]; base = half * 4096; c = inp["c_sample"][s]
            own = xfull[base:base + 4096]
            if half == 0:
                dc = 1
                ctxs = xfull[4096:8192][::-1]; pos_c = np.arange(8191, 4095, -1)
                xs[0, 4097] = xfull[4096]; fl[:, 1] = 1
                xs[1, 4096] = xfull[4096]; fl[:, 2] = 1
                xs[2, 4097] = xfull[4095]; fl[:, 5] = 1
                fl[:, 7] = 1
            else:
                dc = 0
                ctxs = xfull[0:4096]; pos_c = np.arange(0, 4096)
                xs[0, 4096] = xfull[4095]; fl[:, 0] = 1
                xs[1, 4097] = xfull[4095]; fl[:, 3] = 1
                xs[2, 4097] = xfull[4096]; fl[:, 5] = 1
                fl[:, 6] = 1
            xs[2, :4096] = ctxs
        xs[0, :4096] = own
        xs[1, :4096] = own[::-1]
        if dc not in wc_cache:
            cols_c = k_ + v_ + wl[dc] + al[dc] + ak + av
            wc_cache[dc] = (np.ascontiguousarray(w_in[:, cols_c]), mu_ext[cols_c][None, :].copy())
        w2a2 = np.zeros((4, 2, 128, 512), f); w0a0 = np.zeros((4, 2, 512), f)
        for slot, d in enumerate((0, 1, 1, dc)):
            w2a2[slot, 0, :64] = w2[d]; w2a2[slot, 1, 64:] = a2[d]; w0a0[slot, 0] = w0[d]; w0a0[slot, 1] = a0[d]
        rope = np.stack([rope_tab(base + np.arange(4096)), rope_tab(base + np.arange(4096)), rope_tab(pos_c)])
        m = dict(shared)
        m.update(xs=xs, flags=fl, ccol=c.reshape(8, 128).T.copy(), wst2=wc_cache[dc][0], mu2=wc_cache[dc][1],
                 w2a2=w2a2, w0a0=w0a0, rope=rope)
        maps.append(m)
    return maps


def kernel(**inputs):
    inp = {k: np.asarray(v) for k, v in inputs.items()}
    maps = host_layout(inp)
    nc = build_program()
    res = run_bass_kernel_spmd(nc, maps, core_ids=list(range(8)))
    outs = [np.asarray(r["y_out"], dtype=np.float32) for r in res.results]
    y_p = np.stack(outs[0:4])
    y_s = np.stack([np.concatenate([outs[4], outs[5]]), np.concatenate([outs[6], outs[7]])])
    return (y_p, y_s)


def _emit_streams(P, nc, L):
    pe, dve, act, pool, sp = P.pe, P.dve, P.act, P.pool, P.sp
    mm, tr, tt, ts, stt, actf, cp, dma, dump, rsqrt = (L[k] for k in ("mm", "tr", "tt", "ts", "stt", "actf", "cp", "dma", "dump", "rsqrt"))
    nps, tpb, cst, identb, identf, flags, gs, modcol = (L[k] for k in (
        "nps", "tpb", "cst", "identb", "identf", "flags", "gs", "modcol"))
    xs, NCOL, dbg_t = L["xs"], L["NCOL"], L["dbg_t"]
    sb = P.sb
    MASKA, MASKB = cst[:, 512:896], cst[:, 896:1152]
    UTs, SLs, negc = cst[:, 256:384], cst[:, 384:512], cst[:, 1152:1153]
    WG = 256

    Hctx = sb("Hctx", [64, 8, 64])
    yscr = [P.dram(f"yscr{i}", [T_OWN, 512]) for i in range(2)]
    qT_d = P.dram("qT_d", [128, 4, T_OWN], BF16)
    kT_d = P.dram("kT_d", [128, 2, 2 * T_OWN], BF16)
    vx_d = P.dram("vx_d", [128, 64, 130], BF16)
    gate_d = P.dram("gate_d", [T_OWN, 512])
    bon_d = P.dram("bon_d", [T_OWN, 512])
    L.update(yscr=yscr, qT_d=qT_d, kT_d=kT_d, vx_d=vx_d, gate_d=gate_d, bon_d=bon_d)

    cfgs = {
        "c": dict(si=2, slot=3, need_y=False, nshift=1152, offs=dict(k=0, v=512, lora=1024, akv=1152)),
        "a": dict(si=0, slot=0, need_y=True, nshift=1920,
                  offs=dict(r=0, k=512, v=1024, lora=1536, lora2=1664, gl=1792, q=1920, akv=2432)),
        "b": dict(si=1, slot=2, need_y=True, nshift=1664, offs=dict(r=0, k=512, v=1024, lora=1536)),
    }
    order = L.get("stream_order", ("c", "a", "b"))
    for sname in order:
        cf = cfgs[sname]
        si, offs, ncol, nsh, need_y, slot = cf["si"], cf["offs"], NCOL[cf["si"]], cf["nshift"], cf["need_y"], cf["slot"]
        is_a = sname == "a"
        P.push()
        W1 = sb("W1", [128, 8, ncol], BF16)
        W2 = sb("W2", [128, 8, nsh], BF16)
        P.push()
        murep = sb("murep", [128, ncol])
        dma(sp, murep, murep[:], None, L["mu_d"][si].partition_broadcast(128))
        wstg = [sb(f"wstg{i}", [128, 1024]) for i in range(2)]
        wtmp = sb("wtmp", [128, 1024])
        wv = L["wst_d"][si].rearrange("(j p) n -> p j n", p=128)
        k = 0
        for j in range(8):
            for p0 in range(0, ncol, 1024):
                n = min(1024, ncol - p0)
                b = wstg[k % 2]; k += 1
                dma(sp, b, b[:, 0:n], None, wv[:, j, p0:p0 + n])
                tt(dve, wtmp, wtmp[:, 0:n], b, b[:, 0:n], murep, murep[:, p0:p0 + n], ALU.mult)
                tt(pool, W1, W1[:, j, p0:p0 + n], b, b[:, 0:n], wtmp, wtmp[:, 0:n], ALU.subtract)
                if p0 < nsh:
                    m = min(n, nsh - p0)
                    ts(dve, W2, W2[:, j, p0:p0 + m], wtmp, wtmp[:, 0:m], 0.5, None, ALU.mult)
        P.pop()
        if L.get("stop_at") == 1:
            P.pop(); return L
        hring = sb("hring", [128, 4, 8, 128], BF16)
        haloT = sb("haloT", [128, 8, 2], BF16)
        xbuf = [sb("xbuf0", [128, D])]
        xn = sb("xn", [128, D], BF16)
        ssq = sb("ssq", [128, 4])
        hsT = sb("hsT", [128, 8, 128], BF16)
        lora_t = [sb(f"lora{i}", [128, 128], BF16) for i in range(3)]
        w2a2 = sb("w2a2s", [128, 2, 2, 512], BF16)
        w0a0 = sb("w0a0s", [128, 2, 2, 512])
        vecs = sb("vecs", [128, 2, 512])
        qkg = sb("qkg", [128, 2, 512])
        rkrep = sb("rkrep", [128, 512])
        g2l = sb("g2l", [128, 512], BF16)
        P.push()
        wst = sb("w2a2f", [128, 2, 2, 512])
        slots = (slot, 1) if is_a else (slot, slot)
        for q_, sl_ in enumerate(slots):
            dma(sp, wst, wst[:, q_, :, :], None, L["w2a2_d"][sl_].rearrange("k p n -> p k n"))
            dma(sp, w0a0, w0a0[:, q_, :, :].rearrange("p k n -> p (k n)"), None,
                L["w0a0_d"][sl_].rearrange("k n -> (k n)").partition_broadcast(128))
        cp(dve, w2a2, w2a2[:], wst, wst[:])
        dma(sp, vecs, vecs[:].rearrange("p k n -> p (k n)"), None,
            L["vecs_d"][0:2].rearrange("k n -> (k n)").partition_broadcast(128))
        dma(sp, qkg, qkg[:].rearrange("p k n -> p (k n)"), None,
            L["vecs_d"][6:8].rearrange("k n -> (k n)").partition_broadcast(128))
        dma(sp, rkrep, rkrep[:], None, L["vecs_d"][4:5].rearrange("k n -> (k n)").partition_broadcast(128))
        wst2 = sb("g2lf", [128, 512])
        dma(sp, wst2, wst2[:], None, L["g2lora_d"])
        cp(dve, g2l, g2l[:], wst2, wst2[:])
        P.pop()
        if L.get("stop_at") == 2:
            P.pop(); return L

        def f32t(n):
            return sb(n, [128, WG])

        def bft(n):
            return sb(n, [128, WG], BF16)
        kk, a_t, kd, sg, eL, enL, eLx, eh, tmpA, tmpB = (f32t(n) for n in (
            "kk", "a_t", "kd", "sg", "eL", "enL", "eLx", "eh", "tmpA", "tmpB"))
        TMB, TMK, TMA, TMR, Bh, Kh, Vt = (bft(n) for n in ("TMB", "TMK", "TMA", "TMR", "Bh", "Kh", "Vt"))
        TMq = [TMB, TMK, TMA, TMR]
        FM = sb("FM", [128, 8, 128], BF16)
        s8 = sb("s8", [128, 8])
        WCt = sb("WCt", [64, 8])
        SC1 = sb("SC1", [128, 8, 384], BF16)
        SC2 = sb("SC2", [128, 8, 256], BF16)
        LmT = [sb(f"LmT{i}", [128, 8, 256], BF16) for i in range(2)]
        Xb = [sb(f"Xb{i}", [128, 8, 128], BF16) for i in range(2)]
        GD = sb("GD", [64, 16, 128])
        QmT = sb("QmT", [64, 16, 64])
        Y0 = sb("Y0", [128, 8, 64])
        Hs = [sb(f"H{i}", [64, 8, 64]) for i in range(2)]
        ytile = sb("ytile", [128, 512])
        qn = sb("qn", [128, 512]); qr = sb("qr", [128, 512], BF16)
        qsq = qn
        rt = [tmpA, tmpB]
        ksb = sb("ksb", [128, WG]); rsb = sb("rsb", [128, WG])
        ropet = sb("ropet", [128, 64])
        vx = sb("vx", [128, 2, 65], BF16)
        P.I(pool, lambda e: e.memset(vx[:], 1.0), W=[vx])
        qTs = sb("qTs", [128, 5, 128], BF16)
        qd = sb("qd", [128, 256], BF16)
        a2_t = f32t("a2_t"); gt_t = f32t("gt_t"); bon = f32t("bon")

        if sname == "c":
            P.I(pool, lambda e: e.memset(Hs[0][:], 0.0), W=[Hs[0]])
        else:
            fc = 6 if sname == "a" else 7
            ts(dve, Hs[0], Hs[0][:], Hctx, Hctx[:], flags[0:64, fc:fc + 1], None, ALU.mult, sb=[flags])

        xk = [0]

        def build_h(ti, dest_b, dest):
            xt = xbuf[0]
            dma(sp, xt, xt[:], None, xs[si, ti * 128:(ti + 1) * 128, :])
            P.I(pool, lambda e: e.memset(ssq[:, 0:1], 0.0), W=[ssq])
            actf(xn, xn[:], xt, xt[:], AF.Square, accum=ssq[:, 0:1], accb=[ssq])
            ts(dve, ssq, ssq[:, 1:2], ssq, ssq[:, 0:1], 1.0 / D, 1e-6, ALU.mult, ALU.add)
            rsqrt(ssq, ssq[:, 2:3], ssq, ssq[:, 1:2])
            actf(xn, xn[:], xt, xt[:], AF.Copy, scale=ssq[:, 2:3], sb=[ssq])
            for j in range(8):
                tr(tpb, tpb[:, j * 128:(j + 1) * 128], xn, xn[:, j * 128:(j + 1) * 128], L["identb_t"], identb)
            for j in range(8):
                ts(dve, dest_b, dest[:, j, :], tpb, tpb[:, j * 128:(j + 1) * 128], gs[:, j:j + 1], modcol[:, j:j + 1],
                   ALU.mult, ALU.add, sb=[gs, modcol])

        hh = hring
        build_h(32, hring, hring[:, 3])
        fo = {"a": 0, "b": 2, "c": 4}[sname]
        ts(dve, haloT, haloT[:, :, 0:1], hring, hring[:, 3, :, 0:1], flags[:, fo:fo + 1], None, ALU.mult, sb=[flags])
        ts(dve, haloT, haloT[:, :, 1:2], hring, hring[:, 3, :, 1:2], flags[:, fo + 1:fo + 2], None, ALU.mult, sb=[flags])
        if L.get("stop_at") == 3:
            P.pop(); return L
        build_h(0, hring, hring[:, 0])

        hsTs = [hsT, sb("hsTb", [128, 8, 128], BF16)]
        loras = [lora_t, [sb(f"lorab{i}", [128, 128], BF16) for i in range(3)]]

        def mkproj(cur, hsT):
            def proj(ps_ap, ps_b, off, n, shift=True, fm=False):
                nmm = 16 if shift else 8
                c = 0
                for (hb, hap, Wt) in ((hring, cur, W1), (hsT, hsT[:], W2)):
                    if Wt is W2 and not shift:
                        continue
                    for j in range(8):
                        if fm:
                            mm(ps_b, ps_ap, Wt, Wt[:, j, off:off + n], hb, hap[:, j, :], start=(c == 0), stop=(c == nmm - 1))
                        else:
                            mm(ps_b, ps_ap, hb, hap[:, j, :], Wt, Wt[:, j, off:off + n], start=(c == 0), stop=(c == nmm - 1))
                        c += 1

            return proj

        fpi = [0]

        def npsf():
            b_ = L["pps"][5 + fpi[0] % 2]
            fpi[0] += 1
            return b_

        def front(i):
            nps = L["nps"]
            hsT, lora_t = hsTs[i % 2], loras[i % 2]
            if i + 1 < NT:
                build_h(i + 1, hring, hring[:, (i + 1) % 4])
            cur = hring[:, i % 4]
            yield
            prevcol = haloT[:, :, 0:1] if i == 0 else hring[:, (i - 1) % 4, :, 127:128]
            nextcol = haloT[:, :, 1:2] if i == NT - 1 else hring[:, (i + 1) % 4, :, 0:1]
            yield
            proj = mkproj(cur, hsT)
            tt(pool, hsT, hsT[:, :, 1:127], hring, cur[:, :, 0:126], hring, cur[:, :, 2:128], ALU.add)
            tt(pool, hsT, hsT[:, :, 0:1], hring if i else haloT, prevcol, hring, cur[:, :, 1:2], ALU.add)
            yield
            tt(pool, hsT, hsT[:, :, 127:128], hring, cur[:, :, 126:127], haloT if i == NT - 1 else hring, nextcol, ALU.add)

            lgroups = [("lora", 0)] + ([("lora2", 1), ("gl", 2)] if is_a else [])
            yield
            for nm, li in lgroups:
                ps = nps()
                proj(ps[:, 0:128], ps, offs[nm], 128, fm=True)
                if nm == "gl":
                    actf(lora_t[li], lora_t[li][:], ps, ps[:, 0:128], AF.Sigmoid)
                else:
                    actf(lora_t[li], lora_t[li][0:64, :], ps, ps[0:64, 0:128], AF.Tanh)
                    cp(dve, lora_t[li], lora_t[li][64:128, :], ps, ps[64:128, 0:128])

            if sname in ("a", "c"):
                dma(sp, ropet, ropet[:], None, L["rope_d"][si, i * 128:(i + 1) * 128, :])
                jobs = []
                if is_a:
                    pq_ = nps(); proj(pq_[:, :], pq_, offs["q"], 512, shift=False)
                    jobs.append((pq_, pq_[:, 0:512], 8, 0))
                pk_ = nps(); proj(pk_[:, 0:256], pk_, offs["akv"], 256, shift=False)
                jobs.append((pk_, pk_[:, 0:128], 2, 1))
                cp(act, vx, vx[:, :, 0:64], pk_, pk_[:, 128:256].rearrange("p (h n) -> p h n", h=2))
                kt_ = (i if is_a else 32 + i)
                dma(sp, vx_d, vx_d[:, kt_, :], vx, vx[:].rearrange("p h n -> p (h n)"))
                for (pb, pap, nh, gi) in jobs:
                    yield
                    w = nh * 64
                    actf(qsq, qsq[:, 0:w], pb, pap, AF.Square)
                    P.I(dve, lambda e, nh=nh, w=w: e.tensor_reduce(out=s8[:, 0:nh], in_=qsq[:, 0:w].rearrange("p (h n) -> p h n", h=nh),
                                                       axis=AX.X, op=ALU.add), R=[qsq], W=[s8])
                    ts(dve, s8, s8[:, 0:nh], s8, s8[:, 0:nh], 1.0 / 64, 1e-6, ALU.mult, ALU.add)
                    rsqrt(s8, s8[:, 0:nh], s8, s8[:, 0:nh])
                    tt(dve, qn, qn[:, 0:w].rearrange("p (h n) -> p h n", h=nh), pb, pap.rearrange("p (h n) -> p h n", h=nh),
                       s8, s8[:, 0:nh].unsqueeze(2).to_broadcast([128, nh, 64]), ALU.mult)
                    tt(pool, qn, qn[:, 0:w], qn, qn[:, 0:w], qkg, qkg[:, gi, 0:w], ALU.mult)
                    v5 = qn[:, 0:w].rearrange("p (h a f n) -> p h a f n", h=nh, a=2, f=2)
                    o5 = qr[:, 0:w].rearrange("p (h a f n) -> p h a f n", h=nh, a=2, f=2)
                    x1, x2 = v5[:, :, :, 0, :], v5[:, :, :, 1, :]
                    cs_ = ropet[:, 0:32].rearrange("p (a n) -> p a n", a=2).unsqueeze(1).to_broadcast([128, nh, 2, 16])
                    sn_ = ropet[:, 32:64].rearrange("p (a n) -> p a n", a=2).unsqueeze(1).to_broadcast([128, nh, 2, 16])
                    r0 = rt[0][:, 0:w // 2].rearrange("p (h a n) -> p h a n", h=nh, a=2)
                    r1 = rt[1][:, 0:w // 2].rearrange("p (h a n) -> p h a n", h=nh, a=2)
                    tt(dve, rt[0], r0, qn, x1, ropet, cs_, ALU.mult)
                    tt(pool, rt[1], r1, qn, x2, ropet, sn_, ALU.mult)
                    tt(dve, qr, o5[:, :, :, 0, :], rt[0], r0, rt[1], r1, ALU.subtract)
                    tt(dve, rt[0], r0, qn, x1, ropet, sn_, ALU.mult)
                    tt(pool, rt[1], r1, qn, x2, ropet, cs_, ALU.mult)
                    tt(dve, qr, o5[:, :, :, 1, :], rt[0], r0, rt[1], r1, ALU.add)
                    if gi == 0:
                        src_b, src, nt_ = qr, qr, 4
                    else:
                        cp(pool, qd, qd[:].rearrange("p (k d n) -> p k d n", k=2, d=2),
                           qr, qr[:, 0:128].rearrange("p (k n) -> p k n", k=2).unsqueeze(2).to_broadcast([128, 2, 2, 64]))
                        src_b, src, nt_ = qd, qd, 2
                    for t_ in range(nt_):
                        tr(tpb, tpb[:, t_ * 128:(t_ + 1) * 128], src_b, src[:, t_ * 128:(t_ + 1) * 128], L["identb_t"], identb)
                    cp(act, qTs, qTs[:, 0:nt_, :], tpb, tpb[:, 0:nt_ * 128].rearrange("p (t n) -> p t n", t=nt_))
                    if gi == 0:
                        dma(sp, qT_d, qT_d[:, :, i * 128:(i + 1) * 128], qTs, qTs[:, 0:4, :])
                    else:
                        dma(sp, kT_d, kT_d[:, :, kt_ * 128:(kt_ + 1) * 128], qTs, qTs[:, 0:2, :])


        def mkset(tag):
            return (sb("TMA" + tag, [128, WG], BF16), sb("TMR" + tag, [128, WG], BF16), sb("Bh" + tag, [128, WG], BF16),
                    sb("Kh" + tag, [128, WG], BF16), sb("Vt" + tag, [128, WG], BF16), sb("FM" + tag, [128, 8, 128], BF16), sb("WCt" + tag, [64, 8]))
        psets4 = [[(TMA, TMR, Bh, Kh, Vt, FM, WCt), mkset("b")], [mkset("c"), mkset("d")]]
        bpi = [0]

        def npsB():
            b_ = L["pps"][3 + bpi[0] % 4]
            bpi[0] += 1
            return b_

        def Pgen(i, hg):
            nps = L["nps"]
            hsT, lora_t = hsTs[i % 2], loras[i % 2]
            cur = hring[:, i % 4]
            proj = mkproj(cur, hsT)
            TMA, TMR, Bh, Kh, Vt, FM, WCt = psets4[i % 2][hg]
            TMq = [TMB, TMK, TMA, TMR]
            cg = slice(hg * WG, (hg + 1) * WG)
            pk = nps(); proj(pk[:, 0:WG], pk, offs["k"] + hg * WG, WG)
            cp(act, ksb, ksb[:], pk, pk[:, 0:WG])
            pv = nps(); proj(pv[:, 0:WG], pv, offs["v"] + hg * WG, WG)
            yield
            cp(act, Vt, Vt[:], pv, pv[:, 0:WG])
            if need_y:
                pr_ = nps(); proj(pr_[:, 0:WG], pr_, offs["r"] + hg * WG, WG)
                cp(act, rsb, rsb[:], pr_, pr_[:, 0:WG])
            pw = nps()
            yield
            mm(pw, pw[:, 0:WG], lora_t[0], lora_t[0][:, :], w2a2, w2a2[:, 0, 0, cg])
            mm(pw, pw[:, WG:2 * WG], lora_t[0], lora_t[0][:, :], w2a2, w2a2[:, 0, 1, cg])
            yield
            tt(dve, kk, kk[:], ksb, ksb[:], vecs, vecs[:, 0, cg], ALU.mult)
            yield
            tt(pool, tmpB, tmpB[:], kk, kk[:], kk, kk[:], ALU.mult)
            P.I(dve, lambda e: e.tensor_reduce(out=s8[:, 0:4], in_=tmpB[:].rearrange("p (h n) -> p h n", h=4),
                                               axis=AX.X, op=ALU.add), R=[tmpB], W=[s8])
            ts(dve, s8, s8[:, 0:4], s8, s8[:, 0:4], 1e-24, None, ALU.max)
            yield
            rsqrt(s8, s8[:, 0:4], s8, s8[:, 0:4])
            tt(pool, kk, kk[:].rearrange("p (h n) -> p h n", h=4), kk, kk[:].rearrange("p (h n) -> p h n", h=4),
               s8, s8[:, 0:4].unsqueeze(2).to_broadcast([128, 4, 64]), ALU.mult)
            tt(dve, tmpA, tmpA[:], pw, pw[:, WG:2 * WG], w0a0, w0a0[:, 0, 1, cg], ALU.add)
            yield
            actf(a_t, a_t[:], tmpA, tmpA[:], AF.Sigmoid)
            stt(tmpB, tmpB[:], a_t, a_t[:], -1.0, vecs, vecs[:, 1, cg], ALU.add, ALU.mult)
            stt(kd, kd[:], tmpB, tmpB[:], 1.0, ksb, ksb[:], ALU.add, ALU.mult)
            yield
            if is_a:
                pg2 = nps()
                mm(pg2, pg2[:, 0:WG], lora_t[2], lora_t[2][:, :], g2l, g2l[:, cg])
                mm(pg2, pg2[:, WG:2 * WG], lora_t[1], lora_t[1][:, :], w2a2, w2a2[:, 1, 1, cg])
                cp(act, gt_t, gt_t[:], pg2, pg2[:, 0:WG])
                dma(sp, gate_d, gate_d[i * 128:(i + 1) * 128, cg], gt_t, gt_t[:])
                tt(dve, tmpA, tmpA[:], pg2, pg2[:, WG:2 * WG], w0a0, w0a0[:, 1, 1, cg], ALU.add)
                actf(a2_t, a2_t[:], tmpA, tmpA[:], AF.Sigmoid)
                tt(pool, a2_t, a2_t[:], a2_t, a2_t[:], a_t, a_t[:], ALU.add)
                ts(dve, a2_t, a2_t[:], a2_t, a2_t[:], 0.5, -1.0, ALU.mult, ALU.add)
                tt(pool, a2_t, a2_t[:], a2_t, a2_t[:], vecs, vecs[:, 1, cg], ALU.mult)
                stt(bon, bon[:], a2_t, a2_t[:], 1.0, ksb, ksb[:], ALU.add, ALU.mult)
                tt(dve, bon, bon[:], bon, bon[:], rsb, rsb[:], ALU.mult)
                tt(pool, bon, bon[:], bon, bon[:], rkrep, rkrep[:, cg], ALU.mult)
                P.I(dve, lambda e: e.tensor_reduce(out=s8[:, 4:8], in_=bon[:].rearrange("p (h n) -> p h n", h=4),
                                                   axis=AX.X, op=ALU.add), R=[bon], W=[s8])
                tt(dve, bon, bon[:].rearrange("p (h n) -> p h n", h=4), Vt, Vt[:].rearrange("p (h n) -> p h n", h=4),
                   s8, s8[:, 4:8].unsqueeze(2).to_broadcast([128, 4, 64]), ALU.mult)
                dma(sp, bon_d, bon_d[i * 128:(i + 1) * 128, cg], bon, bon[:])
            yield
            tt(dve, tmpA, tmpA[:], pw, pw[:, 0:WG], w0a0, w0a0[:, 0, 0, cg], ALU.add)
            actf(sg, sg[:], tmpA, tmpA[:], AF.Sigmoid)
            yield
            pL = nps()
            mm(pL, pL[:, 0:WG], cst, UTs, sg, sg[:])
            mm(pL, pL[:, WG:2 * WG], cst, SLs, sg, sg[:])
            yield
            actf(enL, enL[:], pL, pL[:, 0:WG], AF.Exp, scale=-1.0)
            actf(eh, eh[:], pL, pL[:, WG:2 * WG], AF.Exp)
            stt(tmpA, tmpA[:], sg, sg[:], -LOGC, pL, pL[:, 0:WG], ALU.mult, ALU.add)
            yield
            actf(eLx, eLx[:], tmpA, tmpA[:], AF.Exp)
            stt(TMA, TMA[:], kk, kk[:], -1.0, eLx, eLx[:], ALU.mult, ALU.mult)
            tt(pool, tmpB, tmpB[:], kk, kk[:], a_t, a_t[:], ALU.mult)
            yield
            tt(pool, TMB, TMB[:], tmpB, tmpB[:], enL, enL[:], ALU.mult)
            tt(pool, Bh, Bh[:], tmpB, tmpB[:], eh, eh[:], ALU.mult)
            tt(dve, TMK, TMK[:], kd, kd[:], enL, enL[:], ALU.mult)
            yield
            tt(dve, Kh, Kh[:], kd, kd[:], eh, eh[:], ALU.mult)
            if need_y:
                actf(eL, eL[:], pL, pL[:, 0:WG], AF.Exp)
                tt(dve, TMR, TMR[:], rsb, rsb[:], eL, eL[:], ALU.mult)
            yield
            for c in range(2):
                pwc = nps()
                for hl in range(4):
                    mm(pwc, pwc[0:64, hl:hl + 1], sg, sg[c * 64:(c + 1) * 64, hl * 64:(hl + 1) * 64],
                       cst, negc[c * 64:(c + 1) * 64, :])
                actf(WCt, WCt[:, c * 4:(c + 1) * 4], pwc, pwc[0:64, 0:4], AF.Exp)
            yield
            yield
            nq = 4 if need_y else 3
            for hpl in range(2):
                for q_ in range(nq):
                    s_ = hpl * 4 + q_
                    tr(tpb, tpb[:, s_ * 128:(s_ + 1) * 128], TMq[q_], TMq[q_][:, hpl * 128:(hpl + 1) * 128], L["identb_t"], identb)
            if need_y:
                cp(act, FM, FM[:], tpb, tpb[:, :].rearrange("p (s n) -> p s n", s=8))
            else:
                for hpl in range(2):
                    cp(act, FM, FM[:, hpl * 4:hpl * 4 + 3, :], tpb,
                       tpb[:, hpl * 512:hpl * 512 + 384].rearrange("p (s n) -> p s n", s=3))

        def Mpart(i, pump):
            nps = npsB
            sets = psets4[i % 2]
            for hl in range(8):
                TMA, TMR, Bh, Kh, Vt, FM, WCt = sets[hl // 4]
                hq = hl % 4
                e_, hpl = hq % 2, hq // 2
                pr = slice(e_ * 64, (e_ + 1) * 64)
                fB, fK, fA = FM[pr, hpl * 4 + 0, :], FM[pr, hpl * 4 + 1, :], FM[pr, hpl * 4 + 2, :]
                p1 = nps(); p2 = nps()
                if need_y:
                    fAR = FM[pr, hpl * 4 + 2:hpl * 4 + 4, :]
                    mm(p1, p1[:, 0:256], FM, fB, FM, fAR)
                    mm(p2, p2[:, 0:256], FM, fK, FM, fAR)
                else:
                    mm(p1, p1[:, 0:128], FM, fB, FM, fA)
                    mm(p2, p2[:, 0:128], FM, fK, FM, fA)
                mm(p1, p1[:, 256:384], FM, fA, FM, fB)
                if need_y:
                    tt(dve, SC1, SC1[:, hl, :], p1, p1[:, 0:384], cst, MASKA, ALU.mult)
                    tt(dve, SC2, SC2[:, hl, :], p2, p2[:, 0:256], cst, MASKB, ALU.mult)
                else:
                    tt(dve, SC1, SC1[:, hl, 0:128], p1, p1[:, 0:128], cst, MASKA[:, 0:128], ALU.mult)
                    tt(dve, SC1, SC1[:, hl, 256:384], p1, p1[:, 256:384], cst, MASKA[:, 256:384], ALU.mult)
                    tt(dve, SC2, SC2[:, hl, 0:128], p2, p2[:, 0:128], cst, MASKB[:, 0:128], ALU.mult)
                pX = nps()
                mm(pX, pX[:, 0:64], SC2, SC2[:, hl, 0:128], Vt, Vt[:, hq * 64:(hq + 1) * 64])
                cp(act, Xb[0], Xb[0][:, hl, 0:64], pX, pX[:, 0:64])
                cp(pool, Xb[0], Xb[0][:, hl, 64:128], TMA, TMA[:, hq * 64:(hq + 1) * 64])
                if hl % 2 == 1:
                    pump()
            for lev in range(6):
                Xc, Xn_ = Xb[lev % 2], Xb[(lev + 1) % 2]
                for hl in range(8):
                    if lev == 0:
                        LTb, LTc, Lmb, Lmc = SC1, SC1[:, hl, 0:128], SC1, SC1[:, hl, 256:384]
                    else:
                        src = LmT[(lev - 1) % 2]
                        LTb, LTc, Lmb, Lmc = src, src[:, hl, 128:256], src, src[:, hl, 0:128]
                    pa = nps()
                    mm(pa, pa[:, 0:128], LTb, LTc, Xc, Xc[:, hl, :])
                    if lev < 5:
                        mm(pa, pa[:, 128:256], LTb, LTc, Lmb, Lmc)
                        mm(pa, pa[:, 256:384], Lmb, Lmc, LTb, LTc)
                    tt(dve, Xn_, Xn_[:, hl, :], Xc, Xc[:, hl, :], pa, pa[:, 0:128], ALU.add)
                    if lev < 5:
                        cp(dve, LmT[lev % 2], LmT[lev % 2][:, hl, :], pa, pa[:, 128:384])
                    if hl % 4 == 3:
                        pump()
            Xf = Xb[0]
            for hl in range(8):
                TMA, TMR, Bh, Kh, Vt, FM, WCt = sets[hl // 4]
                hq = hl % 4
                hs_ = slice(hq * 64, (hq + 1) * 64)
                for c in range(2):
                    pg = nps()
                    cs = slice(c * 64, (c + 1) * 64)
                    mm(pg, pg[0:64, 0:64], Xf, Xf[cs, hl, 64:128], Bh, Bh[cs, hs_])
                    mm(pg, pg[0:64, 64:128], Bh, Bh[cs, hs_], Xf, Xf[cs, hl, 0:64], start=True, stop=False)
                    mm(pg, pg[0:64, 64:128], Kh, Kh[cs, hs_], Vt, Vt[cs, hs_], start=False, stop=True)
                    cp(act, GD, GD[:, hl * 2 + c, :], pg, pg[0:64, 0:128])
                if need_y:
                    for c in range(2):
                        pq2 = nps()
                        cs = slice(c * 64, (c + 1) * 64)
                        mm(pq2, pq2[0:64, 0:64], Xf, Xf[cs, hl, 64:128], SC1, SC1[cs, hl, 128 + c * 64:128 + (c + 1) * 64],
                           start=True, stop=False)
                        mm(pq2, pq2[0:64, 0:64], TMR, TMR[cs, hs_], L["identb_t"], identb[cs, c * 64:(c + 1) * 64],
                           start=False, stop=True)
                        cp(act, QmT, QmT[:, hl * 2 + c, :], pq2, pq2[0:64, 0:64])
                    py = nps()
                    mm(py, py[:, 0:64], SC1, SC1[:, hl, 128:256], Xf, Xf[:, hl, 0:64], start=True, stop=False)
                    mm(py, py[:, 0:64], SC2, SC2[:, hl, 128:256], Vt, Vt[:, hs_], start=False, stop=True)
                    cp(act, Y0, Y0[:, hl, :], py, py[:, 0:64])
                if hl % 2 == 1:
                    pump()
            for c in range(2):
                cs = slice(c * 64, (c + 1) * 64)
                Hc_b, Hn_b = Hs[c], Hs[1 - c]
                ph = nps()
                pyc = nps() if need_y else None
                for hl in range(8):
                    WCt = sets[hl // 4][6]
                    hq = hl % 4
                    hs_ = slice(hl * 64, (hl + 1) * 64)
                    if need_y:
                        mm(pyc, pyc[cs, hs_], QmT, QmT[:, hl * 2 + c, :], Hc_b, Hc_b[:, hl, :])
                    mm(ph, ph[0:64, hs_], GD, GD[:, hl * 2 + c, 0:64], Hc_b, Hc_b[:, hl, :], start=True, stop=False)
                    mm(ph, ph[0:64, hs_], cst, identf[0:64, 0:64], GD, GD[:, hl * 2 + c, 64:128], start=False, stop=True)
                    stt(Hn_b, Hn_b[:, hl, :], Hc_b, Hc_b[:, hl, :], WCt[:, c * 4 + hq:c * 4 + hq + 1], ph, ph[0:64, hs_],
                        ALU.mult, ALU.add, sb=[WCt])
                if need_y:
                    tt(dve, ytile, ytile[cs, :], Y0, Y0[cs, :, :].rearrange("p h n -> p (h n)"), pyc, pyc[cs, 0:512], ALU.add)
                pump()

        def Gchain(i):
            yield from front(i)
            yield from Pgen(i, 0)
            yield from Pgen(i, 1)

        def tile_post(i):
            if need_y:
                dma(sp, yscr[0 if is_a else 1], yscr[0 if is_a else 1][i * 128:(i + 1) * 128, :], ytile, ytile[:])
                if ("ytile_" + sname) in dbg_t and i < 2:
                    dma(sp, Buf(None), dbg_t["ytile_" + sname][i * 128:(i + 1) * 128, :], ytile, ytile[:])


        def run_all(gen):
            for _ in gen:
                pass

        def mkpump(gen, k):
            def pump():
                for _ in range(k):
                    try:
                        next(gen)
                    except StopIteration:
                        return
            return pump
        ntl = L.get("nt_limit", NT)
        L["npool"][0] = 3
        run_all(Gchain(0))
        for i in range(ntl):
            g2 = Gchain(i + 1) if i + 1 < ntl else iter(())
            Mpart(i, mkpump(g2, L.get("spump_k2", 4)))
            run_all(g2)
            tile_post(i)
        L["npool"][0] = 7
        if sname == "c":
            cp(dve, Hctx, Hctx[:], Hs[0], Hs[0][:])
            if "hctx" in dbg_t:
                dma(sp, Buf(None), dbg_t["hctx"][:, :], Hctx, Hctx[:].rearrange("p h n -> p (h n)"))
        P.pop()
    return L


def _emit_attention(P, nc, L):
    pe, dve, act, pool, sp = P.pe, P.dve, P.act, P.pool, P.sp
    mm, tr, tt, ts, stt, actf, cp, dma, dump, rsqrt = (L[k] for k in ("mm", "tr", "tt", "ts", "stt", "actf", "cp", "dma", "dump", "rsqrt"))
    nps, cst, identf, flags, pps, npool = (L[k] for k in ("nps", "cst", "identf", "flags", "pps", "npool"))
    dbg_t = L["dbg_t"]
    sb = P.sb
    P.push()
    qT = sb("qT", [128, 4, T_OWN], BF16)
    dma(sp, qT, qT[:], L["qT_d"], L["qT_d"][:])
    kT = sb("kT2", [128, 2, 2 * T_OWN], BF16)
    dma(sp, kT, kT[:], L["kT_d"], L["kT_d"][:])
    Vx = sb("Vx", [128, 64, 130], BF16)
    dma(sp, Vx, Vx[:], L["vx_d"], L["vx_d"][:])
    outg = sb("outg", [128, 512])
    dma(sp, outg, outg[:], None, L["vecs_d"][5:6].rearrange("k n -> (k n)").partition_broadcast(128))
    pT = [sb(f"pT{i}", [128, 512], BF16) for i in range(3)]
    oTs = sb("oTs", [65, 512])
    osb = sb("osb", [128, 4, 65])
    o2 = sb("o2", [128, 4, 64])
    ssa = sb("ssa", [128, 8])
    yat = sb("yat", [128, 4, 64])
    yattn_d = P.dram("yattn_d", [T_OWN, 512])
    L["yattn_d"] = yattn_d
    npool[0] = 5
    acc = [pps[5], pps[6]]
    it = 0
    uv_bf = P.dram("uv_bf", [L["NEXP"], 2 * D], BF16)
    L["uv_bf"] = uv_bf
    cbf = [sb(f"cbf{i}", [128, 4096]) for i in range(2)]
    cbb = [sb(f"cbb{i}", [128, 4096], BF16) for i in range(2)]
    nchunk = L["NEXP"] // 512
    cast_jobs = [(tb, c) for tb in range(2) for c in range(nchunk)]

    def cast_job(n):
        if n >= len(cast_jobs):
            return
        tb, c = cast_jobs[n]
        src = (L["u_tab"], L["v_tab"])[tb]
        f_, b_ = cbf[n % 2], cbb[n % 2]
        P.D(pool, lambda e: e.dma_start(out=f_[:], in_=src[c * 512:(c + 1) * 512, :].rearrange("(p j) n -> p (j n)", j=4)), W=[f_])
        cp(dve, b_, b_[:], f_, f_[:])
        P.D(pool, lambda e: e.dma_start(out=uv_bf[c * 512:(c + 1) * 512, tb * D:(tb + 1) * D].rearrange("(p j) n -> p j n", j=4),
                                        in_=b_[:].rearrange("p (j n) -> p j n", j=4)), R=[b_], W=[uv_bf])
    njob = [0]
    nh_lim = L.get("attn_heads", 8)
    nqb_lim = L.get("attn_qb", 8)
    for h in range(nh_lim):
        e_, hp, kvh = h % 2, h // 2, h // 4
        pr = slice(e_ * 64, (e_ + 1) * 64)
        for qb in range(nqb_lim):
            po = acc[it % 2]; it += 1
            cast_job(njob[0]); njob[0] += 1
            def smm(kt):
                ps_ = nps()
                mm(ps_, ps_[:, :], kT, kT[pr, kvh, kt * 128:(kt + 1) * 128], qT, qT[pr, hp, qb * 512:(qb + 1) * 512])
                return ps_
            pend = [smm(0), smm(1)]
            for kt in range(64):
                ps = pend.pop(0)
                if kt + 2 < 64:
                    pend.append(smm(kt + 2))
                pt = pT[kt % 3]
                if kt >= 32:
                    actf(pt, pt[:], ps, ps[:, :], AF.Exp, bias=flags[:, 8:9], scale=0.125, sb=[flags])
                else:
                    actf(pt, pt[:], ps, ps[:, :], AF.Exp, scale=0.125)
                mm(po, po[0:65, :], Vx, Vx[:, kt, kvh * 65:(kvh + 1) * 65], pt, pt[:], start=(kt == 0), stop=(kt == 63))
            cp(act, oTs, oTs[:, :], po, po[0:65, :])
            ptp = nps()
            for t in range(4):
                tr(ptp, ptp[:, t * 65:(t + 1) * 65], oTs, oTs[0:65, t * 128:(t + 1) * 128], cst, identf[0:65, 0:65])
            cp(dve, osb, osb[:], ptp, ptp[:, 0:260].rearrange("p (t n) -> p t n", t=4))
            P.I(dve, lambda e: e.reciprocal(out=ssa[:, 0:4], in_=osb[:, :, 64]), R=[osb], W=[ssa])
            tt(dve, o2, o2[:], osb, osb[:, :, 0:64], ssa, ssa[:, 0:4].unsqueeze(2).to_broadcast([128, 4, 64]), ALU.mult)
            tt(pool, yat, yat[:], o2, o2[:], o2, o2[:], ALU.mult)
            P.I(dve, lambda e: e.tensor_reduce(out=ssa[:, 4:8], in_=yat[:], axis=AX.X, op=ALU.add), R=[yat], W=[ssa])
            ts(dve, ssa, ssa[:, 4:8], ssa, ssa[:, 4:8], 1.0 / 64, 1e-6, ALU.mult, ALU.add)
            rsqrt(ssa, ssa[:, 4:8], ssa, ssa[:, 4:8])
            tt(dve, o2, o2[:], o2, o2[:], ssa, ssa[:, 4:8].unsqueeze(2).to_broadcast([128, 4, 64]), ALU.mult)
            tt(pool, yat, yat[:], o2, o2[:], outg, outg[:, h * 64:(h + 1) * 64].unsqueeze(1).to_broadcast([128, 4, 64]), ALU.mult)
            dma(sp, yattn_d, yattn_d[qb * 512:(qb + 1) * 512, h * 64:(h + 1) * 64].rearrange("(t p) n -> p t n", p=128), yat, yat[:])
            if "yattn" in dbg_t and qb == 0:
                dma(sp, Buf(None), dbg_t["yattn"][:, h * 64:(h + 1) * 64].rearrange("(t p) n -> p t n", p=128), yat, yat[:])
    while njob[0] < len(cast_jobs):
        cast_job(njob[0]); njob[0] += 1
    npool[0] = 7
    P.pop()
    return L


def _emit_final(P, nc, L):
    pe, dve, act, pool, sp = P.pe, P.dve, P.act, P.pool, P.sp
    mm, tr, tt, ts, stt, actf, cp, dma, dump, rsqrt = (L[k] for k in ("mm", "tr", "tt", "ts", "stt", "actf", "cp", "dma", "dump", "rsqrt"))
    nps, tpb, cst, identb, identf, flags = (L[k] for k in ("nps", "tpb", "cst", "identb", "identf", "flags"))
    dbg_t, xs, y_out = L["dbg_t"], L["xs"], L["y_out"]
    u_tab, v_tab = L["u_tab"], L["v_tab"]
    sb = P.sb
    yo = Buf(None, "y_out")
    L["yo"] = yo
    P.push()
    Jf = cst[:, 128:256]
    wo = sb("wo", [128, 8, D], BF16)
    wqb = sb("wqb", [128, 8, 2048], BF16)
    skb = sb("skb", [128, 2, 128], BF16)
    P.push()
    wstg = [sb(f"fstg{i}", [128, 2048]) for i in range(2)]
    wov = L["w_out_d"].rearrange("(j p) n -> p j n", p=128)
    wqv = L["wq_d"].rearrange("(j p) n -> p j n", p=128)
    k = 0
    for j in range(8):
        b = wstg[k % 2]; k += 1
        dma(sp, b, b[:, 0:D], None, wov[:, j, :])
        cp(dve, wo, wo[:, j, :], b, b[:, 0:D])
        b = wstg[k % 2]; k += 1
        dma(sp, b, b[:, :], None, wqv[:, j, :])
        cp(pool, wqb, wqb[:, j, :], b, b[:, :])
    b = wstg[0]
    dma(sp, b, b[:, 0:256].rearrange("p (k n) -> p k n", k=2), None, L["skT_d"].rearrange("k p n -> p k n"))
    cp(dve, skb, skb[:], b, b[:, 0:256].rearrange("p (k n) -> p k n", k=2))
    P.pop()
    reps = sb("reps", [128, 4, D])
    dma(sp, reps, reps[:], L["rep_d"], L["rep_d"][:].rearrange("r p n -> p r n"))
    lnr = sb("lnr", [128, 2, 512])
    dma(sp, lnr, lnr[:].rearrange("p k n -> p (k n)"), None, L["vecs_d"][2:4].rearrange("k n -> (k n)").partition_broadcast(128))
    yf, yb_, ysb, ysq, bon, gat, yat = (sb(n, [128, 512]) for n in ("yf", "yb", "ysb", "ysq", "bonf", "gatf", "yatf"))
    st8 = sb("st8", [128, 40])
    mixin = sb("mixin", [128, D], BF16)
    mT = sb("mT", [128, 8, 128], BF16)
    x1 = sb("x1", [128, D]); h2 = sb("h2", [128, D])
    tmpx = h2
    h2b = sb("h2b", [128, D], BF16)
    h2T = sb("h2T", [128, 8, 128], BF16)
    qpT = sb("qpT", [128, 16, 128], BF16)
    sc = sb("sc", [128, 16, 128]); scw = sb("scw", [128, 16, 128])
    sv = sb("sv", [128, 16, 16]); si = sb("si", [128, 16, 16], U32); sif = sb("sif", [128, 16, 16])
    cand = sc; candw = scw
    tv = sb("tv", [128, 8, 16]); ti = sb("ti", [128, 8, 16], U32)
    thi = sb("thi", [128, 8, 16], U32); tlo = sb("tlo", [128, 8, 16], U32)
    thif = sb("thif", [128, 8, 16]); tlof = sb("tlof", [128, 8, 16])
    ge = sb("ge", [128, 8, 16]); gate = sb("gate", [128, 128])
    eq = scw
    sel = sb("sel", [128, 2, 128])
    eidx = sb("eidx", [128, 128], I32)
    GS = 4
    NG = 3
    uvs = [[sb(f"uv{g}_{k}", [128, 2 * D], BF16) for k in range(GS)] for g in range(NG)]
    NPB = 3
    prods = [sb(f"prod{i}", [128, D], BF16) for i in range(NPB)]
    acc = h2
    dgs = [sb(f"dg{i}", [128, 128], BF16) for i in range(4)]
    pacc = [L["pps"][5], L["pps"][6]]
    L["npool"][0] = 5
    avs = [sb(f"av{i}", [128, GS]) for i in range(2)]
    gls = [sb(f"gl{i}", [128, GS]) for i in range(2)]
    wws = [sb(f"ww{i}", [128, GS]) for i in range(2)]
    iota16 = cst[:, 1280:1296]
    nt_lim = L.get("final_tiles", NT)
    x1s = [x1, sb("x1b", [128, D])]
    h2bs = [h2b, sb("h2bb", [128, D], BF16)]
    gates = [gate, sb("gateb", [128, 128])]
    eidxs = [eidx, sb("eidxb", [128, 128], I32)]

    def prefix(i, par):
        x1, h2b, gate, eidx = x1s[par], h2bs[par], gates[par], eidxs[par]
        rows = slice(i * 128, (i + 1) * 128)
        dma(sp, yf, yf[:], L["yscr"][0], L["yscr"][0][rows, :])
        dma(sp, yb_, yb_[:], L["yscr"][1], L["yscr"][1][(NT - 1 - i) * 128:(NT - i) * 128, :])
        yield
        dma(sp, bon, bon[:], L["bon_d"], L["bon_d"][rows, :])
        dma(sp, gat, gat[:], L["gate_d"], L["gate_d"][rows, :])
        dma(sp, yat, yat[:], L["yattn_d"], L["yattn_d"][rows, :])
        yield
        dma(sp, x1, x1[:], None, xs[0, rows, :])
        ps = nps()
        mm(ps, ps[:, :], cst, identf, yf, yf[:], start=True, stop=False)
        yield
        mm(ps, ps[:, :], cst, Jf, yb_, yb_[:], start=False, stop=True)
        cp(act, ysb, ysb[:], ps, ps[:, :])
        y3 = ysb[:].rearrange("p (h n) -> p h n", h=8)
        yield
        P.I(dve, lambda e: e.tensor_reduce(out=st8[:, 0:8], in_=y3, axis=AX.X, op=ALU.add), R=[ysb], W=[st8])
        tt(pool, ysq, ysq[:], ysb, ysb[:], ysb, ysb[:], ALU.mult)
        P.I(dve, lambda e: e.tensor_reduce(out=st8[:, 8:16], in_=ysq[:].rearrange("p (h n) -> p h n", h=8), axis=AX.X, op=ALU.add),
            R=[ysq], W=[st8])
        yield
        ts(dve, st8, st8[:, 0:8], st8, st8[:, 0:8], 1.0 / 64, None, ALU.mult)
        tt(dve, st8, st8[:, 16:24], st8, st8[:, 0:8], st8, st8[:, 0:8], ALU.mult)
        stt(st8, st8[:, 24:32], st8, st8[:, 8:16], 1.0 / 64, st8, st8[:, 16:24], ALU.mult, ALU.subtract)
        yield
        ts(dve, st8, st8[:, 24:32], st8, st8[:, 24:32], 64e-5, None, ALU.add)
        rsqrt(st8, st8[:, 24:32], st8, st8[:, 24:32])
        tt(dve, ysb, y3, ysb, y3, st8, st8[:, 0:8].unsqueeze(2).to_broadcast([128, 8, 64]), ALU.subtract)
        yield
        tt(dve, ysb, y3, ysb, y3, st8, st8[:, 24:32].unsqueeze(2).to_broadcast([128, 8, 64]), ALU.mult)
        tt(pool, ysb, ysb[:], ysb, ysb[:], lnr, lnr[:, 0, :], ALU.mult)
        tt(pool, ysb, ysb[:], ysb, ysb[:], lnr, lnr[:, 1, :], ALU.add)
        yield
        tt(pool, ysb, ysb[:], ysb, ysb[:], bon, bon[:], ALU.add)
        tt(dve, mixin, mixin[:, 0:512], ysb, ysb[:], gat, gat[:], ALU.mult)
        cp(pool, mixin, mixin[:, 512:1024], yat, yat[:])
        yield
        if "yrwkv" in dbg_t and i < 2:
            tt(pool, ysb, ysb[:], ysb, ysb[:], gat, gat[:], ALU.mult)
            dma(sp, Buf(None), dbg_t["yrwkv"][rows, :], ysb, ysb[:])
        for j in range(8):
            tr(tpb, tpb[:, j * 128:(j + 1) * 128], mixin, mixin[:, j * 128:(j + 1) * 128], L["identb_t"], identb)
        cp(act, mT, mT[:], tpb, tpb[:, :].rearrange("p (j n) -> p j n", j=8))
        yield
        for hf in range(2):
            hs_ = slice(hf * 512, (hf + 1) * 512)
            ps = nps()
            for j in range(8):
                mm(ps, ps[:, :], mT, mT[:, j, :], wo, wo[:, j, hs_], start=(j == 0), stop=(j == 7))
            tt(dve, tmpx, tmpx[:, hs_], ps, ps[:, :], reps, reps[:, 0, hs_], ALU.mult)
            tt(pool, x1, x1[:, hs_], tmpx, tmpx[:, hs_], x1, x1[:, hs_], ALU.add)
        if "x1" in dbg_t and i < 2:
            dma(sp, Buf(None), dbg_t["x1"][rows, :], x1, x1[:])
        P.I(pool, lambda e: e.memset(st8[:, 32:33], 0.0), W=[st8])
        yield
        actf(h2b, h2b[:], x1, x1[:], AF.Square, accum=st8[:, 32:33], accb=[st8])
        ts(dve, st8, st8[:, 33:34], st8, st8[:, 32:33], 1.0 / D, 1e-6, ALU.mult, ALU.add)
        rsqrt(st8, st8[:, 33:34], st8, st8[:, 33:34])
        yield
        stt(h2, h2[:], x1, x1[:], st8[:, 33:34], reps, reps[:, 2, :], ALU.mult, ALU.mult, sb=[st8])
        tt(pool, h2, h2[:], h2, h2[:], reps, reps[:, 3, :], ALU.add)
        cp(act, h2b, h2b[:], h2, h2[:])
        yield
        for j in range(8):
            tr(tpb, tpb[:, j * 128:(j + 1) * 128], h2b, h2b[:, j * 128:(j + 1) * 128], L["identb_t"], identb)
        cp(act, h2T, h2T[:], tpb, tpb[:, :].rearrange("p (j n) -> p j n", j=8))
        for g4 in range(4):
            ps = nps()
            for gg in range(4):
                g = g4 * 4 + gg
                for j in range(8):
                    mm(ps, ps[:, gg * 128:(gg + 1) * 128], wqb, wqb[:, j, g * 128:(g + 1) * 128], h2T, h2T[:, j, :],
                       start=(j == 0), stop=(j == 7))
            cp(act, qpT, qpT[:, g4 * 4:(g4 + 1) * 4, :], ps, ps[:, :].rearrange("p (g n) -> p g n", g=4))
            yield
        yield
        for g4 in range(4):
            ps = nps()
            for gg in range(4):
                g = g4 * 4 + gg
                mm(ps, ps[:, gg * 128:(gg + 1) * 128], qpT, qpT[:, g, :], skb, skb[:, g % 2, :])
            cp(dve, sc, sc[:, g4 * 4:(g4 + 1) * 4, :], ps, ps[:, :].rearrange("p (g n) -> p g n", g=4))

        def top16(vals_b, vals, work_b, work, ov_b, ov, oi_b, oi):
            P.I(dve, lambda e: e.max(out=ov[:, 0:8], in_=vals), R=[vals_b], W=[ov_b])
            P.I(dve, lambda e: e.match_replace(out=work, in_to_replace=ov[:, 0:8], in_values=vals, imm_value=-1e30),
                R=[vals_b, ov_b], W=[work_b])
            P.I(dve, lambda e: e.max(out=ov[:, 8:16], in_=work), R=[work_b], W=[ov_b])
            P.I(dve, lambda e: e.max_index(out=oi[:, 0:8], in_max=ov[:, 0:8], in_values=vals), R=[vals_b, ov_b], W=[oi_b])
            P.I(dve, lambda e: e.max_index(out=oi[:, 8:16], in_max=ov[:, 8:16], in_values=vals), R=[vals_b, ov_b], W=[oi_b])
        for g in range(16):
            top16(sc, sc[:, g, :], scw, scw[:, g, :], sv, sv[:, g, :], si, si[:, g, :])
            if g % 2 == 1:
                yield
        yield
        sv4 = sv[:].rearrange("p (h k) n -> p h k n", k=2)
        candv = cand[:].rearrange("p g n -> p (g n)").rearrange("p (h n) -> p h n", h=8)
        candwv = candw[:].rearrange("p g n -> p (g n)").rearrange("p (h n) -> p h n", h=8)
        yield
        eqv = eq[:].rearrange("p g n -> p (g n)").rearrange("p (h a b) -> p h a b", h=8, a=16)
        tt(dve, cand, candv.rearrange("p h (a b) -> p h a b", a=16), sv, sv4[:, :, 0, :].unsqueeze(3).to_broadcast([128, 8, 16, 16]),
           sv, sv4[:, :, 1, :].unsqueeze(2).to_broadcast([128, 8, 16, 16]), ALU.add)
        for h in range(8):
            top16(cand, candv[:, h, :], candw, candwv[:, h, :], tv, tv[:, h, :], ti, ti[:, h, :])
            if h % 2 == 1:
                yield
        yield
        ts(dve, st8, st8[:, 0:8], tv, tv[:, :, 0], -1.0, None, ALU.mult)
        P.I(pool, lambda e: e.memset(st8[:, 8:16], 0.0), W=[st8])
        for h in range(8):
            actf(ge, ge[:, h, :], tv, tv[:, h, :], AF.Exp, bias=st8[:, h:h + 1], sb=[st8], accum=st8[:, 8 + h:9 + h], accb=[st8])
        yield
        P.I(dve, lambda e: e.reciprocal(out=st8[:, 16:24], in_=st8[:, 8:16]), R=[st8], W=[st8])
        tt(dve, gate, gate[:].rearrange("p (h n) -> p h n", h=8), ge, ge[:], st8, st8[:, 16:24].unsqueeze(2).to_broadcast([128, 8, 16]), ALU.mult)
        ts(dve, thi, thi[:], ti, ti[:], 4, None, ALU.logical_shift_right)
        yield
        ts(dve, tlo, tlo[:], ti, ti[:], 15, None, ALU.bitwise_and)
        cp(dve, thif, thif[:], thi, thi[:])
        cp(dve, tlof, tlof[:], tlo, tlo[:])
        yield
        cp(dve, sif, sif[:], si, si[:])
        sif4 = sif[:].rearrange("p (h k) n -> p h k n", k=2)
        io4 = iota16.unsqueeze(1).unsqueeze(1).to_broadcast([128, 8, 16, 16])
        yield
        for q_, (tf_b, kk_) in enumerate(((thif, 0), (tlof, 1))):
            tt(dve, eq, eqv, tf_b, tf_b[:].unsqueeze(3).to_broadcast([128, 8, 16, 16]), cst, io4, ALU.is_equal)
            tt(dve, eq, eqv, eq, eqv, sif, sif4[:, :, kk_, :].unsqueeze(2).to_broadcast([128, 8, 16, 16]), ALU.mult)
            P.I(dve, lambda e, q_=q_: e.tensor_reduce(out=sel[:, q_, :].rearrange("p (h n) -> p h n", h=8), in_=eqv, axis=AX.X, op=ALU.add),
                R=[eq], W=[sel])
        stt(sel, sel[:, 0, :], sel, sel[:, 0, :], 128.0, sel, sel[:, 1, :], ALU.mult, ALU.add)
        cp(dve, eidx, eidx[:], sel, sel[:, 0, :])
        yield
        if "eidx" in dbg_t and i < 1:
            dma(sp, Buf(None), dbg_t["eidx"][:, :], sel, sel[:, 0, :])
            dma(sp, Buf(None), dbg_t["gate"][:, :], gate, gate[:])
    def tailp(i, par, pump):
        x1, h2b, gate, eidx = x1s[par], h2bs[par], gates[par], eidxs[par]
        rows = slice(i * 128, (i + 1) * 128)
        ngrp = 128 // GS

        def gath(g):
            for k_ in range(GS):
                m = g * GS + k_
                t_ = uvs[g % NG][k_]
                P.D(pool, lambda e, t_=t_, m=m: e.indirect_dma_start(
                    out=t_[:], out_offset=None, in_=L["uv_bf"][:, :],
                    in_offset=bass.IndirectOffsetOnAxis(ap=eidx[:, m:m + 1].bitcast(U32), axis=0)), R=[eidx], W=[t_])

        def udots(g):
            av_, gl_ = avs[g % 2], gls[g % 2]
            P.I(pool, lambda e, av_=av_: e.memset(av_[:], 0.0), W=[av_], cost=120.0)
            for k_ in range(GS):
                t_ = uvs[g % NG][k_]
                pj = prods[(g * GS + k_) % NPB]
                tt(dve, pj, pj[:], t_, t_[:, 0:D], h2b, h2b[:], ALU.mult)
                actf(pj, pj[:], pj, pj[:], AF.Copy, accum=av_[:, k_:k_ + 1], accb=[av_])
            actf(gl_, gl_[:], av_, av_[:], AF.Gelu)

        def vaxpy(g):
            ms = slice(g * GS, (g + 1) * GS)
            gl_, ww_ = gls[g % 2], wws[g % 2]
            tt(dve, ww_, ww_[:], gl_, gl_[:], gate, gate[:, ms], ALU.mult)
            for k_ in range(GS):
                m = g * GS + k_
                t_ = uvs[g % NG][k_]
                dg = dgs[m % 4]
                ts(dve, dg, dg[:], L["identb_t"], identb, ww_[:, k_:k_ + 1], None, ALU.mult, sb=[ww_])
                for hf in range(2):
                    mm(pacc[hf], pacc[hf][:, :], dg, dg[:], t_, t_[:, D + hf * 512:D + (hf + 1) * 512], start=(m == 0), stop=(m == 127))
        gath(0); gath(1)
        udots(0)
        for g in range(ngrp):
            if g + 2 < ngrp:
                gath(g + 2)
            if g + 1 < ngrp:
                udots(g + 1)
            vaxpy(g)
            pump()
        for hf in range(2):
            hs_ = slice(hf * 512, (hf + 1) * 512)
            if "peer" in dbg_t and i < 2:
                cp(act, acc, acc[:, hs_], pacc[hf], pacc[hf][:, :])
                dma(sp, Buf(None), dbg_t["peer"][rows, hs_], acc, acc[:, hs_])
            tt(dve, acc, acc[:, hs_], pacc[hf], pacc[hf][:, :], reps, reps[:, 1, hs_], ALU.mult)
        tt(pool, acc, acc[:], acc, acc[:], x1, x1[:], ALU.add)
        dma(sp, yo, y_out[rows, :], acc, acc[:])

    def run_all(gen):
        for _ in gen:
            pass
    run_all(prefix(0, 0))
    for i in range(nt_lim):
        nxt = prefix(i + 1, (i + 1) % 2) if i + 1 < nt_lim else None

        def pump(nxt=nxt, k=L.get("pump_k", 2)):
            if nxt is None:
                return
            for _ in range(k):
                try:
                    next(nxt)
                except StopIteration:
                    return
        tailp(i, i % 2, pump)
        if nxt is not None:
            run_all(nxt)
    L["npool"][0] = 7
    P.pop()
    return L
```

```python
import numpy as np
from contextlib import ExitStack
import concourse.bass as bass
import concourse.mybir as mybir
from concourse.bass_utils import run_bass_kernel_spmd

F32 = mybir.dt.float32
BF16 = mybir.dt.bfloat16
I32 = mybir.dt.int32
U32 = mybir.dt.uint32
ALU = mybir.AluOpType
AF = mybir.ActivationFunctionType
AX = mybir.AxisListType

class Buf:
    __slots__ = ("t", "lw", "rd", "name", "psum")

    def __init__(self, t, name=""):
        self.t = t
        self.lw = None
        self.rd = []
        self.name = name
        self.psum = False

    def __getitem__(self, k):
        return self.t[k]


class Eng:
    def __init__(self, P, name, eng, kind):
        self.P = P
        self.name = name
        self.eng = eng
        self.kind = kind
        self.seen = {}
        self.sem = P.newsem(name)
        self.cnt = 0
        self.pool = []
        self.ndma = 0


class Op:
    __slots__ = ("id", "eng", "fn", "kind", "cost", "preds", "tag")


class Prog:
    K = 12
    HOP = 1000.0
    SELF = 150.0

    def __init__(self, nc, ctx):
        self.nc = nc
        self.ctx = ctx
        self.sems = {}
        self.pe = Eng(self, "pe", nc.tensor, "c")
        self.dve = Eng(self, "dve", nc.vector, "c")
        self.act = Eng(self, "act", nc.scalar, "c")
        self.pool = Eng(self, "pool", nc.gpsimd, "c")
        self.sp = Eng(self, "sp", nc.sync, "c")
        self.engs = (self.pe, self.dve, self.act, self.pool, self.sp)
        for e in (self.sp, self.act, self.pool):
            e.pool = [self.newsem(f"{e.name}_d{i}") for i in range(self.K)]
        self.nins = 0
        self.ops = []
        self.nid = 0
        self.base = 0
        self.reorder = REORDER

    def newsem(self, name):
        s = self.ctx.enter_context(self.nc.semaphore(name))
        self.sems[name] = s
        return name

    def sb(self, name, shape, dt=F32):
        self._uid = getattr(self, "_uid", 0) + 1
        t = self.ctx.enter_context(self.nc.sbuf_tensor(f"s{self._uid}_" + name, list(shape), dt))
        return Buf(t, name)

    def ps(self, name, shape, dt=F32):
        t = self.ctx.enter_context(self.nc.psum_tensor("p_" + name, list(shape), dt))
        b = Buf(t, name)
        b.psum = True
        return b

    def dram(self, name, shape, dt=F32, kind="Internal"):
        t = self.nc.dram_tensor(name, list(shape), dt, kind=kind)
        return Buf(t, name)

    def _record(self, E, fn, R, W, kind, cost):
        W = list(W) + [b for b in R if b.psum]
        R = [b for b in R if not b.psum]
        op = Op()
        op.id = self.nid
        self.nid += 1
        op.eng, op.fn, op.kind, op.cost, op.tag = E, fn, kind, cost, None
        preds = set()
        base = self.base
        for b in R:
            if b.lw is not None and b.lw >= base:
                preds.add(b.lw)
        for b in W:
            if b.lw is not None and b.lw >= base:
                preds.add(b.lw)
            for r in b.rd:
                if r >= base:
                    preds.add(r)
        op.preds = preds
        for b in W:
            b.lw = op.id
            b.rd = []
        for b in R:
            b.rd.append(op.id)
        self.ops.append(op)
        self.nins += 1

    def I(self, E, fn, R=(), W=(), cost=250.0):
        self._record(E, fn, R, W, "I", cost)

    def D(self, Q, fn, R=(), W=(), cost=3000.0):
        self._record(Q, fn, R, W, "D", cost)

    def _schedule(self, ops):
        import heapq
        n = len(ops)
        base = self.base
        succs = [[] for _ in range(n)]
        indeg = [0] * n
        for k, op in enumerate(ops):
            for p in op.preds:
                succs[p - base].append(k)
                indeg[k] += 1
        if not self.reorder:
            return list(range(n))
        future = {e.name: [] for e in self.engs}
        avail = {e.name: [] for e in self.engs}
        free = {e.name: 0.0 for e in self.engs}
        finish = [0.0] * n
        rtime = [0.0] * n
        bl = [0.0] * n
        for k in range(n - 1, -1, -1):
            op = ops[k]
            m_ = 0.0
            for s_ in succs[k]:
                lat = self.SELF if ops[s_].eng is op.eng else self.HOP
                v = lat + bl[s_]
                if v > m_:
                    m_ = v
            bl[k] = m_ + (120.0 if op.kind == "D" else op.cost)
        PRI = PRIORITY
        for k in range(n):
            if indeg[k] == 0:
                heapq.heappush(avail[ops[k].eng.name], ((-bl[k], k) if PRI else (k, k)))
        order = []
        while len(order) < n:
            best = None
            for e in self.engs:
                nm = e.name
                fu, av = future[nm], avail[nm]
                while fu and fu[0][0] <= free[nm]:
                    k_ = heapq.heappop(fu)[1]
                    heapq.heappush(av, ((-bl[k_], k_) if PRI else (k_, k_)))
                if av:
                    cand = (free[nm], av[0][1], nm, True)
                elif fu:
                    cand = (fu[0][0], fu[0][1], nm, False)
                else:
                    continue
                if best is None or cand[:2] < best[:2]:
                    best = cand
            start, k, nm, from_av = best
            if from_av:
                heapq.heappop(avail[nm])
            else:
                heapq.heappop(future[nm])
            op = ops[k]
            if op.kind == "D":
                free[nm] = start + 120.0
                finish[k] = start + op.cost
            else:
                free[nm] = start + op.cost
                finish[k] = free[nm]
            order.append(k)
            for s_ in succs[k]:
                lat = self.SELF if ops[s_].eng is op.eng else self.HOP
                t_ = finish[k] + lat
                if t_ > rtime[s_]:
                    rtime[s_] = t_
                indeg[s_] -= 1
                if indeg[s_] == 0:
                    heapq.heappush(future[ops[s_].eng.name], (rtime[s_], s_))
        return order

    def flush(self):
        ops = self.ops
        if not ops:
            return
        order = self._schedule(ops)
        base = self.base
        for k in order:
            op = ops[k]
            E = op.eng
            need = {}
            for p in op.preds:
                po = ops[p - base]
                if po.eng is E and E is self.pe:
                    continue
                key, val = po.tag
                if need.get(key, 0) < val:
                    need[key] = val
            for key, val in need.items():
                if E.seen.get(key, 0) < val:
                    E.eng.wait_ge(self.sems[key], val)
                    E.seen[key] = val
            if op.kind == "I":
                ins = op.fn(E.eng)
                E.cnt += 1
                ins.then_inc(self.sems[E.sem], 1)
                op.tag = (E.sem, E.cnt)
            else:
                j = E.ndma
                sname = E.pool[j % self.K]
                prev = 16 * (j // self.K)
                if prev > 0 and E.seen.get(sname, 0) < prev:
                    E.eng.wait_ge(self.sems[sname], prev)
                    E.seen[sname] = prev
                ins = op.fn(E.eng)
                ins.then_inc(self.sems[sname], 16)
                E.ndma += 1
                op.tag = (sname, prev + 16)
            op.fn = None
        self.base = self.nid
        self.ops = []

    def push(self):
        self._saved = getattr(self, "_saved", [])
        self._saved.append(self.ctx)
        self.ctx = ExitStack()
        self.ctx.__enter__()

    def barrier(self):
        self.flush()
        engs = self.engs
        for E in engs:
            for X in engs:
                if X is not E and X.cnt > 0 and E.seen.get(X.sem, 0) < X.cnt:
                    E.eng.wait_ge(self.sems[X.sem], X.cnt)
                    E.seen[X.sem] = X.cnt
            for Q in (self.sp, self.act, self.pool):
                for i, sname in enumerate(Q.pool):
                    n = (Q.ndma - i + self.K - 1) // self.K if Q.ndma > i else 0
                    if n > 0 and E.seen.get(sname, 0) < 16 * n:
                        E.eng.wait_ge(self.sems[sname], 16 * n)
                        E.seen[sname] = 16 * n

    def pop(self):
        self.barrier()
        self.ctx.__exit__(None, None, None)
        self.ctx = self._saved.pop()

    def finish(self, bufs):
        self.barrier()


T_OWN = 4096
REORDER = True
PRIORITY = True
NT = 32
D = 1024
LOGC = -0.6065306597126334
NEG = -100.0


def build_program(dbg=(), stages=(), small=False):
    nc = bass.Bass("TRN2", target_bir_lowering=False)

    def din(name, shape, dt=F32):
        return nc.dram_tensor(name, list(shape), dt, kind="ExternalInput").ap()

    xs = din("xs", [3, 33 * 128, D])
    flags_d = din("flags", [128, 16])
    ccol_d = din("ccol", [128, 8])
    ada_w = din("ada_w", [D, 6 * D])
    ada_b = din("ada_b", [1, 6 * D])
    g1col_d = din("g1col", [128, 8])
    g2col_d = din("g2col", [128, 8])
    g2row_d = din("g2row", [1, D])
    NCOL = [2688, 1664, 1408]
    wst_d = [din(f"wst{s}", [D, NCOL[s]]) for s in range(3)]
    mu_d = [din(f"mu{s}", [1, NCOL[s]]) for s in range(3)]
    w2a2_d = din("w2a2", [4, 2, 128, 512])
    w0a0_d = din("w0a0", [4, 2, 512])
    g2lora_d = din("g2lora", [128, 512])
    vecs_d = din("vecs", [8, 512])
    rope_d = din("rope", [3, 4096, 64])
    w_out_d = din("w_out", [D, D])
    wq_d = din("wq", [D, 2048])
    skT_d = din("skT", [2, 128, 128])
    NEXP = 512 if small else 16384
    u_tab = din("u_tab", [NEXP, D])
    v_tab = din("v_tab", [NEXP, D])
    consts_d = din("consts", [128, 2048])

    y_out = nc.dram_tensor("y_out", [T_OWN, D], F32, kind="ExternalOutput").ap()
    dbg_t = {}
    for name, shape in dbg:
        dbg_t[name] = nc.dram_tensor("dbg_" + name, list(shape), F32, kind="ExternalOutput").ap()

    ctx = ExitStack()
    with ctx:
        P = Prog(nc, ctx)
        L2 = _emit(P, nc, locals())
        for kv in stages:
            if isinstance(kv, tuple):
                L2[kv[0]] = kv[1]
        if "stop0" not in stages:
            _emit_streams(P, nc, L2)
        if "attn" in stages or not stages:
            _emit_attention(P, nc, L2)
        if "final" in stages or not stages:
            _emit_final(P, nc, L2)
            P.finish([L2["yo"]])
        else:
            P.finish([])
    return nc


def _emit(P, nc, L):
    xs, flags_d, ccol_d, ada_w, ada_b = L["xs"], L["flags_d"], L["ccol_d"], L["ada_w"], L["ada_b"]
    dbg_t = L["dbg_t"]
    pe, dve, act, pool, sp = P.pe, P.dve, P.act, P.pool, P.sp

    def fsz(ap):
        try:
            return float(ap.free_size())
        except Exception:
            return 256.0

    def mm(ob, o, lb, l, rb, r, start=True, stop=True):
        c = 64.0 + fsz(r) / 1.4
        if l.dtype == F32:
            c *= 4
        return P.I(pe, lambda e: e.matmul(o, lhsT=l, rhs=r, start=start, stop=stop), R=[lb, rb], W=[ob], cost=c)

    def tr(ob, o, ib, i, idb, idap):
        c = 64.0 + fsz(i) / 1.4
        if i.dtype == F32:
            c *= 4
        return P.I(pe, lambda e: e.transpose(o, i, idap), R=[ib, idb], W=[ob], cost=c)

    def vcost(E, o):
        if E is pool:
            return 150.0 + fsz(o) / 0.7
        return 80.0 + fsz(o) / 0.96

    def tt(E, ob, o, ab, a, bb, b, op):
        return P.I(E, lambda e: e.tensor_tensor(out=o, in0=a, in1=b, op=op), R=[ab, bb], W=[ob], cost=vcost(E, o))

    def ts(E, ob, o, ab, a, s1, s2, op0, op1=None, sb=()):
        if op1 is None:
            return P.I(E, lambda e: e.tensor_scalar(out=o, in0=a, scalar1=s1, scalar2=None, op0=op0), R=[ab, *sb], W=[ob], cost=vcost(E, o))
        return P.I(E, lambda e: e.tensor_scalar(out=o, in0=a, scalar1=s1, scalar2=s2, op0=op0, op1=op1), R=[ab, *sb], W=[ob], cost=vcost(E, o))

    def stt(ob, o, ab, a, sc, bb, b, op0, op1, sb=()):
        return P.I(dve, lambda e: e.scalar_tensor_tensor(out=o, in0=a, scalar=sc, in1=b, op0=op0, op1=op1), R=[ab, bb, *sb], W=[ob],
                   cost=vcost(dve, o))

    def actf(ob, o, ib, i, func, bias=0.0, scale=1.0, sb=(), accum=None, accb=()):
        def f(e):
            kw = dict(out=o, in_=i, func=func, bias=bias, scale=scale)
            if accum is not None:
                kw["accum_out"] = accum
            return e.activation(**kw)
        return P.I(act, f, R=[ib, *sb], W=[ob, *accb], cost=220.0 + fsz(o) / 1.4)

    def rsqrt(ob, o, ib, i):
        P.I(act, lambda e: e.activation(out=o, in_=i, func=AF.Sqrt), R=[ib], W=[ob], cost=220.0 + fsz(o) / 1.4)
        P.I(dve, lambda e: e.reciprocal(out=o, in_=o), R=[ob], W=[ob], cost=vcost(dve, o))

    def cp(E, ob, o, ib, i):
        if E is act:
            return P.I(E, lambda e: e.copy(out=o, in_=i), R=[ib], W=[ob], cost=220.0 + fsz(o) / 1.4)
        return P.I(E, lambda e: e.tensor_copy(out=o, in_=i), R=[ib], W=[ob], cost=vcost(E, o))

    def dma(Q, ob, o, ib, i):
        return P.D(Q, lambda e: e.dma_start(out=o, in_=i), R=[ib] if ib is not None else [], W=[ob] if ob is not None else [],
                   cost=2500.0 + fsz(o) * 128 * 4 / 150.0)

    DR = Buf(None, "dram_in")

    def dump(name, buf, ap):
        if name in dbg_t:
            dma(sp, Buf(None), dbg_t[name], buf, ap)

    cst = P.sb("cst", [128, 2048])
    dma(sp, cst, cst[:], None, L["consts_d"])
    identf = cst[:, 0:128]
    identb_t = P.sb("identb", [128, 256], BF16)
    cp(dve, identb_t, identb_t[:], cst, cst[:, 0:256])
    identb = identb_t[:, 0:128]
    flags = P.sb("flags", [128, 16])
    dma(sp, flags, flags[:], None, flags_d)
    ones1 = P.sb("ones1", [1, 128])
    P.I(dve, lambda e: e.memset(ones1[:], 1.0), W=[ones1])

    pps = [P.ps(f"pp{i}", [128, 512]) for i in range(7)]
    ppi = [0]

    npool = [7]

    def nps():
        b = pps[ppi[0] % npool[0]]
        ppi[0] += 1
        return b
    tpb = P.ps("tpb", [128, 1024], BF16)

    gs = P.sb("gs", [128, 16])
    modcol = P.sb("modcol", [128, 32])
    rep_d = P.dram("rep_d", [4, 128, D])
    P.push()
    cT = P.sb("cT", [128, 8])
    dma(sp, cT, cT[:], None, ccol_d)
    sT = P.sb("sT", [128, 8])
    actf(sT, sT[:], cT, cT[:], AF.Silu)
    adab = P.sb("adab", [1, 6 * D])
    dma(sp, adab, adab[:], None, ada_b)
    modrow = P.sb("modrow", [1, 6 * D])
    awv = ada_w.rearrange("(j p) n -> p j n", p=128)
    stg = [P.sb(f"stg{i}", [128, 4096]) for i in range(2)]
    for g in range(12):
        b = stg[g % 2]
        bv = b[:].rearrange("p (j n) -> p j n", j=8)
        dma(sp, b, bv, None, awv[:, :, g * 512:(g + 1) * 512])
        ps = nps()
        for j in range(8):
            mm(ps, ps[0:1, :], sT, sT[:, j:j + 1], b, bv[:, j, :], start=(j == 0), stop=(j == 7))
        tt(dve, modrow, modrow[0:1, g * 512:(g + 1) * 512], ps, ps[0:1, :], adab, adab[0:1, g * 512:(g + 1) * 512], ALU.add)
    dump("modrow", modrow, modrow[:])
    ps = nps()
    for pi, off in enumerate((0, 1024, 3072, 4096)):
        for j in range(8):
            mm(ps, ps[:, pi * 8 + j:pi * 8 + j + 1], modrow, modrow[0:1, off + j * 128:off + (j + 1) * 128],
               ones1, ones1[0:1, 0:1])
    cp(dve, modcol, modcol[:], ps, ps[:, 0:32])
    gcol = P.sb("gcol", [128, 16])
    dma(sp, gcol, gcol[:, 0:8], None, L["g1col_d"])
    dma(sp, gcol, gcol[:, 8:16], None, L["g2col_d"])
    stt(gs, gs[:, 0:8], modcol, modcol[:, 8:16], 1.0, gcol, gcol[:, 0:8], ALU.add, ALU.mult)
    stt(gs, gs[:, 8:16], modcol, modcol[:, 24:32], 1.0, gcol, gcol[:, 8:16], ALU.add, ALU.mult)
    g2rep = P.sb("g2rep", [128, D])
    dma(sp, g2rep, g2rep[:], None, L["g2row_d"].partition_broadcast(128))
    for ri, (name, off) in enumerate((("gt1", 2048), ("gt2", 5120), ("sc2", 4096), ("sh2", 3072))):
        t = stg[ri % 2]
        for hf in range(2):
            ps = nps()
            mm(ps, ps[:, :], ones1, ones1[0:1, :], modrow, modrow[0:1, off + hf * 512:off + (hf + 1) * 512])
            cp(act, t, t[:, hf * 512:(hf + 1) * 512], ps, ps[:, :])
        if name == "sc2":
            stt(t, t[:, 0:D], t, t[:, 0:D], 1.0, g2rep, g2rep[:], ALU.add, ALU.mult)
        dma(sp, rep_d, rep_d[ri], t, t[:, 0:D])
    dump("gs", gs, gs[:])
    P.pop()
    L2 = dict(L)
    L2.update(locals())
    return L2


def _consts():
    c = np.zeros((128, 2048), np.float32)
    i = np.arange(128)
    c[:, 0:128] = np.eye(128)
    c[:, 128:256] = np.eye(128)[::-1]
    same = (i[:, None] // 64) == (i[None, :] // 64)
    s_le_t = same & (i[:, None] <= i[None, :])
    s_lt_t = same & (i[:, None] < i[None, :])
    s_gt_t = same & (i[:, None] > i[None, :])
    c[:, 256:384] = LOGC * s_le_t
    c[:, 384:512] = LOGC * s_gt_t
    c[:, 512:640] = s_lt_t
    c[:, 640:768] = s_le_t
    c[:, 768:896] = s_gt_t
    c[:, 896:1024] = s_lt_t
    c[:, 1024:1152] = s_le_t
    c[:, 1152] = LOGC
    c[:, 1153] = 1.0
    c[:, 1280:1296] = np.arange(16)[None, :]
    return c


def host_layout(inp):
    f = np.float32
    w_in = inp["w_in"][0]
    mu = inp["mu_shift"][0]
    sl = lambda a, b: list(range(a, b))
    r_, k_, v_ = sl(0, 512), sl(512, 1024), sl(1024, 1536)
    wl = [sl(1536, 1600), sl(1600, 1664)]
    al = [sl(1664, 1728), sl(1728, 1792)]
    gl, q_, ak, av = sl(1792, 1920), sl(1920, 2432), sl(2432, 2560), sl(2560, 2688)
    mu_ext = np.concatenate([mu, np.zeros(768, f)])
    cols_a = r_ + k_ + v_ + wl[0] + al[0] + wl[1] + al[1] + gl + q_ + ak + av
    cols_b = r_ + k_ + v_ + wl[1] + al[1]
    w2, a2, w0, a0 = inp["w2"][0], inp["a2"][0], inp["w0"][0], inp["a0"][0]
    vecs = np.zeros((8, 512), f)
    vecs[0] = inp["k_k"][0]; vecs[1] = inp["k_a"][0]; vecs[2] = inp["ln_x_w"][0]; vecs[3] = inp["ln_x_b"][0]
    vecs[4] = inp["r_k"][0].reshape(-1); vecs[5] = inp["attn_out_g"][0]
    vecs[6] = np.tile(inp["q_norm_g"][0], 8); vecs[7] = np.tile(inp["k_norm_g"][0], 8)
    inv_freq = (10000.0 ** (-np.arange(0, 32, 2, dtype=f) / 32.0)).astype(f)

    def rope_tab(pos):
        rows = (pos // 64).astype(f); colsp = (pos % 64).astype(f)
        ar = rows[:, None] * inv_freq[None, :]; ac = colsp[:, None] * inv_freq[None, :]
        return np.concatenate([np.cos(ar), np.cos(ac), np.sin(ar), np.sin(ac)], axis=1).astype(f)

    consts = _consts()
    shared = dict(
        ada_w=inp["ada_w"][0], ada_b=inp["ada_b"], g1col=inp["norm1_g"][0].reshape(8, 128).T.copy(),
        g2col=inp["norm2_g"][0].reshape(8, 128).T.copy(), g2row=inp["norm2_g"],
        g2lora=inp["g2"][0], vecs=vecs, w_out=inp["w_out"][0], wq=inp["peer_wq"][0],
        skT=np.ascontiguousarray(inp["peer_sub_keys"][0].transpose(0, 2, 1)),
        u_tab=inp["peer_u"][0], v_tab=inp["peer_v"][0], consts=consts,
        wst0=np.ascontiguousarray(w_in[:, cols_a]), mu0=mu_ext[cols_a][None, :].copy(),
        wst1=np.ascontiguousarray(w_in[:, cols_b]), mu1=mu_ext[cols_b][None, :].copy(),
    )
    wc_cache = {}
    maps = []
    for core in range(8):
        xs = np.zeros((3, 33 * 128, D), f)
        fl = np.zeros((128, 16), f)
        if core < 4:
            xfull = inp["x_prompt"][core]; base = 0; c = inp["c_prompt"][core]; dc = 0
            own = xfull
            fl[:, 8] = NEG
            pos_c = np.zeros(4096, np.int64)
            xs[2, :4096] = xfull
        else:
            s, half = (core - 4) // 2, (core - 4) % 2
            xfull = inp["x_sample"][s]; base = half * 4096; c = inp["c_sample"][s]
            own = xfull[base:base + 4096]
            if half == 0:
                dc = 1
                ctxs = xfull[4096:8192][::-1]; pos_c = np.arange(8191, 4095, -1)
                xs[0, 4097] = xfull[4096]; fl[:, 1] = 1
                xs[1, 4096] = xfull[4096]; fl[:, 2] = 1
                xs[2, 4097] = xfull[4095]; fl[:, 5] = 1
                fl[:, 7] = 1
            else:
                dc = 0
                ctxs = xfull[0:4096]; pos_c = np.arange(0, 4096)
                xs[0, 4096] = xfull[4095]; fl[:, 0] = 1
                xs[1, 4097] = xfull[4095]; fl[:, 3] = 1
                xs[2, 4097] = xfull[4096]; fl[:, 5] = 1
                fl[:, 6] = 1
            xs[2, :4096] = ctxs
        xs[0, :4096] = own
        xs[1, :4096] = own[::-1]
        if dc not in wc_cache:
            cols_c = k_ + v_ + wl[dc] + al[dc] + ak + av
            wc_cache[dc] = (np.ascontiguousarray(w_in[:, cols_c]), mu_ext[cols_c][None, :].copy())
        w2a2 = np.zeros((4, 2, 128, 512), f); w0a0 = np.zeros((4, 2, 512), f)
        for slot, d in enumerate((0, 1, 1, dc)):
            w2a2[slot, 0, :64] = w2[d]; w2a2[slot, 1, 64:] = a2[d]; w0a0[slot, 0] = w0[d]; w0a0[slot, 1] = a0[d]
        rope = np.stack([rope_tab(base + np.arange(4096)), rope_tab(base + np.arange(4096)), rope_tab(pos_c)])
        m = dict(shared)
        m.update(xs=xs, flags=fl, ccol=c.reshape(8, 128).T.copy(), wst2=wc_cache[dc][0], mu2=wc_cache[dc][1],
                 w2a2=w2a2, w0a0=w0a0, rope=rope)
        maps.append(m)
    return maps


def kernel(**inputs):
    inp = {k: np.asarray(v) for k, v in inputs.items()}
    maps = host_layout(inp)
    nc = build_program()
    res = run_bass_kernel_spmd(nc, maps, core_ids=list(range(8)))
    outs = [np.asarray(r["y_out"], dtype=np.float32) for r in res.results]
    y_p = np.stack(outs[0:4])
    y_s = np.stack([np.concatenate([outs[4], outs[5]]), np.concatenate([outs[6], outs[7]])])
    return (y_p, y_s)


def _emit_streams(P, nc, L):
    pe, dve, act, pool, sp = P.pe, P.dve, P.act, P.pool, P.sp
    mm, tr, tt, ts, stt, actf, cp, dma, dump, rsqrt = (L[k] for k in ("mm", "tr", "tt", "ts", "stt", "actf", "cp", "dma", "dump", "rsqrt"))
    nps, tpb, cst, identb, identf, flags, gs, modcol = (L[k] for k in (
        "nps", "tpb", "cst", "identb", "identf", "flags", "gs", "modcol"))
    xs, NCOL, dbg_t = L["xs"], L["NCOL"], L["dbg_t"]
    sb = P.sb
    MASKA, MASKB = cst[:, 512:896], cst[:, 896:1152]
    UTs, SLs, negc = cst[:, 256:384], cst[:, 384:512], cst[:, 1152:1153]
    WG = 256

    Hctx = sb("Hctx", [64, 8, 64])
    yscr = [P.dram(f"yscr{i}", [T_OWN, 512]) for i in range(2)]
    qT_d = P.dram("qT_d", [128, 4, T_OWN], BF16)
    kT_d = P.dram("kT_d", [128, 2, 2 * T_OWN], BF16)
    vx_d = P.dram("vx_d", [128, 64, 130], BF16)
    gate_d = P.dram("gate_d", [T_OWN, 512])
    bon_d = P.dram("bon_d", [T_OWN, 512])
    L.update(yscr=yscr, qT_d=qT_d, kT_d=kT_d, vx_d=vx_d, gate_d=gate_d, bon_d=bon_d)

    cfgs = {
        "c": dict(si=2, slot=3, need_y=False, nshift=1152, offs=dict(k=0, v=512, lora=1024, akv=1152)),
        "a": dict(si=0, slot=0, need_y=True, nshift=1920,
                  offs=dict(r=0, k=512, v=1024, lora=1536, lora2=1664, gl=1792, q=1920, akv=2432)),
        "b": dict(si=1, slot=2, need_y=True, nshift=1664, offs=dict(r=0, k=512, v=1024, lora=1536)),
    }
    order = L.get("stream_order", ("c", "a", "b"))
    for sname in order:
        cf = cfgs[sname]
        si, offs, ncol, nsh, need_y, slot = cf["si"], cf["offs"], NCOL[cf["si"]], cf["nshift"], cf["need_y"], cf["slot"]
        is_a = sname == "a"
        P.push()
        W1 = sb("W1", [128, 8, ncol], BF16)
        W2 = sb("W2", [128, 8, nsh], BF16)
        P.push()
        murep = sb("murep", [128, ncol])
        dma(sp, murep, murep[:], None, L["mu_d"][si].partition_broadcast(128))
        wstg = [sb(f"wstg{i}", [128, 1024]) for i in range(2)]
        wtmp = sb("wtmp", [128, 1024])
        wv = L["wst_d"][si].rearrange("(j p) n -> p j n", p=128)
        k = 0
        for j in range(8):
            for p0 in range(0, ncol, 1024):
                n = min(1024, ncol - p0)
                b = wstg[k % 2]; k += 1
                dma(sp, b, b[:, 0:n], None, wv[:, j, p0:p0 + n])
                tt(dve, wtmp, wtmp[:, 0:n], b, b[:, 0:n], murep, murep[:, p0:p0 + n], ALU.mult)
                tt(pool, W1, W1[:, j, p0:p0 + n], b, b[:, 0:n], wtmp, wtmp[:, 0:n], ALU.subtract)
                if p0 < nsh:
                    m = min(n, nsh - p0)
                    ts(dve, W2, W2[:, j, p0:p0 + m], wtmp, wtmp[:, 0:m], 0.5, None, ALU.mult)
        P.pop()
        if L.get("stop_at") == 1:
            P.pop(); return L
        hring = sb("hring", [128, 4, 8, 128], BF16)
        haloT = sb("haloT", [128, 8, 2], BF16)
        xbuf = [sb("xbuf0", [128, D])]
        xn = sb("xn", [128, D], BF16)
        ssq = sb("ssq", [128, 4])
        hsT = sb("hsT", [128, 8, 128], BF16)
        lora_t = [sb(f"lora{i}", [128, 128], BF16) for i in range(3)]
        w2a2 = sb("w2a2s", [128, 2, 2, 512], BF16)
        w0a0 = sb("w0a0s", [128, 2, 2, 512])
        vecs = sb("vecs", [128, 2, 512])
        qkg = sb("qkg", [128, 2, 512])
        rkrep = sb("rkrep", [128, 512])
        g2l = sb("g2l", [128, 512], BF16)
        P.push()
        wst = sb("w2a2f", [128, 2, 2, 512])
        slots = (slot, 1) if is_a else (slot, slot)
        for q_, sl_ in enumerate(slots):
            dma(sp, wst, wst[:, q_, :, :], None, L["w2a2_d"][sl_].rearrange("k p n -> p k n"))
            dma(sp, w0a0, w0a0[:, q_, :, :].rearrange("p k n -> p (k n)"), None,
                L["w0a0_d"][sl_].rearrange("k n -> (k n)").partition_broadcast(128))
        cp(dve, w2a2, w2a2[:], wst, wst[:])
        dma(sp, vecs, vecs[:].rearrange("p k n -> p (k n)"), None,
            L["vecs_d"][0:2].rearrange("k n -> (k n)").partition_broadcast(128))
        dma(sp, qkg, qkg[:].rearrange("p k n -> p (k n)"), None,
            L["vecs_d"][6:8].rearrange("k n -> (k n)").partition_broadcast(128))
        dma(sp, rkrep, rkrep[:], None, L["vecs_d"][4:5].rearrange("k n -> (k n)").partition_broadcast(128))
        wst2 = sb("g2lf", [128, 512])
        dma(sp, wst2, wst2[:], None, L["g2lora_d"])
        cp(dve, g2l, g2l[:], wst2, wst2[:])
        P.pop()
        if L.get("stop_at") == 2:
            P.pop(); return L

        def f32t(n):
            return sb(n, [128, WG])

        def bft(n):
            return sb(n, [128, WG], BF16)
        kk, a_t, kd, sg, eL, enL, eLx, eh, tmpA, tmpB = (f32t(n) for n in (
            "kk", "a_t", "kd", "sg", "eL", "enL", "eLx", "eh", "tmpA", "tmpB"))
        TMB, TMK, TMA, TMR, Bh, Kh, Vt = (bft(n) for n in ("TMB", "TMK", "TMA", "TMR", "Bh", "Kh", "Vt"))
        TMq = [TMB, TMK, TMA, TMR]
        FM = sb("FM", [128, 8, 128], BF16)
        s8 = sb("s8", [128, 8])
        WCt = sb("WCt", [64, 8])
        SC1 = sb("SC1", [128, 8, 384], BF16)
        SC2 = sb("SC2", [128, 8, 256], BF16)
        LmT = [sb(f"LmT{i}", [128, 8, 256], BF16) for i in range(2)]
        Xb = [sb(f"Xb{i}", [128, 8, 128], BF16) for i in range(2)]
        GD = sb("GD", [64, 16, 128])
        QmT = sb("QmT", [64, 16, 64])
        Y0 = sb("Y0", [128, 8, 64])
        Hs = [sb(f"H{i}", [64, 8, 64]) for i in range(2)]
        ytile = sb("ytile", [128, 512])
        qn = sb("qn", [128, 512]); qr = sb("qr", [128, 512], BF16)
        qsq = qn
        rt = [tmpA, tmpB]
        ksb = sb("ksb", [128, WG]); rsb = sb("rsb", [128, WG])
        ropet = sb("ropet", [128, 64])
        vx = sb("vx", [128, 2, 65], BF16)
        P.I(pool, lambda e: e.memset(vx[:], 1.0), W=[vx])
        qTs = sb("qTs", [128, 5, 128], BF16)
        qd = sb("qd", [128, 256], BF16)
        a2_t = f32t("a2_t"); gt_t = f32t("gt_t"); bon = f32t("bon")

        if sname == "c":
            P.I(pool, lambda e: e.memset(Hs[0][:], 0.0), W=[Hs[0]])
        else:
            fc = 6 if sname == "a" else 7
            ts(dve, Hs[0], Hs[0][:], Hctx, Hctx[:], flags[0:64, fc:fc + 1], None, ALU.mult, sb=[flags])

        xk = [0]

        def build_h(ti, dest_b, dest):
            xt = xbuf[0]
            dma(sp, xt, xt[:], None, xs[si, ti * 128:(ti + 1) * 128, :])
            P.I(pool, lambda e: e.memset(ssq[:, 0:1], 0.0), W=[ssq])
            actf(xn, xn[:], xt, xt[:], AF.Square, accum=ssq[:, 0:1], accb=[ssq])
            ts(dve, ssq, ssq[:, 1:2], ssq, ssq[:, 0:1], 1.0 / D, 1e-6, ALU.mult, ALU.add)
            rsqrt(ssq, ssq[:, 2:3], ssq, ssq[:, 1:2])
            actf(xn, xn[:], xt, xt[:], AF.Copy, scale=ssq[:, 2:3], sb=[ssq])
            for j in range(8):
                tr(tpb, tpb[:, j * 128:(j + 1) * 128], xn, xn[:, j * 128:(j + 1) * 128], L["identb_t"], identb)
            for j in range(8):
                ts(dve, dest_b, dest[:, j, :], tpb, tpb[:, j * 128:(j + 1) * 128], gs[:, j:j + 1], modcol[:, j:j + 1],
                   ALU.mult, ALU.add, sb=[gs, modcol])

        hh = hring
        build_h(32, hring, hring[:, 3])
        fo = {"a": 0, "b": 2, "c": 4}[sname]
        ts(dve, haloT, haloT[:, :, 0:1], hring, hring[:, 3, :, 0:1], flags[:, fo:fo + 1], None, ALU.mult, sb=[flags])
        ts(dve, haloT, haloT[:, :, 1:2], hring, hring[:, 3, :, 1:2], flags[:, fo + 1:fo + 2], None, ALU.mult, sb=[flags])
        if L.get("stop_at") == 3:
            P.pop(); return L
        build_h(0, hring, hring[:, 0])

        hsTs = [hsT, sb("hsTb", [128, 8, 128], BF16)]
        loras = [lora_t, [sb(f"lorab{i}", [128, 128], BF16) for i in range(3)]]

        def mkproj(cur, hsT):
            def proj(ps_ap, ps_b, off, n, shift=True, fm=False):
                nmm = 16 if shift else 8
                c = 0
                for (hb, hap, Wt) in ((hring, cur, W1), (hsT, hsT[:], W2)):
                    if Wt is W2 and not shift:
                        continue
                    for j in range(8):
                        if fm:
                            mm(ps_b, ps_ap, Wt, Wt[:, j, off:off + n], hb, hap[:, j, :], start=(c == 0), stop=(c == nmm - 1))
                        else:
                            mm(ps_b, ps_ap, hb, hap[:, j, :], Wt, Wt[:, j, off:off + n], start=(c == 0), stop=(c == nmm - 1))
                        c += 1

            return proj

        fpi = [0]

        def npsf():
            b_ = L["pps"][5 + fpi[0] % 2]
            fpi[0] += 1
            return b_

        def front(i):
            nps = L["nps"]
            hsT, lora_t = hsTs[i % 2], loras[i % 2]
            if i + 1 < NT:
                build_h(i + 1, hring, hring[:, (i + 1) % 4])
            cur = hring[:, i % 4]
            yield
            prevcol = haloT[:, :, 0:1] if i == 0 else hring[:, (i - 1) % 4, :, 127:128]
            nextcol = haloT[:, :, 1:2] if i == NT - 1 else hring[:, (i + 1) % 4, :, 0:1]
            yield
            proj = mkproj(cur, hsT)
            tt(pool, hsT, hsT[:, :, 1:127], hring, cur[:, :, 0:126], hring, cur[:, :, 2:128], ALU.add)
            tt(pool, hsT, hsT[:, :, 0:1], hring if i else haloT, prevcol, hring, cur[:, :, 1:2], ALU.add)
            yield
            tt(pool, hsT, hsT[:, :, 127:128], hring, cur[:, :, 126:127], haloT if i == NT - 1 else hring, nextcol, ALU.add)

            lgroups = [("lora", 0)] + ([("lora2", 1), ("gl", 2)] if is_a else [])
            yield
            for nm, li in lgroups:
                ps = nps()
                proj(ps[:, 0:128], ps, offs[nm], 128, fm=True)
                if nm == "gl":
                    actf(lora_t[li], lora_t[li][:], ps, ps[:, 0:128], AF.Sigmoid)
                else:
                    actf(lora_t[li], lora_t[li][0:64, :], ps, ps[0:64, 0:128], AF.Tanh)
                    cp(dve, lora_t[li], lora_t[li][64:128, :], ps, ps[64:128, 0:128])

            if sname in ("a", "c"):
                dma(sp, ropet, ropet[:], None, L["rope_d"][si, i * 128:(i + 1) * 128, :])
                jobs = []
                if is_a:
                    pq_ = nps(); proj(pq_[:, :], pq_, offs["q"], 512, shift=False)
                    jobs.append((pq_, pq_[:, 0:512], 8, 0))
                pk_ = nps(); proj(pk_[:, 0:256], pk_, offs["akv"], 256, shift=False)
                jobs.append((pk_, pk_[:, 0:128], 2, 1))
                cp(act, vx, vx[:, :, 0:64], pk_, pk_[:, 128:256].rearrange("p (h n) -> p h n", h=2))
                kt_ = (i if is_a else 32 + i)
                dma(sp, vx_d, vx_d[:, kt_, :], vx, vx[:].rearrange("p h n -> p (h n)"))
                for (pb, pap, nh, gi) in jobs:
                    yield
                    w = nh * 64
                    actf(qsq, qsq[:, 0:w], pb, pap, AF.Square)
                    P.I(dve, lambda e, nh=nh, w=w: e.tensor_reduce(out=s8[:, 0:nh], in_=qsq[:, 0:w].rearrange("p (h n) -> p h n", h=nh),
                                                       axis=AX.X, op=ALU.add), R=[qsq], W=[s8])
                    ts(dve, s8, s8[:, 0:nh], s8, s8[:, 0:nh], 1.0 / 64, 1e-6, ALU.mult, ALU.add)
                    rsqrt(s8, s8[:, 0:nh], s8, s8[:, 0:nh])
                    tt(dve, qn, qn[:, 0:w].rearrange("p (h n) -> p h n", h=nh), pb, pap.rearrange("p (h n) -> p h n", h=nh),
                       s8, s8[:, 0:nh].unsqueeze(2).to_broadcast([128, nh, 64]), ALU.mult)
                    tt(pool, qn, qn[:, 0:w], qn, qn[:, 0:w], qkg, qkg[:, gi, 0:w], ALU.mult)
                    v5 = qn[:, 0:w].rearrange("p (h a f n) -> p h a f n", h=nh, a=2, f=2)
                    o5 = qr[:, 0:w].rearrange("p (h a f n) -> p h a f n", h=nh, a=2, f=2)
                    x1, x2 = v5[:, :, :, 0, :], v5[:, :, :, 1, :]
                    cs_ = ropet[:, 0:32].rearrange("p (a n) -> p a n", a=2).unsqueeze(1).to_broadcast([128, nh, 2, 16])
                    sn_ = ropet[:, 32:64].rearrange("p (a n) -> p a n", a=2).unsqueeze(1).to_broadcast([128, nh, 2, 16])
                    r0 = rt[0][:, 0:w // 2].rearrange("p (h a n) -> p h a n", h=nh, a=2)
                    r1 = rt[1][:, 0:w // 2].rearrange("p (h a n) -> p h a n", h=nh, a=2)
                    tt(dve, rt[0], r0, qn, x1, ropet, cs_, ALU.mult)
                    tt(pool, rt[1], r1, qn, x2, ropet, sn_, ALU.mult)
                    tt(dve, qr, o5[:, :, :, 0, :], rt[0], r0, rt[1], r1, ALU.subtract)
                    tt(dve, rt[0], r0, qn, x1, ropet, sn_, ALU.mult)
                    tt(pool, rt[1], r1, qn, x2, ropet, cs_, ALU.mult)
                    tt(dve, qr, o5[:, :, :, 1, :], rt[0], r0, rt[1], r1, ALU.add)
                    if gi == 0:
                        src_b, src, nt_ = qr, qr, 4
                    else:
                        cp(pool, qd, qd[:].rearrange("p (k d n) -> p k d n", k=2, d=2),
                           qr, qr[:, 0:128].rearrange("p (k n) -> p k n", k=2).unsqueeze(2).to_broadcast([128, 2, 2, 64]))
                        src_b, src, nt_ = qd, qd, 2
                    for t_ in range(nt_):
                        tr(tpb, tpb[:, t_ * 128:(t_ + 1) * 128], src_b, src[:, t_ * 128:(t_ + 1) * 128], L["identb_t"], identb)
                    cp(act, qTs, qTs[:, 0:nt_, :], tpb, tpb[:, 0:nt_ * 128].rearrange("p (t n) -> p t n", t=nt_))
                    if gi == 0:
                        dma(sp, qT_d, qT_d[:, :, i * 128:(i + 1) * 128], qTs, qTs[:, 0:4, :])
                    else:
                        dma(sp, kT_d, kT_d[:, :, kt_ * 128:(kt_ + 1) * 128], qTs, qTs[:, 0:2, :])


        def mkset(tag):
            return (sb("TMA" + tag, [128, WG], BF16), sb("TMR" + tag, [128, WG], BF16), sb("Bh" + tag, [128, WG], BF16),
                    sb("Kh" + tag, [128, WG], BF16), sb("Vt" + tag, [128, WG], BF16), sb("FM" + tag, [128, 8, 128], BF16), sb("WCt" + tag, [64, 8]))
        psets4 = [[(TMA, TMR, Bh, Kh, Vt, FM, WCt), mkset("b")], [mkset("c"), mkset("d")]]
        bpi = [0]

        def npsB():
            b_ = L["pps"][3 + bpi[0] % 4]
            bpi[0] += 1
            return b_

        def Pgen(i, hg):
            nps = L["nps"]
            hsT, lora_t = hsTs[i % 2], loras[i % 2]
            cur = hring[:, i % 4]
            proj = mkproj(cur, hsT)
            TMA, TMR, Bh, Kh, Vt, FM, WCt = psets4[i % 2][hg]
            TMq = [TMB, TMK, TMA, TMR]
            cg = slice(hg * WG, (hg + 1) * WG)
            pk = nps(); proj(pk[:, 0:WG], pk, offs["k"] + hg * WG, WG)
            cp(act, ksb, ksb[:], pk, pk[:, 0:WG])
            pv = nps(); proj(pv[:, 0:WG], pv, offs["v"] + hg * WG, WG)
            yield
            cp(act, Vt, Vt[:], pv, pv[:, 0:WG])
            if need_y:
                pr_ = nps(); proj(pr_[:, 0:WG], pr_, offs["r"] + hg * WG, WG)
                cp(act, rsb, rsb[:], pr_, pr_[:, 0:WG])
            pw = nps()
            yield
            mm(pw, pw[:, 0:WG], lora_t[0], lora_t[0][:, :], w2a2, w2a2[:, 0, 0, cg])
            mm(pw, pw[:, WG:2 * WG], lora_t[0], lora_t[0][:, :], w2a2, w2a2[:, 0, 1, cg])
            yield
            tt(dve, kk, kk[:], ksb, ksb[:], vecs, vecs[:, 0, cg], ALU.mult)
            yield
            tt(pool, tmpB, tmpB[:], kk, kk[:], kk, kk[:], ALU.mult)
            P.I(dve, lambda e: e.tensor_reduce(out=s8[:, 0:4], in_=tmpB[:].rearrange("p (h n) -> p h n", h=4),
                                               axis=AX.X, op=ALU.add), R=[tmpB], W=[s8])
            ts(dve, s8, s8[:, 0:4], s8, s8[:, 0:4], 1e-24, None, ALU.max)
            yield
            rsqrt(s8, s8[:, 0:4], s8, s8[:, 0:4])
            tt(pool, kk, kk[:].rearrange("p (h n) -> p h n", h=4), kk, kk[:].rearrange("p (h n) -> p h n", h=4),
               s8, s8[:, 0:4].unsqueeze(2).to_broadcast([128, 4, 64]), ALU.mult)
            tt(dve, tmpA, tmpA[:], pw, pw[:, WG:2 * WG], w0a0, w0a0[:, 0, 1, cg], ALU.add)
            yield
            actf(a_t, a_t[:], tmpA, tmpA[:], AF.Sigmoid)
            stt(tmpB, tmpB[:], a_t, a_t[:], -1.0, vecs, vecs[:, 1, cg], ALU.add, ALU.mult)
            stt(kd, kd[:], tmpB, tmpB[:], 1.0, ksb, ksb[:], ALU.add, ALU.mult)
            yield
            if is_a:
                pg2 = nps()
                mm(pg2, pg2[:, 0:WG], lora_t[2], lora_t[2][:, :], g2l, g2l[:, cg])
                mm(pg2, pg2[:, WG:2 * WG], lora_t[1], lora_t[1][:, :], w2a2, w2a2[:, 1, 1, cg])
                cp(act, gt_t, gt_t[:], pg2, pg2[:, 0:WG])
                dma(sp, gate_d, gate_d[i * 128:(i + 1) * 128, cg], gt_t, gt_t[:])
                tt(dve, tmpA, tmpA[:], pg2, pg2[:, WG:2 * WG], w0a0, w0a0[:, 1, 1, cg], ALU.add)
                actf(a2_t, a2_t[:], tmpA, tmpA[:], AF.Sigmoid)
                tt(pool, a2_t, a2_t[:], a2_t, a2_t[:], a_t, a_t[:], ALU.add)
                ts(dve, a2_t, a2_t[:], a2_t, a2_t[:], 0.5, -1.0, ALU.mult, ALU.add)
                tt(pool, a2_t, a2_t[:], a2_t, a2_t[:], vecs, vecs[:, 1, cg], ALU.mult)
                stt(bon, bon[:], a2_t, a2_t[:], 1.0, ksb, ksb[:], ALU.add, ALU.mult)
                tt(dve, bon, bon[:], bon, bon[:], rsb, rsb[:], ALU.mult)
                tt(pool, bon, bon[:], bon, bon[:], rkrep, rkrep[:, cg], ALU.mult)
                P.I(dve, lambda e: e.tensor_reduce(out=s8[:, 4:8], in_=bon[:].rearrange("p (h n) -> p h n", h=4),
                                                   axis=AX.X, op=ALU.add), R=[bon], W=[s8])
                tt(dve, bon, bon[:].rearrange("p (h n) -> p h n", h=4), Vt, Vt[:].rearrange("p (h n) -> p h n", h=4),
                   s8, s8[:, 4:8].unsqueeze(2).to_broadcast([128, 4, 64]), ALU.mult)
                dma(sp, bon_d, bon_d[i * 128:(i + 1) * 128, cg], bon, bon[:])
            yield
            tt(dve, tmpA, tmpA[:], pw, pw[:, 0:WG], w0a0, w0a0[:, 0, 0, cg], ALU.add)
            actf(sg, sg[:], tmpA, tmpA[:], AF.Sigmoid)
            yield
            pL = nps()
            mm(pL, pL[:, 0:WG], cst, UTs, sg, sg[:])
            mm(pL, pL[:, WG:2 * WG], cst, SLs, sg, sg[:])
            yield
            actf(enL, enL[:], pL, pL[:, 0:WG], AF.Exp, scale=-1.0)
            actf(eh, eh[:], pL, pL[:, WG:2 * WG], AF.Exp)
            stt(tmpA, tmpA[:], sg, sg[:], -LOGC, pL, pL[:, 0:WG], ALU.mult, ALU.add)
            yield
            actf(eLx, eLx[:], tmpA, tmpA[:], AF.Exp)
            stt(TMA, TMA[:], kk, kk[:], -1.0, eLx, eLx[:], ALU.mult, ALU.mult)
            tt(pool, tmpB, tmpB[:], kk, kk[:], a_t, a_t[:], ALU.mult)
            yield
            tt(pool, TMB, TMB[:], tmpB, tmpB[:], enL, enL[:], ALU.mult)
            tt(pool, Bh, Bh[:], tmpB, tmpB[:], eh, eh[:], ALU.mult)
            tt(dve, TMK, TMK[:], kd, kd[:], enL, enL[:], ALU.mult)
            yield
            tt(dve, Kh, Kh[:], kd, kd[:], eh, eh[:], ALU.mult)
            if need_y:
                actf(eL, eL[:], pL, pL[:, 0:WG], AF.Exp)
                tt(dve, TMR, TMR[:], rsb, rsb[:], eL, eL[:], ALU.mult)
            yield
            for c in range(2):
                pwc = nps()
                for hl in range(4):
                    mm(pwc, pwc[0:64, hl:hl + 1], sg, sg[c * 64:(c + 1) * 64, hl * 64:(hl + 1) * 64],
                       cst, negc[c * 64:(c + 1) * 64, :])
                actf(WCt, WCt[:, c * 4:(c + 1) * 4], pwc, pwc[0:64, 0:4], AF.Exp)
            yield
            yield
            nq = 4 if need_y else 3
            for hpl in range(2):
                for q_ in range(nq):
                    s_ = hpl * 4 + q_
                    tr(tpb, tpb[:, s_ * 128:(s_ + 1) * 128], TMq[q_], TMq[q_][:, hpl * 128:(hpl + 1) * 128], L["identb_t"], identb)
            if need_y:
                cp(act, FM, FM[:], tpb, tpb[:, :].rearrange("p (s n) -> p s n", s=8))
            else:
                for hpl in range(2):
                    cp(act, FM, FM[:, hpl * 4:hpl * 4 + 3, :], tpb,
                       tpb[:, hpl * 512:hpl * 512 + 384].rearrange("p (s n) -> p s n", s=3))

        def Mpart(i, pump):
            nps = npsB
            sets = psets4[i % 2]
            for hl in range(8):
                TMA, TMR, Bh, Kh, Vt, FM, WCt = sets[hl // 4]
                hq = hl % 4
                e_, hpl = hq % 2, hq // 2
                pr = slice(e_ * 64, (e_ + 1) * 64)
                fB, fK, fA = FM[pr, hpl * 4 + 0, :], FM[pr, hpl * 4 + 1, :], FM[pr, hpl * 4 + 2, :]
                p1 = nps(); p2 = nps()
                if need_y:
                    fAR = FM[pr, hpl * 4 + 2:hpl * 4 + 4, :]
                    mm(p1, p1[:, 0:256], FM, fB, FM, fAR)
                    mm(p2, p2[:, 0:256], FM, fK, FM, fAR)
                else:
                    mm(p1, p1[:, 0:128], FM, fB, FM, fA)
                    mm(p2, p2[:, 0:128], FM, fK, FM, fA)
                mm(p1, p1[:, 256:384], FM, fA, FM, fB)
                if need_y:
                    tt(dve, SC1, SC1[:, hl, :], p1, p1[:, 0:384], cst, MASKA, ALU.mult)
                    tt(dve, SC2, SC2[:, hl, :], p2, p2[:, 0:256], cst, MASKB, ALU.mult)
                else:
                    tt(dve, SC1, SC1[:, hl, 0:128], p1, p1[:, 0:128], cst, MASKA[:, 0:128], ALU.mult)
                    tt(dve, SC1, SC1[:, hl, 256:384], p1, p1[:, 256:384], cst, MASKA[:, 256:384], ALU.mult)
                    tt(dve, SC2, SC2[:, hl, 0:128], p2, p2[:, 0:128], cst, MASKB[:, 0:128], ALU.mult)
                pX = nps()
                mm(pX, pX[:, 0:64], SC2, SC2[:, hl, 0:128], Vt, Vt[:, hq * 64:(hq + 1) * 64])
                cp(act, Xb[0], Xb[0][:, hl, 0:64], pX, pX[:, 0:64])
                cp(pool, Xb[0], Xb[0][:, hl, 64:128], TMA, TMA[:, hq * 64:(hq + 1) * 64])
                if hl % 2 == 1:
                    pump()
            for lev in range(6):
                Xc, Xn_ = Xb[lev % 2], Xb[(lev + 1) % 2]
                for hl in range(8):
                    if lev == 0:
                        LTb, LTc, Lmb, Lmc = SC1, SC1[:, hl, 0:128], SC1, SC1[:, hl, 256:384]
                    else:
                        src = LmT[(lev - 1) % 2]
                        LTb, LTc, Lmb, Lmc = src, src[:, hl, 128:256], src, src[:, hl, 0:128]
                    pa = nps()
                    mm(pa, pa[:, 0:128], LTb, LTc, Xc, Xc[:, hl, :])
                    if lev < 5:
                        mm(pa, pa[:, 128:256], LTb, LTc, Lmb, Lmc)
                        mm(pa, pa[:, 256:384], Lmb, Lmc, LTb, LTc)
                    tt(dve, Xn_, Xn_[:, hl, :], Xc, Xc[:, hl, :], pa, pa[:, 0:128], ALU.add)
                    if lev < 5:
                        cp(act, LmT[lev % 2], LmT[lev % 2][:, hl, :], pa, pa[:, 128:384])
                    if hl % 4 == 3:
                        pump()
            Xf = Xb[0]
            for hl in range(8):
                TMA, TMR, Bh, Kh, Vt, FM, WCt = sets[hl // 4]
                hq = hl % 4
                hs_ = slice(hq * 64, (hq + 1) * 64)
                for c in range(2):
                    pg = nps()
                    cs = slice(c * 64, (c + 1) * 64)
                    mm(pg, pg[0:64, 0:64], Xf, Xf[cs, hl, 64:128], Bh, Bh[cs, hs_])
                    mm(pg, pg[0:64, 64:128], Bh, Bh[cs, hs_], Xf, Xf[cs, hl, 0:64], start=True, stop=False)
                    mm(pg, pg[0:64, 64:128], Kh, Kh[cs, hs_], Vt, Vt[cs, hs_], start=False, stop=True)
                    cp(act, GD, GD[:, hl * 2 + c, :], pg, pg[0:64, 0:128])
                if need_y:
                    for c in range(2):
                        pq2 = nps()
                        cs = slice(c * 64, (c + 1) * 64)
                        mm(pq2, pq2[0:64, 0:64], Xf, Xf[cs, hl, 64:128], SC1, SC1[cs, hl, 128 + c * 64:128 + (c + 1) * 64],
                           start=True, stop=False)
                        mm(pq2, pq2[0:64, 0:64], TMR, TMR[cs, hs_], L["identb_t"], identb[cs, c * 64:(c + 1) * 64],
                           start=False, stop=True)
                        cp(act, QmT, QmT[:, hl * 2 + c, :], pq2, pq2[0:64, 0:64])
                    py = nps()
                    mm(py, py[:, 0:64], SC1, SC1[:, hl, 128:256], Xf, Xf[:, hl, 0:64], start=True, stop=False)
                    mm(py, py[:, 0:64], SC2, SC2[:, hl, 128:256], Vt, Vt[:, hs_], start=False, stop=True)
                    cp(act, Y0, Y0[:, hl, :], py, py[:, 0:64])
                if hl % 2 == 1:
                    pump()
            for c in range(2):
                cs = slice(c * 64, (c + 1) * 64)
                Hc_b, Hn_b = Hs[c], Hs[1 - c]
                ph = nps()
                pyc = nps() if need_y else None
                for hl in range(8):
                    WCt = sets[hl // 4][6]
                    hq = hl % 4
                    hs_ = slice(hl * 64, (hl + 1) * 64)
                    if need_y:
                        mm(pyc, pyc[cs, hs_], QmT, QmT[:, hl * 2 + c, :], Hc_b, Hc_b[:, hl, :])
                    mm(ph, ph[0:64, hs_], GD, GD[:, hl * 2 + c, 0:64], Hc_b, Hc_b[:, hl, :], start=True, stop=False)
                    mm(ph, ph[0:64, hs_], cst, identf[0:64, 0:64], GD, GD[:, hl * 2 + c, 64:128], start=False, stop=True)
                    stt(Hn_b, Hn_b[:, hl, :], Hc_b, Hc_b[:, hl, :], WCt[:, c * 4 + hq:c * 4 + hq + 1], ph, ph[0:64, hs_],
                        ALU.mult, ALU.add, sb=[WCt])
                if need_y:
                    tt(dve, ytile, ytile[cs, :], Y0, Y0[cs, :, :].rearrange("p h n -> p (h n)"), pyc, pyc[cs, 0:512], ALU.add)
                pump()

        def Gchain(i):
            yield from front(i)
            yield from Pgen(i, 0)
            yield from Pgen(i, 1)

        def tile_post(i):
            if need_y:
                dma(sp, yscr[0 if is_a else 1], yscr[0 if is_a else 1][i * 128:(i + 1) * 128, :], ytile, ytile[:])
                if ("ytile_" + sname) in dbg_t and i < 2:
                    dma(sp, Buf(None), dbg_t["ytile_" + sname][i * 128:(i + 1) * 128, :], ytile, ytile[:])


        def run_all(gen):
            for _ in gen:
                pass

        def mkpump(gen, k):
            def pump():
                for _ in range(k):
                    try:
                        next(gen)
                    except StopIteration:
                        return
            return pump
        ntl = L.get("nt_limit", NT)
        L["npool"][0] = 3
        run_all(Gchain(0))
        for i in range(ntl):
            g2 = Gchain(i + 1) if i + 1 < ntl else iter(())
            Mpart(i, mkpump(g2, L.get("spump_k2", 4)))
            run_all(g2)
            tile_post(i)
        L["npool"][0] = 7
        if sname == "c":
            cp(dve, Hctx, Hctx[:], Hs[0], Hs[0][:])
            if "hctx" in dbg_t:
                dma(sp, Buf(None), dbg_t["hctx"][:, :], Hctx, Hctx[:].rearrange("p h n -> p (h n)"))
        P.pop()
    return L


def _emit_attention(P, nc, L):
    pe, dve, act, pool, sp = P.pe, P.dve, P.act, P.pool, P.sp
    mm, tr, tt, ts, stt, actf, cp, dma, dump, rsqrt = (L[k] for k in ("mm", "tr", "tt", "ts", "stt", "actf", "cp", "dma", "dump", "rsqrt"))
    nps, cst, identf, flags, pps, npool = (L[k] for k in ("nps", "cst", "identf", "flags", "pps", "npool"))
    dbg_t = L["dbg_t"]
    sb = P.sb
    P.push()
    qT = sb("qT", [128, 4, T_OWN], BF16)
    dma(sp, qT, qT[:], L["qT_d"], L["qT_d"][:])
    kT = sb("kT2", [128, 2, 2 * T_OWN], BF16)
    dma(sp, kT, kT[:], L["kT_d"], L["kT_d"][:])
    Vx = sb("Vx", [128, 64, 130], BF16)
    dma(sp, Vx, Vx[:], L["vx_d"], L["vx_d"][:])
    outg = sb("outg", [128, 512])
    dma(sp, outg, outg[:], None, L["vecs_d"][5:6].rearrange("k n -> (k n)").partition_broadcast(128))
    pT = [sb(f"pT{i}", [128, 512], BF16) for i in range(3)]
    oTs = sb("oTs", [65, 512])
    osb = sb("osb", [128, 4, 65])
    o2 = sb("o2", [128, 4, 64])
    ssa = sb("ssa", [128, 8])
    yat = sb("yat", [128, 4, 64])
    yattn_d = P.dram("yattn_d", [T_OWN, 512])
    L["yattn_d"] = yattn_d
    npool[0] = 5
    acc = [pps[5], pps[6]]
    it = 0
    uv_bf = P.dram("uv_bf", [L["NEXP"], 2 * D], BF16)
    L["uv_bf"] = uv_bf
    cbf = [sb(f"cbf{i}", [128, 4096]) for i in range(2)]
    cbb = [sb(f"cbb{i}", [128, 4096], BF16) for i in range(2)]
    nchunk = L["NEXP"] // 512
    cast_jobs = [(tb, c) for tb in range(2) for c in range(nchunk)]

    def cast_job(n):
        if n >= len(cast_jobs):
            return
        tb, c = cast_jobs[n]
        src = (L["u_tab"], L["v_tab"])[tb]
        f_, b_ = cbf[n % 2], cbb[n % 2]
        P.D(pool, lambda e: e.dma_start(out=f_[:], in_=src[c * 512:(c + 1) * 512, :].rearrange("(p j) n -> p (j n)", j=4)), W=[f_])
        cp(dve, b_, b_[:], f_, f_[:])
        P.D(pool, lambda e: e.dma_start(out=uv_bf[c * 512:(c + 1) * 512, tb * D:(tb + 1) * D].rearrange("(p j) n -> p j n", j=4),
                                        in_=b_[:].rearrange("p (j n) -> p j n", j=4)), R=[b_], W=[uv_bf])
    njob = [0]
    nh_lim = L.get("attn_heads", 8)
    nqb_lim = L.get("attn_qb", 8)
    for h in range(nh_lim):
        e_, hp, kvh = h % 2, h // 2, h // 4
        pr = slice(e_ * 64, (e_ + 1) * 64)
        for qb in range(nqb_lim):
            po = acc[it % 2]; it += 1
            cast_job(njob[0]); njob[0] += 1
            def smm(kt):
                ps_ = nps()
                mm(ps_, ps_[:, :], kT, kT[pr, kvh, kt * 128:(kt + 1) * 128], qT, qT[pr, hp, qb * 512:(qb + 1) * 512])
                return ps_
            pend = [smm(0), smm(1)]
            for kt in range(64):
                ps = pend.pop(0)
                if kt + 2 < 64:
                    pend.append(smm(kt + 2))
                pt = pT[kt % 3]
                if kt >= 32:
                    actf(pt, pt[:], ps, ps[:, :], AF.Exp, bias=flags[:, 8:9], scale=0.125, sb=[flags])
                else:
                    actf(pt, pt[:], ps, ps[:, :], AF.Exp, scale=0.125)
                mm(po, po[0:65, :], Vx, Vx[:, kt, kvh * 65:(kvh + 1) * 65], pt, pt[:], start=(kt == 0), stop=(kt == 63))
            cp(act, oTs, oTs[:, :], po, po[0:65, :])
            ptp = nps()
            for t in range(4):
                tr(ptp, ptp[:, t * 65:(t + 1) * 65], oTs, oTs[0:65, t * 128:(t + 1) * 128], cst, identf[0:65, 0:65])
            cp(dve, osb, osb[:], ptp, ptp[:, 0:260].rearrange("p (t n) -> p t n", t=4))
            P.I(dve, lambda e: e.reciprocal(out=ssa[:, 0:4], in_=osb[:, :, 64]), R=[osb], W=[ssa])
            tt(dve, o2, o2[:], osb, osb[:, :, 0:64], ssa, ssa[:, 0:4].unsqueeze(2).to_broadcast([128, 4, 64]), ALU.mult)
            tt(pool, yat, yat[:], o2, o2[:], o2, o2[:], ALU.mult)
            P.I(dve, lambda e: e.tensor_reduce(out=ssa[:, 4:8], in_=yat[:], axis=AX.X, op=ALU.add), R=[yat], W=[ssa])
            ts(dve, ssa, ssa[:, 4:8], ssa, ssa[:, 4:8], 1.0 / 64, 1e-6, ALU.mult, ALU.add)
            rsqrt(ssa, ssa[:, 4:8], ssa, ssa[:, 4:8])
            tt(dve, o2, o2[:], o2, o2[:], ssa, ssa[:, 4:8].unsqueeze(2).to_broadcast([128, 4, 64]), ALU.mult)
            tt(pool, yat, yat[:], o2, o2[:], outg, outg[:, h * 64:(h + 1) * 64].unsqueeze(1).to_broadcast([128, 4, 64]), ALU.mult)
            dma(sp, yattn_d, yattn_d[qb * 512:(qb + 1) * 512, h * 64:(h + 1) * 64].rearrange("(t p) n -> p t n", p=128), yat, yat[:])
            if "yattn" in dbg_t and qb == 0:
                dma(sp, Buf(None), dbg_t["yattn"][:, h * 64:(h + 1) * 64].rearrange("(t p) n -> p t n", p=128), yat, yat[:])
    while njob[0] < len(cast_jobs):
        cast_job(njob[0]); njob[0] += 1
    npool[0] = 7
    P.pop()
    return L


def _emit_final(P, nc, L):
    pe, dve, act, pool, sp = P.pe, P.dve, P.act, P.pool, P.sp
    mm, tr, tt, ts, stt, actf, cp, dma, dump, rsqrt = (L[k] for k in ("mm", "tr", "tt", "ts", "stt", "actf", "cp", "dma", "dump", "rsqrt"))
    nps, tpb, cst, identb, identf, flags = (L[k] for k in ("nps", "tpb", "cst", "identb", "identf", "flags"))
    dbg_t, xs, y_out = L["dbg_t"], L["xs"], L["y_out"]
    u_tab, v_tab = L["u_tab"], L["v_tab"]
    sb = P.sb
    yo = Buf(None, "y_out")
    L["yo"] = yo
    P.push()
    Jf = cst[:, 128:256]
    wo = sb("wo", [128, 8, D], BF16)
    wqb = sb("wqb", [128, 8, 2048], BF16)
    skb = sb("skb", [128, 2, 128], BF16)
    P.push()
    wstg = [sb(f"fstg{i}", [128, 2048]) for i in range(2)]
    wov = L["w_out_d"].rearrange("(j p) n -> p j n", p=128)
    wqv = L["wq_d"].rearrange("(j p) n -> p j n", p=128)
    k = 0
    for j in range(8):
        b = wstg[k % 2]; k += 1
        dma(sp, b, b[:, 0:D], None, wov[:, j, :])
        cp(dve, wo, wo[:, j, :], b, b[:, 0:D])
        b = wstg[k % 2]; k += 1
        dma(sp, b, b[:, :], None, wqv[:, j, :])
        cp(pool, wqb, wqb[:, j, :], b, b[:, :])
    b = wstg[0]
    dma(sp, b, b[:, 0:256].rearrange("p (k n) -> p k n", k=2), None, L["skT_d"].rearrange("k p n -> p k n"))
    cp(dve, skb, skb[:], b, b[:, 0:256].rearrange("p (k n) -> p k n", k=2))
    P.pop()
    reps = sb("reps", [128, 4, D])
    dma(sp, reps, reps[:], L["rep_d"], L["rep_d"][:].rearrange("r p n -> p r n"))
    lnr = sb("lnr", [128, 2, 512])
    dma(sp, lnr, lnr[:].rearrange("p k n -> p (k n)"), None, L["vecs_d"][2:4].rearrange("k n -> (k n)").partition_broadcast(128))
    yf, yb_, ysb, ysq, bon, gat, yat = (sb(n, [128, 512]) for n in ("yf", "yb", "ysb", "ysq", "bonf", "gatf", "yatf"))
    st8 = sb("st8", [128, 40])
    mixin = sb("mixin", [128, D], BF16)
    mT = sb("mT", [128, 8, 128], BF16)
    x1 = sb("x1", [128, D]); h2 = sb("h2", [128, D])
    tmpx = h2
    h2b = sb("h2b", [128, D], BF16)
    h2T = sb("h2T", [128, 8, 128], BF16)
    qpT = sb("qpT", [128, 16, 128], BF16)
    sc = sb("sc", [128, 16, 128]); scw = sb("scw", [128, 16, 128])
    sv = sb("sv", [128, 16, 16]); si = sb("si", [128, 16, 16], U32); sif = sb("sif", [128, 16, 16])
    cand = sc; candw = scw
    tv = sb("tv", [128, 8, 16]); ti = sb("ti", [128, 8, 16], U32)
    thi = sb("thi", [128, 8, 16], U32); tlo = sb("tlo", [128, 8, 16], U32)
    thif = sb("thif", [128, 8, 16]); tlof = sb("tlof", [128, 8, 16])
    ge = sb("ge", [128, 8, 16]); gate = sb("gate", [128, 128])
    eq = scw
    sel = sb("sel", [128, 2, 128])
    eidx = sb("eidx", [128, 128], I32)
    GS = 4
    NG = 3
    uvs = [[sb(f"uv{g}_{k}", [128, 2 * D], BF16) for k in range(GS)] for g in range(NG)]
    NPB = 3
    prods = [sb(f"prod{i}", [128, D], BF16) for i in range(NPB)]
    acc = h2
    dgs = [sb(f"dg{i}", [128, 128], BF16) for i in range(4)]
    pacc = [L["pps"][5], L["pps"][6]]
    L["npool"][0] = 5
    avs = [sb(f"av{i}", [128, GS]) for i in range(2)]
    gls = [sb(f"gl{i}", [128, GS]) for i in range(2)]
    wws = [sb(f"ww{i}", [128, GS]) for i in range(2)]
    iota16 = cst[:, 1280:1296]
    nt_lim = L.get("final_tiles", NT)
    x1s = [x1, sb("x1b", [128, D])]
    h2bs = [h2b, sb("h2bb", [128, D], BF16)]
    gates = [gate, sb("gateb", [128, 128])]
    eidxs = [eidx, sb("eidxb", [128, 128], I32)]

    def prefix(i, par):
        x1, h2b, gate, eidx = x1s[par], h2bs[par], gates[par], eidxs[par]
        rows = slice(i * 128, (i + 1) * 128)
        dma(sp, yf, yf[:], L["yscr"][0], L["yscr"][0][rows, :])
        dma(sp, yb_, yb_[:], L["yscr"][1], L["yscr"][1][(NT - 1 - i) * 128:(NT - i) * 128, :])
        yield
        dma(sp, bon, bon[:], L["bon_d"], L["bon_d"][rows, :])
        dma(sp, gat, gat[:], L["gate_d"], L["gate_d"][rows, :])
        dma(sp, yat, yat[:], L["yattn_d"], L["yattn_d"][rows, :])
        yield
        dma(sp, x1, x1[:], None, xs[0, rows, :])
        ps = nps()
        mm(ps, ps[:, :], cst, identf, yf, yf[:], start=True, stop=False)
        yield
        mm(ps, ps[:, :], cst, Jf, yb_, yb_[:], start=False, stop=True)
        cp(act, ysb, ysb[:], ps, ps[:, :])
        y3 = ysb[:].rearrange("p (h n) -> p h n", h=8)
        yield
        P.I(dve, lambda e: e.tensor_reduce(out=st8[:, 0:8], in_=y3, axis=AX.X, op=ALU.add), R=[ysb], W=[st8])
        tt(pool, ysq, ysq[:], ysb, ysb[:], ysb, ysb[:], ALU.mult)
        P.I(dve, lambda e: e.tensor_reduce(out=st8[:, 8:16], in_=ysq[:].rearrange("p (h n) -> p h n", h=8), axis=AX.X, op=ALU.add),
            R=[ysq], W=[st8])
        yield
        ts(dve, st8, st8[:, 0:8], st8, st8[:, 0:8], 1.0 / 64, None, ALU.mult)
        tt(dve, st8, st8[:, 16:24], st8, st8[:, 0:8], st8, st8[:, 0:8], ALU.mult)
        stt(st8, st8[:, 24:32], st8, st8[:, 8:16], 1.0 / 64, st8, st8[:, 16:24], ALU.mult, ALU.subtract)
        yield
        ts(dve, st8, st8[:, 24:32], st8, st8[:, 24:32], 64e-5, None, ALU.add)
        rsqrt(st8, st8[:, 24:32], st8, st8[:, 24:32])
        tt(dve, ysb, y3, ysb, y3, st8, st8[:, 0:8].unsqueeze(2).to_broadcast([128, 8, 64]), ALU.subtract)
        yield
        tt(dve, ysb, y3, ysb, y3, st8, st8[:, 24:32].unsqueeze(2).to_broadcast([128, 8, 64]), ALU.mult)
        tt(pool, ysb, ysb[:], ysb, ysb[:], lnr, lnr[:, 0, :], ALU.mult)
        tt(pool, ysb, ysb[:], ysb, ysb[:], lnr, lnr[:, 1, :], ALU.add)
        yield
        tt(pool, ysb, ysb[:], ysb, ysb[:], bon, bon[:], ALU.add)
        tt(dve, mixin, mixin[:, 0:512], ysb, ysb[:], gat, gat[:], ALU.mult)
        cp(pool, mixin, mixin[:, 512:1024], yat, yat[:])
        yield
        if "yrwkv" in dbg_t and i < 2:
            tt(pool, ysb, ysb[:], ysb, ysb[:], gat, gat[:], ALU.mult)
            dma(sp, Buf(None), dbg_t["yrwkv"][rows, :], ysb, ysb[:])
        for j in range(8):
            tr(tpb, tpb[:, j * 128:(j + 1) * 128], mixin, mixin[:, j * 128:(j + 1) * 128], L["identb_t"], identb)
        cp(act, mT, mT[:], tpb, tpb[:, :].rearrange("p (j n) -> p j n", j=8))
        yield
        for hf in range(2):
            hs_ = slice(hf * 512, (hf + 1) * 512)
            ps = nps()
            for j in range(8):
                mm(ps, ps[:, :], mT, mT[:, j, :], wo, wo[:, j, hs_], start=(j == 0), stop=(j == 7))
            tt(dve, tmpx, tmpx[:, hs_], ps, ps[:, :], reps, reps[:, 0, hs_], ALU.mult)
            tt(pool, x1, x1[:, hs_], tmpx, tmpx[:, hs_], x1, x1[:, hs_], ALU.add)
        if "x1" in dbg_t and i < 2:
            dma(sp, Buf(None), dbg_t["x1"][rows, :], x1, x1[:])
        P.I(pool, lambda e: e.memset(st8[:, 32:33], 0.0), W=[st8])
        yield
        actf(h2b, h2b[:], x1, x1[:], AF.Square, accum=st8[:, 32:33], accb=[st8])
        ts(dve, st8, st8[:, 33:34], st8, st8[:, 32:33], 1.0 / D, 1e-6, ALU.mult, ALU.add)
        rsqrt(st8, st8[:, 33:34], st8, st8[:, 33:34])
        yield
        stt(h2, h2[:], x1, x1[:], st8[:, 33:34], reps, reps[:, 2, :], ALU.mult, ALU.mult, sb=[st8])
        tt(pool, h2, h2[:], h2, h2[:], reps, reps[:, 3, :], ALU.add)
        cp(act, h2b, h2b[:], h2, h2[:])
        yield
        for j in range(8):
            tr(tpb, tpb[:, j * 128:(j + 1) * 128], h2b, h2b[:, j * 128:(j + 1) * 128], L["identb_t"], identb)
        cp(act, h2T, h2T[:], tpb, tpb[:, :].rearrange("p (j n) -> p j n", j=8))
        for g4 in range(4):
            ps = nps()
            for gg in range(4):
                g = g4 * 4 + gg
                for j in range(8):
                    mm(ps, ps[:, gg * 128:(gg + 1) * 128], wqb, wqb[:, j, g * 128:(g + 1) * 128], h2T, h2T[:, j, :],
                       start=(j == 0), stop=(j == 7))
            cp(act, qpT, qpT[:, g4 * 4:(g4 + 1) * 4, :], ps, ps[:, :].rearrange("p (g n) -> p g n", g=4))
            yield
        yield
        for g4 in range(4):
            ps = nps()
            for gg in range(4):
                g = g4 * 4 + gg
                mm(ps, ps[:, gg * 128:(gg + 1) * 128], qpT, qpT[:, g, :], skb, skb[:, g % 2, :])
            cp(dve, sc, sc[:, g4 * 4:(g4 + 1) * 4, :], ps, ps[:, :].rearrange("p (g n) -> p g n", g=4))

        def top16(vals_b, vals, work_b, work, ov_b, ov, oi_b, oi):
            P.I(dve, lambda e: e.max(out=ov[:, 0:8], in_=vals), R=[vals_b], W=[ov_b])
            P.I(dve, lambda e: e.match_replace(out=work, in_to_replace=ov[:, 0:8], in_values=vals, imm_value=-1e30),
                R=[vals_b, ov_b], W=[work_b])
            P.I(dve, lambda e: e.max(out=ov[:, 8:16], in_=work), R=[work_b], W=[ov_b])
            P.I(dve, lambda e: e.max_index(out=oi[:, 0:8], in_max=ov[:, 0:8], in_values=vals), R=[vals_b, ov_b], W=[oi_b])
            P.I(dve, lambda e: e.max_index(out=oi[:, 8:16], in_max=ov[:, 8:16], in_values=vals), R=[vals_b, ov_b], W=[oi_b])
        for g in range(16):
            top16(sc, sc[:, g, :], scw, scw[:, g, :], sv, sv[:, g, :], si, si[:, g, :])
            if g % 2 == 1:
                yield
        yield
        sv4 = sv[:].rearrange("p (h k) n -> p h k n", k=2)
        candv = cand[:].rearrange("p g n -> p (g n)").rearrange("p (h n) -> p h n", h=8)
        candwv = candw[:].rearrange("p g n -> p (g n)").rearrange("p (h n) -> p h n", h=8)
        yield
        eqv = eq[:].rearrange("p g n -> p (g n)").rearrange("p (h a b) -> p h a b", h=8, a=16)
        tt(dve, cand, candv.rearrange("p h (a b) -> p h a b", a=16), sv, sv4[:, :, 0, :].unsqueeze(3).to_broadcast([128, 8, 16, 16]),
           sv, sv4[:, :, 1, :].unsqueeze(2).to_broadcast([128, 8, 16, 16]), ALU.add)
        for h in range(8):
            top16(cand, candv[:, h, :], candw, candwv[:, h, :], tv, tv[:, h, :], ti, ti[:, h, :])
            if h % 2 == 1:
                yield
        yield
        ts(dve, st8, st8[:, 0:8], tv, tv[:, :, 0], -1.0, None, ALU.mult)
        P.I(pool, lambda e: e.memset(st8[:, 8:16], 0.0), W=[st8])
        for h in range(8):
            actf(ge, ge[:, h, :], tv, tv[:, h, :], AF.Exp, bias=st8[:, h:h + 1], sb=[st8], accum=st8[:, 8 + h:9 + h], accb=[st8])
        yield
        P.I(dve, lambda e: e.reciprocal(out=st8[:, 16:24], in_=st8[:, 8:16]), R=[st8], W=[st8])
        tt(dve, gate, gate[:].rearrange("p (h n) -> p h n", h=8), ge, ge[:], st8, st8[:, 16:24].unsqueeze(2).to_broadcast([128, 8, 16]), ALU.mult)
        ts(dve, thi, thi[:], ti, ti[:], 4, None, ALU.logical_shift_right)
        yield
        ts(dve, tlo, tlo[:], ti, ti[:], 15, None, ALU.bitwise_and)
        cp(dve, thif, thif[:], thi, thi[:])
        cp(dve, tlof, tlof[:], tlo, tlo[:])
        yield
        cp(dve, sif, sif[:], si, si[:])
        sif4 = sif[:].rearrange("p (h k) n -> p h k n", k=2)
        io4 = iota16.unsqueeze(1).unsqueeze(1).to_broadcast([128, 8, 16, 16])
        yield
        for q_, (tf_b, kk_) in enumerate(((thif, 0), (tlof, 1))):
            tt(dve, eq, eqv, tf_b, tf_b[:].unsqueeze(3).to_broadcast([128, 8, 16, 16]), cst, io4, ALU.is_equal)
            tt(dve, eq, eqv, eq, eqv, sif, sif4[:, :, kk_, :].unsqueeze(2).to_broadcast([128, 8, 16, 16]), ALU.mult)
            P.I(dve, lambda e, q_=q_: e.tensor_reduce(out=sel[:, q_, :].rearrange("p (h n) -> p h n", h=8), in_=eqv, axis=AX.X, op=ALU.add),
                R=[eq], W=[sel])
        stt(sel, sel[:, 0, :], sel, sel[:, 0, :], 128.0, sel, sel[:, 1, :], ALU.mult, ALU.add)
        cp(dve, eidx, eidx[:], sel, sel[:, 0, :])
        yield
        if "eidx" in dbg_t and i < 1:
            dma(sp, Buf(None), dbg_t["eidx"][:, :], sel, sel[:, 0, :])
            dma(sp, Buf(None), dbg_t["gate"][:, :], gate, gate[:])
    def tailp(i, par, pump):
        x1, h2b, gate, eidx = x1s[par], h2bs[par], gates[par], eidxs[par]
        rows = slice(i * 128, (i + 1) * 128)
        ngrp = 128 // GS

        def gath(g):
            for k_ in range(GS):
                m = g * GS + k_
                t_ = uvs[g % NG][k_]
                P.D(pool, lambda e, t_=t_, m=m: e.indirect_dma_start(
                    out=t_[:], out_offset=None, in_=L["uv_bf"][:, :],
                    in_offset=bass.IndirectOffsetOnAxis(ap=eidx[:, m:m + 1].bitcast(U32), axis=0)), R=[eidx], W=[t_])

        def udots(g):
            av_, gl_ = avs[g % 2], gls[g % 2]
            P.I(pool, lambda e, av_=av_: e.memset(av_[:], 0.0), W=[av_], cost=120.0)
            for k_ in range(GS):
                t_ = uvs[g % NG][k_]
                pj = prods[(g * GS + k_) % NPB]
                tt(dve, pj, pj[:], t_, t_[:, 0:D], h2b, h2b[:], ALU.mult)
                actf(pj, pj[:], pj, pj[:], AF.Copy, accum=av_[:, k_:k_ + 1], accb=[av_])
            actf(gl_, gl_[:], av_, av_[:], AF.Gelu)

        def vaxpy(g):
            ms = slice(g * GS, (g + 1) * GS)
            gl_, ww_ = gls[g % 2], wws[g % 2]
            tt(dve, ww_, ww_[:], gl_, gl_[:], gate, gate[:, ms], ALU.mult)
            for k_ in range(GS):
                m = g * GS + k_
                t_ = uvs[g % NG][k_]
                dg = dgs[m % 4]
                ts(dve, dg, dg[:], L["identb_t"], identb, ww_[:, k_:k_ + 1], None, ALU.mult, sb=[ww_])
                for hf in range(2):
                    mm(pacc[hf], pacc[hf][:, :], dg, dg[:], t_, t_[:, D + hf * 512:D + (hf + 1) * 512], start=(m == 0), stop=(m == 127))
        gath(0); gath(1)
        udots(0)
        for g in range(ngrp):
            if g + 2 < ngrp:
                gath(g + 2)
            if g + 1 < ngrp:
                udots(g + 1)
            vaxpy(g)
            pump()
        for hf in range(2):
            hs_ = slice(hf * 512, (hf + 1) * 512)
            if "peer" in dbg_t and i < 2:
                cp(act, acc, acc[:, hs_], pacc[hf], pacc[hf][:, :])
                dma(sp, Buf(None), dbg_t["peer"][rows, hs_], acc, acc[:, hs_])
            tt(dve, acc, acc[:, hs_], pacc[hf], pacc[hf][:, :], reps, reps[:, 1, hs_], ALU.mult)
        tt(pool, acc, acc[:], acc, acc[:], x1, x1[:], ALU.add)
        dma(sp, yo, y_out[rows, :], acc, acc[:])

    def run_all(gen):
        for _ in gen:
            pass
    run_all(prefix(0, 0))
    for i in range(nt_lim):
        nxt = prefix(i + 1, (i + 1) % 2) if i + 1 < nt_lim else None

        def pump(nxt=nxt, k=L.get("pump_k", 2)):
            if nxt is None:
                return
            for _ in range(k):
                try:
                    next(nxt)
                except StopIteration:
                    return
        tailp(i, i % 2, pump)
        if nxt is not None:
            run_all(nxt)
    L["npool"][0] = 7
    P.pop()
    return L
```

```python
import numpy as np
from contextlib import ExitStack
import concourse.bass as bass
import concourse.mybir as mybir
from concourse.bass_utils import run_bass_kernel_spmd

F32 = mybir.dt.float32
BF16 = mybir.dt.bfloat16
I32 = mybir.dt.int32
U32 = mybir.dt.uint32
ALU = mybir.AluOpType
AF = mybir.ActivationFunctionType
AX = mybir.AxisListType

class Buf:
    __slots__ = ("t", "lw", "rd", "name", "psum")

    def __init__(self, t, name=""):
        self.t = t
        self.lw = None
        self.rd = []
        self.name = name
        self.psum = False

    def __getitem__(self, k):
        return self.t[k]


class Eng:
    def __init__(self, P, name, eng, kind):
        self.P = P
        self.name = name
        self.eng = eng
        self.kind = kind
        self.seen = {}
        self.sem = P.newsem(name)
        self.cnt = 0
        self.pool = []
        self.ndma = 0


class Op:
    __slots__ = ("id", "eng", "fn", "kind", "cost", "preds", "tag")


class Prog:
    K = 12
    HOP = 2000.0
    SELF = 150.0

    def __init__(self, nc, ctx):
        self.nc = nc
        self.ctx = ctx
        self.sems = {}
        self.pe = Eng(self, "pe", nc.tensor, "c")
        self.dve = Eng(self, "dve", nc.vector, "c")
        self.act = Eng(self, "act", nc.scalar, "c")
        self.pool = Eng(self, "pool", nc.gpsimd, "c")
        self.sp = Eng(self, "sp", nc.sync, "c")
        self.engs = (self.pe, self.dve, self.act, self.pool, self.sp)
        for e in (self.sp, self.act, self.pool):
            e.pool = [self.newsem(f"{e.name}_d{i}") for i in range(self.K)]
        self.nins = 0
        self.ops = []
        self.nid = 0
        self.base = 0
        self.reorder = REORDER

    def newsem(self, name):
        s = self.ctx.enter_context(self.nc.semaphore(name))
        self.sems[name] = s
        return name

    def sb(self, name, shape, dt=F32):
        self._uid = getattr(self, "_uid", 0) + 1
        t = self.ctx.enter_context(self.nc.sbuf_tensor(f"s{self._uid}_" + name, list(shape), dt))
        return Buf(t, name)

    def ps(self, name, shape, dt=F32):
        t = self.ctx.enter_context(self.nc.psum_tensor("p_" + name, list(shape), dt))
        b = Buf(t, name)
        b.psum = True
        return b

    def dram(self, name, shape, dt=F32, kind="Internal"):
        t = self.nc.dram_tensor(name, list(shape), dt, kind=kind)
        return Buf(t, name)

    def _record(self, E, fn, R, W, kind, cost):
        W = list(W) + [b for b in R if b.psum]
        R = [b for b in R if not b.psum]
        op = Op()
        op.id = self.nid
        self.nid += 1
        op.eng, op.fn, op.kind, op.cost, op.tag = E, fn, kind, cost, None
        preds = set()
        base = self.base
        for b in R:
            if b.lw is not None and b.lw >= base:
                preds.add(b.lw)
        for b in W:
            if b.lw is not None and b.lw >= base:
                preds.add(b.lw)
            for r in b.rd:
                if r >= base:
                    preds.add(r)
        op.preds = preds
        for b in W:
            b.lw = op.id
            b.rd = []
        for b in R:
            b.rd.append(op.id)
        self.ops.append(op)
        self.nins += 1

    def I(self, E, fn, R=(), W=(), cost=250.0):
        self._record(E, fn, R, W, "I", cost)

    def D(self, Q, fn, R=(), W=(), cost=3000.0):
        self._record(Q, fn, R, W, "D", cost)

    def _schedule(self, ops):
        import heapq
        n = len(ops)
        base = self.base
        succs = [[] for _ in range(n)]
        indeg = [0] * n
        for k, op in enumerate(ops):
            for p in op.preds:
                succs[p - base].append(k)
                indeg[k] += 1
        if not self.reorder:
            return list(range(n))
        future = {e.name: [] for e in self.engs}
        avail = {e.name: [] for e in self.engs}
        free = {e.name: 0.0 for e in self.engs}
        finish = [0.0] * n
        rtime = [0.0] * n
        bl = [0.0] * n
        for k in range(n - 1, -1, -1):
            op = ops[k]
            m_ = 0.0
            for s_ in succs[k]:
                lat = self.SELF if ops[s_].eng is op.eng else self.HOP
                v = lat + bl[s_]
                if v > m_:
                    m_ = v
            bl[k] = m_ + (120.0 if op.kind == "D" else op.cost)
        PRI = PRIORITY
        for k in range(n):
            if indeg[k] == 0:
                heapq.heappush(avail[ops[k].eng.name], ((-bl[k], k) if PRI else (k, k)))
        order = []
        while len(order) < n:
            best = None
            for e in self.engs:
                nm = e.name
                fu, av = future[nm], avail[nm]
                while fu and fu[0][0] <= free[nm]:
                    k_ = heapq.heappop(fu)[1]
                    heapq.heappush(av, ((-bl[k_], k_) if PRI else (k_, k_)))
                if av:
                    cand = (free[nm], av[0][1], nm, True)
                elif fu:
                    cand = (fu[0][0], fu[0][1], nm, False)
                else:
                    continue
                if best is None or cand[:2] < best[:2]:
                    best = cand
            start, k, nm, from_av = best
            if from_av:
                heapq.heappop(avail[nm])
            else:
                heapq.heappop(future[nm])
            op = ops[k]
            if op.kind == "D":
                free[nm] = start + 120.0
                finish[k] = start + op.cost
            else:
                free[nm] = start + op.cost
                finish[k] = free[nm]
            order.append(k)
            for s_ in succs[k]:
                lat = self.SELF if ops[s_].eng is op.eng else self.HOP
                t_ = finish[k] + lat
                if t_ > rtime[s_]:
                    rtime[s_] = t_
                indeg[s_] -= 1
                if indeg[s_] == 0:
                    heapq.heappush(future[ops[s_].eng.name], (rtime[s_], s_))
        return order

    def flush(self):
        ops = self.ops
        if not ops:
            return
        order = self._schedule(ops)
        base = self.base
        for k in order:
            op = ops[k]
            E = op.eng
            need = {}
            for p in op.preds:
                po = ops[p - base]
                if po.eng is E and E is self.pe:
                    continue
                key, val = po.tag
                if need.get(key, 0) < val:
                    need[key] = val
            for key, val in need.items():
                if E.seen.get(key, 0) < val:
                    E.eng.wait_ge(self.sems[key], val)
                    E.seen[key] = val
            if op.kind == "I":
                ins = op.fn(E.eng)
                E.cnt += 1
                ins.then_inc(self.sems[E.sem], 1)
                op.tag = (E.sem, E.cnt)
            else:
                j = E.ndma
                sname = E.pool[j % self.K]
                prev = 16 * (j // self.K)
                if prev > 0 and E.seen.get(sname, 0) < prev:
                    E.eng.wait_ge(self.sems[sname], prev)
                    E.seen[sname] = prev
                ins = op.fn(E.eng)
                ins.then_inc(self.sems[sname], 16)
                E.ndma += 1
                op.tag = (sname, prev + 16)
            op.fn = None
        self.base = self.nid
        self.ops = []

    def push(self):
        self._saved = getattr(self, "_saved", [])
        self._saved.append(self.ctx)
        self.ctx = ExitStack()
        self.ctx.__enter__()

    def barrier(self):
        self.flush()
        engs = self.engs
        for E in engs:
            for X in engs:
                if X is not E and X.cnt > 0 and E.seen.get(X.sem, 0) < X.cnt:
                    E.eng.wait_ge(self.sems[X.sem], X.cnt)
                    E.seen[X.sem] = X.cnt
            for Q in (self.sp, self.act, self.pool):
                for i, sname in enumerate(Q.pool):
                    n = (Q.ndma - i + self.K - 1) // self.K if Q.ndma > i else 0
                    if n > 0 and E.seen.get(sname, 0) < 16 * n:
                        E.eng.wait_ge(self.sems[sname], 16 * n)
                        E.seen[sname] = 16 * n

    def pop(self):
        self.barrier()
        self.ctx.__exit__(None, None, None)
        self.ctx = self._saved.pop()

    def finish(self, bufs):
        self.barrier()


T_OWN = 4096
REORDER = True
PRIORITY = True
NT = 32
D = 1024
LOGC = -0.6065306597126334
NEG = -100.0


def build_program(dbg=(), stages=(), small=False):
    nc = bass.Bass("TRN2", target_bir_lowering=False)

    def din(name, shape, dt=F32):
        return nc.dram_tensor(name, list(shape), dt, kind="ExternalInput").ap()

    xs = din("xs", [3, 33 * 128, D])
    flags_d = din("flags", [128, 16])
    ccol_d = din("ccol", [128, 8])
    ada_w = din("ada_w", [D, 6 * D])
    ada_b = din("ada_b", [1, 6 * D])
    g1col_d = din("g1col", [128, 8])
    g2col_d = din("g2col", [128, 8])
    g2row_d = din("g2row", [1, D])
    NCOL = [2688, 1664, 1408]
    wst_d = [din(f"wst{s}", [D, NCOL[s]]) for s in range(3)]
    mu_d = [din(f"mu{s}", [1, NCOL[s]]) for s in range(3)]
    w2a2_d = din("w2a2", [4, 2, 128, 512])
    w0a0_d = din("w0a0", [4, 2, 512])
    g2lora_d = din("g2lora", [128, 512])
    vecs_d = din("vecs", [8, 512])
    rope_d = din("rope", [3, 4096, 64])
    w_out_d = din("w_out", [D, D])
    wq_d = din("wq", [D, 2048])
    skT_d = din("skT", [2, 128, 128])
    NEXP = 512 if small else 16384
    u_tab = din("u_tab", [NEXP, D])
    v_tab = din("v_tab", [NEXP, D])
    consts_d = din("consts", [128, 2048])

    y_out = nc.dram_tensor("y_out", [T_OWN, D], F32, kind="ExternalOutput").ap()
    dbg_t = {}
    for name, shape in dbg:
        dbg_t[name] = nc.dram_tensor("dbg_" + name, list(shape), F32, kind="ExternalOutput").ap()

    ctx = ExitStack()
    with ctx:
        P = Prog(nc, ctx)
        L2 = _emit(P, nc, locals())
        for kv in stages:
            if isinstance(kv, tuple):
                L2[kv[0]] = kv[1]
        if "stop0" not in stages:
            _emit_streams(P, nc, L2)
        if "attn" in stages or not stages:
            _emit_attention(P, nc, L2)
        if "final" in stages or not stages:
            _emit_final(P, nc, L2)
            P.finish([L2["yo"]])
        else:
            P.finish([])
    return nc


def _emit(P, nc, L):
    xs, flags_d, ccol_d, ada_w, ada_b = L["xs"], L["flags_d"], L["ccol_d"], L["ada_w"], L["ada_b"]
    dbg_t = L["dbg_t"]
    pe, dve, act, pool, sp = P.pe, P.dve, P.act, P.pool, P.sp

    def fsz(ap):
        try:
            return float(ap.free_size())
        except Exception:
            return 256.0

    def mm(ob, o, lb, l, rb, r, start=True, stop=True):
        c = 64.0 + fsz(r) / 1.4
        if l.dtype == F32:
            c *= 4
        return P.I(pe, lambda e: e.matmul(o, lhsT=l, rhs=r, start=start, stop=stop), R=[lb, rb], W=[ob], cost=c)

    def tr(ob, o, ib, i, idb, idap):
        c = 64.0 + fsz(i) / 1.4
        if i.dtype == F32:
            c *= 4
        return P.I(pe, lambda e: e.transpose(o, i, idap), R=[ib, idb], W=[ob], cost=c)

    def vcost(E, o):
        if E is pool:
            return 150.0 + fsz(o) / 0.7
        return 80.0 + fsz(o) / 0.96

    def tt(E, ob, o, ab, a, bb, b, op):
        return P.I(E, lambda e: e.tensor_tensor(out=o, in0=a, in1=b, op=op), R=[ab, bb], W=[ob], cost=vcost(E, o))

    def ts(E, ob, o, ab, a, s1, s2, op0, op1=None, sb=()):
        if op1 is None:
            return P.I(E, lambda e: e.tensor_scalar(out=o, in0=a, scalar1=s1, scalar2=None, op0=op0), R=[ab, *sb], W=[ob], cost=vcost(E, o))
        return P.I(E, lambda e: e.tensor_scalar(out=o, in0=a, scalar1=s1, scalar2=s2, op0=op0, op1=op1), R=[ab, *sb], W=[ob], cost=vcost(E, o))

    def stt(ob, o, ab, a, sc, bb, b, op0, op1, sb=()):
        return P.I(dve, lambda e: e.scalar_tensor_tensor(out=o, in0=a, scalar=sc, in1=b, op0=op0, op1=op1), R=[ab, bb, *sb], W=[ob],
                   cost=vcost(dve, o))

    def actf(ob, o, ib, i, func, bias=0.0, scale=1.0, sb=(), accum=None, accb=()):
        def f(e):
            kw = dict(out=o, in_=i, func=func, bias=bias, scale=scale)
            if accum is not None:
                kw["accum_out"] = accum
            return e.activation(**kw)
        return P.I(act, f, R=[ib, *sb], W=[ob, *accb], cost=220.0 + fsz(o) / 1.4)

    def rsqrt(ob, o, ib, i):
        P.I(act, lambda e: e.activation(out=o, in_=i, func=AF.Sqrt), R=[ib], W=[ob], cost=220.0 + fsz(o) / 1.4)
        P.I(dve, lambda e: e.reciprocal(out=o, in_=o), R=[ob], W=[ob], cost=vcost(dve, o))

    def cp(E, ob, o, ib, i):
        if E is act:
            return P.I(E, lambda e: e.copy(out=o, in_=i), R=[ib], W=[ob], cost=220.0 + fsz(o) / 1.4)
        return P.I(E, lambda e: e.tensor_copy(out=o, in_=i), R=[ib], W=[ob], cost=vcost(E, o))

    def dma(Q, ob, o, ib, i):
        return P.D(Q, lambda e: e.dma_start(out=o, in_=i), R=[ib] if ib is not None else [], W=[ob] if ob is not None else [],
                   cost=2500.0 + fsz(o) * 128 * 4 / 150.0)

    DR = Buf(None, "dram_in")

    def dump(name, buf, ap):
        if name in dbg_t:
            dma(sp, Buf(None), dbg_t[name], buf, ap)

    cst = P.sb("cst", [128, 2048])
    dma(sp, cst, cst[:], None, L["consts_d"])
    identf = cst[:, 0:128]
    identb_t = P.sb("identb", [128, 256], BF16)
    cp(dve, identb_t, identb_t[:], cst, cst[:, 0:256])
    identb = identb_t[:, 0:128]
    flags = P.sb("flags", [128, 16])
    dma(sp, flags, flags[:], None, flags_d)
    ones1 = P.sb("ones1", [1, 128])
    P.I(dve, lambda e: e.memset(ones1[:], 1.0), W=[ones1])

    pps = [P.ps(f"pp{i}", [128, 512]) for i in range(7)]
    ppi = [0]

    npool = [7]

    def nps():
        b = pps[ppi[0] % npool[0]]
        ppi[0] += 1
        return b
    tpb = P.ps("tpb", [128, 1024], BF16)

    gs = P.sb("gs", [128, 16])
    modcol = P.sb("modcol", [128, 32])
    rep_d = P.dram("rep_d", [4, 128, D])
    P.push()
    cT = P.sb("cT", [128, 8])
    dma(sp, cT, cT[:], None, ccol_d)
    sT = P.sb("sT", [128, 8])
    actf(sT, sT[:], cT, cT[:], AF.Silu)
    adab = P.sb("adab", [1, 6 * D])
    dma(sp, adab, adab[:], None, ada_b)
    modrow = P.sb("modrow", [1, 6 * D])
    awv = ada_w.rearrange("(j p) n -> p j n", p=128)
    stg = [P.sb(f"stg{i}", [128, 4096]) for i in range(2)]
    for g in range(12):
        b = stg[g % 2]
        bv = b[:].rearrange("p (j n) -> p j n", j=8)
        dma(sp, b, bv, None, awv[:, :, g * 512:(g + 1) * 512])
        ps = nps()
        for j in range(8):
            mm(ps, ps[0:1, :], sT, sT[:, j:j + 1], b, bv[:, j, :], start=(j == 0), stop=(j == 7))
        tt(dve, modrow, modrow[0:1, g * 512:(g + 1) * 512], ps, ps[0:1, :], adab, adab[0:1, g * 512:(g + 1) * 512], ALU.add)
    dump("modrow", modrow, modrow[:])
    ps = nps()
    for pi, off in enumerate((0, 1024, 3072, 4096)):
        for j in range(8):
            mm(ps, ps[:, pi * 8 + j:pi * 8 + j + 1], modrow, modrow[0:1, off + j * 128:off + (j + 1) * 128],
               ones1, ones1[0:1, 0:1])
    cp(dve, modcol, modcol[:], ps, ps[:, 0:32])
    gcol = P.sb("gcol", [128, 16])
    dma(sp, gcol, gcol[:, 0:8], None, L["g1col_d"])
    dma(sp, gcol, gcol[:, 8:16], None, L["g2col_d"])
    stt(gs, gs[:, 0:8], modcol, modcol[:, 8:16], 1.0, gcol, gcol[:, 0:8], ALU.add, ALU.mult)
    stt(gs, gs[:, 8:16], modcol, modcol[:, 24:32], 1.0, gcol, gcol[:, 8:16], ALU.add, ALU.mult)
    g2rep = P.sb("g2rep", [128, D])
    dma(sp, g2rep, g2rep[:], None, L["g2row_d"].partition_broadcast(128))
    for ri, (name, off) in enumerate((("gt1", 2048), ("gt2", 5120), ("sc2", 4096), ("sh2", 3072))):
        t = stg[ri % 2]
        for hf in range(2):
            ps = nps()
            mm(ps, ps[:, :], ones1, ones1[0:1, :], modrow, modrow[0:1, off + hf * 512:off + (hf + 1) * 512])
            cp(act, t, t[:, hf * 512:(hf + 1) * 512], ps, ps[:, :])
        if name == "sc2":
            stt(t, t[:, 0:D], t, t[:, 0:D], 1.0, g2rep, g2rep[:], ALU.add, ALU.mult)
        dma(sp, rep_d, rep_d[ri], t, t[:, 0:D])
    dump("gs", gs, gs[:])
    P.pop()
    L2 = dict(L)
    L2.update(locals())
    return L2


def _consts():
    c = np.zeros((128, 2048), np.float32)
    i = np.arange(128)
    c[:, 0:128] = np.eye(128)
    c[:, 128:256] = np.eye(128)[::-1]
    same = (i[:, None] // 64) == (i[None, :] // 64)
    s_le_t = same & (i[:, None] <= i[None, :])
    s_lt_t = same & (i[:, None] < i[None, :])
    s_gt_t = same & (i[:, None] > i[None, :])
    c[:, 256:384] = LOGC * s_le_t
    c[:, 384:512] = LOGC * s_gt_t
    c[:, 512:640] = s_lt_t
    c[:, 640:768] = s_le_t
    c[:, 768:896] = s_gt_t
    c[:, 896:1024] = s_lt_t
    c[:, 1024:1152] = s_le_t
    c[:, 1152] = LOGC
    c[:, 1153] = 1.0
    c[:, 1280:1296] = np.arange(16)[None, :]
    return c


def host_layout(inp):
    f = np.float32
    w_in = inp["w_in"][0]
    mu = inp["mu_shift"][0]
    sl = lambda a, b: list(range(a, b))
    r_, k_, v_ = sl(0, 512), sl(512, 1024), sl(1024, 1536)
    wl = [sl(1536, 1600), sl(1600, 1664)]
    al = [sl(1664, 1728), sl(1728, 1792)]
    gl, q_, ak, av = sl(1792, 1920), sl(1920, 2432), sl(2432, 2560), sl(2560, 2688)
    mu_ext = np.concatenate([mu, np.zeros(768, f)])
    cols_a = r_ + k_ + v_ + wl[0] + al[0] + wl[1] + al[1] + gl + q_ + ak + av
    cols_b = r_ + k_ + v_ + wl[1] + al[1]
    w2, a2, w0, a0 = inp["w2"][0], inp["a2"][0], inp["w0"][0], inp["a0"][0]
    vecs = np.zeros((8, 512), f)
    vecs[0] = inp["k_k"][0]; vecs[1] = inp["k_a"][0]; vecs[2] = inp["ln_x_w"][0]; vecs[3] = inp["ln_x_b"][0]
    vecs[4] = inp["r_k"][0].reshape(-1); vecs[5] = inp["attn_out_g"][0]
    vecs[6] = np.tile(inp["q_norm_g"][0], 8); vecs[7] = np.tile(inp["k_norm_g"][0], 8)
    inv_freq = (10000.0 ** (-np.arange(0, 32, 2, dtype=f) / 32.0)).astype(f)

    def rope_tab(pos):
        rows = (pos // 64).astype(f); colsp = (pos % 64).astype(f)
        ar = rows[:, None] * inv_freq[None, :]; ac = colsp[:, None] * inv_freq[None, :]
        return np.concatenate([np.cos(ar), np.cos(ac), np.sin(ar), np.sin(ac)], axis=1).astype(f)

    consts = _consts()
    shared = dict(
        ada_w=inp["ada_w"][0], ada_b=inp["ada_b"], g1col=inp["norm1_g"][0].reshape(8, 128).T.copy(),
        g2col=inp["norm2_g"][0].reshape(8, 128).T.copy(), g2row=inp["norm2_g"],
        g2lora=inp["g2"][0], vecs=vecs, w_out=inp["w_out"][0], wq=inp["peer_wq"][0],
        skT=np.ascontiguousarray(inp["peer_sub_keys"][0].transpose(0, 2, 1)),
        u_tab=inp["peer_u"][0], v_tab=inp["peer_v"][0], consts=consts,
        wst0=np.ascontiguousarray(w_in[:, cols_a]), mu0=mu_ext[cols_a][None, :].copy(),
        wst1=np.ascontiguousarray(w_in[:, cols_b]), mu1=mu_ext[cols_b][None, :].copy(),
    )
    wc_cache = {}
    maps = []
    for core in range(8):
        xs = np.zeros((3, 33 * 128, D), f)
        fl = np.zeros((128, 16), f)
        if core < 4:
            xfull = inp["x_prompt"][core]; base = 0; c = inp["c_prompt"][core]; dc = 0
            own = xfull
            fl[:, 8] = NEG
            pos_c = np.zeros(4096, np.int64)
            xs[2, :4096] = xfull
        else:
            s, half = (core - 4) // 2, (core - 4) % 2
            xfull = inp["x_sample"][s]; base = half * 4096; c = inp["c_sample"][s]
            own = xfull[base:base + 4096]
            if half == 0:
                dc = 1
                ctxs = xfull[4096:8192][::-1]; pos_c = np.arange(8191, 4095, -1)
                xs[0, 4097] = xfull[4096]; fl[:, 1] = 1
                xs[1, 4096] = xfull[4096]; fl[:, 2] = 1
                xs[2, 4097] = xfull[4095]; fl[:, 5] = 1
                fl[:, 7] = 1
            else:
                dc = 0
                ctxs = xfull[0:4096]; pos_c = np.arange(0, 4096)
                xs[0, 4096] = xfull[4095]; fl[:, 0] = 1
                xs[1, 4097] = xfull[4095]; fl[:, 3] = 1
                xs[2, 4097] = xfull[4096]; fl[:, 5] = 1
                fl[:, 6] = 1
            xs[2, :4096] = ctxs
        xs[0, :4096] = own
        xs[1, :4096] = own[::-1]
        if dc not in wc_cache:
            cols_c = k_ + v_ + wl[dc] + al[dc] + ak + av
            wc_cache[dc] = (np.ascontiguousarray(w_in[:, cols_c]), mu_ext[cols_c][None, :].copy())
        w2a2 = np.zeros((4, 2, 128, 512), f); w0a0 = np.zeros((4, 2, 512), f)
        for slot, d in enumerate((0, 1, 1, dc)):
            w2a2[slot, 0, :64] = w2[d]; w2a2[slot, 1, 64:] = a2[d]; w0a0[slot, 0] = w0[d]; w0a0[slot, 1] = a0[d]
        rope = np.stack([rope_tab(base + np.arange(4096)), rope_tab(base + np.arange(4096)), rope_tab(pos_c)])
        m = dict(shared)
        m.update(xs=xs, flags=fl, ccol=c.reshape(8, 128).T.copy(), wst2=wc_cache[dc][0], mu2=wc_cache[dc][1],
                 w2a2=w2a2, w0a0=w0a0, rope=rope)
        maps.append(m)
    return maps


def kernel(**inputs):
    inp = {k: np.asarray(v) for k, v in inputs.items()}
    maps = host_layout(inp)
    nc = build_program()
    res = run_bass_kernel_spmd(nc, maps, core_ids=list(range(8)))
    outs = [np.asarray(r["y_out"], dtype=np.float32) for r in res.results]
    y_p = np.stack(outs[0:4])
    y_s = np.stack([np.concatenate([outs[4], outs[5]]), np.concatenate([outs[6], outs[7]])])
    return (y_p, y_s)


def _emit_streams(P, nc, L):
    pe, dve, act, pool, sp = P.pe, P.dve, P.act, P.pool, P.sp
    mm, tr, tt, ts, stt, actf, cp, dma, dump, rsqrt = (L[k] for k in ("mm", "tr", "tt", "ts", "stt", "actf", "cp", "dma", "dump", "rsqrt"))
    nps, tpb, cst, identb, identf, flags, gs, modcol = (L[k] for k in (
        "nps", "tpb", "cst", "identb", "identf", "flags", "gs", "modcol"))
    xs, NCOL, dbg_t = L["xs"], L["NCOL"], L["dbg_t"]
    sb = P.sb
    MASKA, MASKB = cst[:, 512:896], cst[:, 896:1152]
    UTs, SLs, negc = cst[:, 256:384], cst[:, 384:512], cst[:, 1152:1153]
    WG = 256

    Hctx = sb("Hctx", [64, 8, 64])
    yscr = [P.dram(f"yscr{i}", [T_OWN, 512]) for i in range(2)]
    qT_d = P.dram("qT_d", [128, 4, T_OWN], BF16)
    kT_d = P.dram("kT_d", [128, 2, 2 * T_OWN], BF16)
    vx_d = P.dram("vx_d", [128, 64, 130], BF16)
    gate_d = P.dram("gate_d", [T_OWN, 512])
    bon_d = P.dram("bon_d", [T_OWN, 512])
    L.update(yscr=yscr, qT_d=qT_d, kT_d=kT_d, vx_d=vx_d, gate_d=gate_d, bon_d=bon_d)

    cfgs = {
        "c": dict(si=2, slot=3, need_y=False, nshift=1152, offs=dict(k=0, v=512, lora=1024, akv=1152)),
        "a": dict(si=0, slot=0, need_y=True, nshift=1920,
                  offs=dict(r=0, k=512, v=1024, lora=1536, lora2=1664, gl=1792, q=1920, akv=2432)),
        "b": dict(si=1, slot=2, need_y=True, nshift=1664, offs=dict(r=0, k=512, v=1024, lora=1536)),
    }
    order = L.get("stream_order", ("c", "a", "b"))
    for sname in order:
        cf = cfgs[sname]
        si, offs, ncol, nsh, need_y, slot = cf["si"], cf["offs"], NCOL[cf["si"]], cf["nshift"], cf["need_y"], cf["slot"]
        is_a = sname == "a"
        P.push()
        W1 = sb("W1", [128, 8, ncol], BF16)
        W2 = sb("W2", [128, 8, nsh], BF16)
        P.push()
        murep = sb("murep", [128, ncol])
        dma(sp, murep, murep[:], None, L["mu_d"][si].partition_broadcast(128))
        wstg = [sb(f"wstg{i}", [128, 1024]) for i in range(2)]
        wtmp = sb("wtmp", [128, 1024])
        wv = L["wst_d"][si].rearrange("(j p) n -> p j n", p=128)
        k = 0
        for j in range(8):
            for p0 in range(0, ncol, 1024):
                n = min(1024, ncol - p0)
                b = wstg[k % 2]; k += 1
                dma(sp, b, b[:, 0:n], None, wv[:, j, p0:p0 + n])
                tt(dve, wtmp, wtmp[:, 0:n], b, b[:, 0:n], murep, murep[:, p0:p0 + n], ALU.mult)
                tt(pool, W1, W1[:, j, p0:p0 + n], b, b[:, 0:n], wtmp, wtmp[:, 0:n], ALU.subtract)
                if p0 < nsh:
                    m = min(n, nsh - p0)
                    ts(dve, W2, W2[:, j, p0:p0 + m], wtmp, wtmp[:, 0:m], 0.5, None, ALU.mult)
        P.pop()
        if L.get("stop_at") == 1:
            P.pop(); return L
        hring = sb("hring", [128, 4, 8, 128], BF16)
        haloT = sb("haloT", [128, 8, 2], BF16)
        xbuf = [sb("xbuf0", [128, D])]
        xn = sb("xn", [128, D], BF16)
        ssq = sb("ssq", [128, 4])
        hsT = sb("hsT", [128, 8, 128], BF16)
        lora_t = [sb(f"lora{i}", [128, 128], BF16) for i in range(3)]
        w2a2 = sb("w2a2s", [128, 2, 2, 512], BF16)
        w0a0 = sb("w0a0s", [128, 2, 2, 512])
        vecs = sb("vecs", [128, 2, 512])
        qkg = sb("qkg", [128, 2, 512])
        rkrep = sb("rkrep", [128, 512])
        g2l = sb("g2l", [128, 512], BF16)
        P.push()
        wst = sb("w2a2f", [128, 2, 2, 512])
        slots = (slot, 1) if is_a else (slot, slot)
        for q_, sl_ in enumerate(slots):
            dma(sp, wst, wst[:, q_, :, :], None, L["w2a2_d"][sl_].rearrange("k p n -> p k n"))
            dma(sp, w0a0, w0a0[:, q_, :, :].rearrange("p k n -> p (k n)"), None,
                L["w0a0_d"][sl_].rearrange("k n -> (k n)").partition_broadcast(128))
        cp(dve, w2a2, w2a2[:], wst, wst[:])
        dma(sp, vecs, vecs[:].rearrange("p k n -> p (k n)"), None,
            L["vecs_d"][0:2].rearrange("k n -> (k n)").partition_broadcast(128))
        dma(sp, qkg, qkg[:].rearrange("p k n -> p (k n)"), None,
            L["vecs_d"][6:8].rearrange("k n -> (k n)").partition_broadcast(128))
        dma(sp, rkrep, rkrep[:], None, L["vecs_d"][4:5].rearrange("k n -> (k n)").partition_broadcast(128))
        wst2 = sb("g2lf", [128, 512])
        dma(sp, wst2, wst2[:], None, L["g2lora_d"])
        cp(dve, g2l, g2l[:], wst2, wst2[:])
        P.pop()
        if L.get("stop_at") == 2:
            P.pop(); return L

        def f32t(n):
            return sb(n, [128, WG])

        def bft(n):
            return sb(n, [128, WG], BF16)
        kk, a_t, kd, sg, eL, enL, eLx, eh, tmpA, tmpB = (f32t(n) for n in (
            "kk", "a_t", "kd", "sg", "eL", "enL", "eLx", "eh", "tmpA", "tmpB"))
        TMB, TMK, TMA, TMR, Bh, Kh, Vt = (bft(n) for n in ("TMB", "TMK", "TMA", "TMR", "Bh", "Kh", "Vt"))
        TMq = [TMB, TMK, TMA, TMR]
        FM = sb("FM", [128, 8, 128], BF16)
        s8 = sb("s8", [128, 8])
        WCt = sb("WCt", [64, 8])
        SC1 = sb("SC1", [128, 8, 384], BF16)
        SC2 = sb("SC2", [128, 8, 256], BF16)
        LmT = [sb(f"LmT{i}", [128, 8, 256], BF16) for i in range(2)]
        Xb = [sb(f"Xb{i}", [128, 8, 128], BF16) for i in range(2)]
        GD = sb("GD", [64, 16, 128])
        QmT = sb("QmT", [64, 16, 64])
        Y0 = sb("Y0", [128, 8, 64])
        Hs = [sb(f"H{i}", [64, 8, 64]) for i in range(2)]
        ytile = sb("ytile", [128, 512])
        qn = sb("qn", [128, 512]); qr = sb("qr", [128, 512], BF16)
        qsq = qn
        rt = [tmpA, tmpB]
        ksb = sb("ksb", [128, WG]); rsb = sb("rsb", [128, WG])
        ropet = sb("ropet", [128, 64])
        vx = sb("vx", [128, 2, 65], BF16)
        P.I(pool, lambda e: e.memset(vx[:], 1.0), W=[vx])
        qTs = sb("qTs", [128, 5, 128], BF16)
        qd = sb("qd", [128, 256], BF16)
        a2_t = f32t("a2_t"); gt_t = f32t("gt_t"); bon = f32t("bon")

        if sname == "c":
            P.I(pool, lambda e: e.memset(Hs[0][:], 0.0), W=[Hs[0]])
        else:
            fc = 6 if sname == "a" else 7
            ts(dve, Hs[0], Hs[0][:], Hctx, Hctx[:], flags[0:64, fc:fc + 1], None, ALU.mult, sb=[flags])

        xk = [0]

        def build_h(ti, dest_b, dest):
            xt = xbuf[0]
            dma(sp, xt, xt[:], None, xs[si, ti * 128:(ti + 1) * 128, :])
            P.I(pool, lambda e: e.memset(ssq[:, 0:1], 0.0), W=[ssq])
            actf(xn, xn[:], xt, xt[:], AF.Square, accum=ssq[:, 0:1], accb=[ssq])
            ts(dve, ssq, ssq[:, 1:2], ssq, ssq[:, 0:1], 1.0 / D, 1e-6, ALU.mult, ALU.add)
            rsqrt(ssq, ssq[:, 2:3], ssq, ssq[:, 1:2])
            actf(xn, xn[:], xt, xt[:], AF.Copy, scale=ssq[:, 2:3], sb=[ssq])
            for j in range(8):
                tr(tpb, tpb[:, j * 128:(j + 1) * 128], xn, xn[:, j * 128:(j + 1) * 128], L["identb_t"], identb)
            for j in range(8):
                ts(dve, dest_b, dest[:, j, :], tpb, tpb[:, j * 128:(j + 1) * 128], gs[:, j:j + 1], modcol[:, j:j + 1],
                   ALU.mult, ALU.add, sb=[gs, modcol])

        hh = hring
        build_h(32, hring, hring[:, 3])
        fo = {"a": 0, "b": 2, "c": 4}[sname]
        ts(dve, haloT, haloT[:, :, 0:1], hring, hring[:, 3, :, 0:1], flags[:, fo:fo + 1], None, ALU.mult, sb=[flags])
        ts(dve, haloT, haloT[:, :, 1:2], hring, hring[:, 3, :, 1:2], flags[:, fo + 1:fo + 2], None, ALU.mult, sb=[flags])
        if L.get("stop_at") == 3:
            P.pop(); return L
        build_h(0, hring, hring[:, 0])

        hsTs = [hsT, sb("hsTb", [128, 8, 128], BF16)]
        loras = [lora_t, [sb(f"lorab{i}", [128, 128], BF16) for i in range(3)]]

        def mkproj(cur, hsT):
            def proj(ps_ap, ps_b, off, n, shift=True, fm=False):
                nmm = 16 if shift else 8
                c = 0
                for (hb, hap, Wt) in ((hring, cur, W1), (hsT, hsT[:], W2)):
                    if Wt is W2 and not shift:
                        continue
                    for j in range(8):
                        if fm:
                            mm(ps_b, ps_ap, Wt, Wt[:, j, off:off + n], hb, hap[:, j, :], start=(c == 0), stop=(c == nmm - 1))
                        else:
                            mm(ps_b, ps_ap, hb, hap[:, j, :], Wt, Wt[:, j, off:off + n], start=(c == 0), stop=(c == nmm - 1))
                        c += 1

            return proj

        fpi = [0]

        def npsf():
            b_ = L["pps"][5 + fpi[0] % 2]
            fpi[0] += 1
            return b_

        def front(i):
            nps = L["nps"]
            hsT, lora_t = hsTs[i % 2], loras[i % 2]
            if i + 1 < NT:
                build_h(i + 1, hring, hring[:, (i + 1) % 4])
            cur = hring[:, i % 4]
            yield
            prevcol = haloT[:, :, 0:1] if i == 0 else hring[:, (i - 1) % 4, :, 127:128]
            nextcol = haloT[:, :, 1:2] if i == NT - 1 else hring[:, (i + 1) % 4, :, 0:1]
            yield
            proj = mkproj(cur, hsT)
            tt(pool, hsT, hsT[:, :, 1:127], hring, cur[:, :, 0:126], hring, cur[:, :, 2:128], ALU.add)
            tt(pool, hsT, hsT[:, :, 0:1], hring if i else haloT, prevcol, hring, cur[:, :, 1:2], ALU.add)
            yield
            tt(pool, hsT, hsT[:, :, 127:128], hring, cur[:, :, 126:127], haloT if i == NT - 1 else hring, nextcol, ALU.add)

            lgroups = [("lora", 0)] + ([("lora2", 1), ("gl", 2)] if is_a else [])
            yield
            for nm, li in lgroups:
                ps = nps()
                proj(ps[:, 0:128], ps, offs[nm], 128, fm=True)
                if nm == "gl":
                    actf(lora_t[li], lora_t[li][:], ps, ps[:, 0:128], AF.Sigmoid)
                else:
                    actf(lora_t[li], lora_t[li][0:64, :], ps, ps[0:64, 0:128], AF.Tanh)
                    cp(dve, lora_t[li], lora_t[li][64:128, :], ps, ps[64:128, 0:128])

            if sname in ("a", "c"):
                dma(sp, ropet, ropet[:], None, L["rope_d"][si, i * 128:(i + 1) * 128, :])
                jobs = []
                if is_a:
                    pq_ = nps(); proj(pq_[:, :], pq_, offs["q"], 512, shift=False)
                    jobs.append((pq_, pq_[:, 0:512], 8, 0))
                pk_ = nps(); proj(pk_[:, 0:256], pk_, offs["akv"], 256, shift=False)
                jobs.append((pk_, pk_[:, 0:128], 2, 1))
                cp(act, vx, vx[:, :, 0:64], pk_, pk_[:, 128:256].rearrange("p (h n) -> p h n", h=2))
                kt_ = (i if is_a else 32 + i)
                dma(sp, vx_d, vx_d[:, kt_, :], vx, vx[:].rearrange("p h n -> p (h n)"))
                for (pb, pap, nh, gi) in jobs:
                    yield
                    w = nh * 64
                    actf(qsq, qsq[:, 0:w], pb, pap, AF.Square)
                    P.I(dve, lambda e, nh=nh, w=w: e.tensor_reduce(out=s8[:, 0:nh], in_=qsq[:, 0:w].rearrange("p (h n) -> p h n", h=nh),
                                                       axis=AX.X, op=ALU.add), R=[qsq], W=[s8])
                    ts(dve, s8, s8[:, 0:nh], s8, s8[:, 0:nh], 1.0 / 64, 1e-6, ALU.mult, ALU.add)
                    rsqrt(s8, s8[:, 0:nh], s8, s8[:, 0:nh])
                    tt(dve, qn, qn[:, 0:w].rearrange("p (h n) -> p h n", h=nh), pb, pap.rearrange("p (h n) -> p h n", h=nh),
                       s8, s8[:, 0:nh].unsqueeze(2).to_broadcast([128, nh, 64]), ALU.mult)
                    tt(pool, qn, qn[:, 0:w], qn, qn[:, 0:w], qkg, qkg[:, gi, 0:w], ALU.mult)
                    v5 = qn[:, 0:w].rearrange("p (h a f n) -> p h a f n", h=nh, a=2, f=2)
                    o5 = qr[:, 0:w].rearrange("p (h a f n) -> p h a f n", h=nh, a=2, f=2)
                    x1, x2 = v5[:, :, :, 0, :], v5[:, :, :, 1, :]
                    cs_ = ropet[:, 0:32].rearrange("p (a n) -> p a n", a=2).unsqueeze(1).to_broadcast([128, nh, 2, 16])
                    sn_ = ropet[:, 32:64].rearrange("p (a n) -> p a n", a=2).unsqueeze(1).to_broadcast([128, nh, 2, 16])
                    r0 = rt[0][:, 0:w // 2].rearrange("p (h a n) -> p h a n", h=nh, a=2)
                    r1 = rt[1][:, 0:w // 2].rearrange("p (h a n) -> p h a n", h=nh, a=2)
                    tt(dve, rt[0], r0, qn, x1, ropet, cs_, ALU.mult)
                    tt(pool, rt[1], r1, qn, x2, ropet, sn_, ALU.mult)
                    tt(dve, qr, o5[:, :, :, 0, :], rt[0], r0, rt[1], r1, ALU.subtract)
                    tt(dve, rt[0], r0, qn, x1, ropet, sn_, ALU.mult)
                    tt(pool, rt[1], r1, qn, x2, ropet, cs_, ALU.mult)
                    tt(dve, qr, o5[:, :, :, 1, :], rt[0], r0, rt[1], r1, ALU.add)
                    if gi == 0:
                        src_b, src, nt_ = qr, qr, 4
                    else:
                        cp(pool, qd, qd[:].rearrange("p (k d n) -> p k d n", k=2, d=2),
                           qr, qr[:, 0:128].rearrange("p (k n) -> p k n", k=2).unsqueeze(2).to_broadcast([128, 2, 2, 64]))
                        src_b, src, nt_ = qd, qd, 2
                    for t_ in range(nt_):
                        tr(tpb, tpb[:, t_ * 128:(t_ + 1) * 128], src_b, src[:, t_ * 128:(t_ + 1) * 128], L["identb_t"], identb)
                    cp(act, qTs, qTs[:, 0:nt_, :], tpb, tpb[:, 0:nt_ * 128].rearrange("p (t n) -> p t n", t=nt_))
                    if gi == 0:
                        dma(sp, qT_d, qT_d[:, :, i * 128:(i + 1) * 128], qTs, qTs[:, 0:4, :])
                    else:
                        dma(sp, kT_d, kT_d[:, :, kt_ * 128:(kt_ + 1) * 128], qTs, qTs[:, 0:2, :])


        def mkset(tag):
            return (sb("TMA" + tag, [128, WG], BF16), sb("TMR" + tag, [128, WG], BF16), sb("Bh" + tag, [128, WG], BF16),
                    sb("Kh" + tag, [128, WG], BF16), sb("Vt" + tag, [128, WG], BF16), sb("FM" + tag, [128, 8, 128], BF16), sb("WCt" + tag, [64, 8]))
        psets4 = [[(TMA, TMR, Bh, Kh, Vt, FM, WCt), mkset("b")], [mkset("c"), mkset("d")]]
        bpi = [0]

        def npsB():
            b_ = L["pps"][3 + bpi[0] % 4]
            bpi[0] += 1
            return b_

        def Pgen(i, hg):
            nps = L["nps"]
            hsT, lora_t = hsTs[i % 2], loras[i % 2]
            cur = hring[:, i % 4]
            proj = mkproj(cur, hsT)
            TMA, TMR, Bh, Kh, Vt, FM, WCt = psets4[i % 2][hg]
            TMq = [TMB, TMK, TMA, TMR]
            cg = slice(hg * WG, (hg + 1) * WG)
            pk = nps(); proj(pk[:, 0:WG], pk, offs["k"] + hg * WG, WG)
            cp(act, ksb, ksb[:], pk, pk[:, 0:WG])
            pv = nps(); proj(pv[:, 0:WG], pv, offs["v"] + hg * WG, WG)
            yield
            cp(act, Vt, Vt[:], pv, pv[:, 0:WG])
            if need_y:
                pr_ = nps(); proj(pr_[:, 0:WG], pr_, offs["r"] + hg * WG, WG)
                cp(act, rsb, rsb[:], pr_, pr_[:, 0:WG])
            pw = nps()
            yield
            mm(pw, pw[:, 0:WG], lora_t[0], lora_t[0][:, :], w2a2, w2a2[:, 0, 0, cg])
            mm(pw, pw[:, WG:2 * WG], lora_t[0], lora_t[0][:, :], w2a2, w2a2[:, 0, 1, cg])
            yield
            tt(dve, kk, kk[:], ksb, ksb[:], vecs, vecs[:, 0, cg], ALU.mult)
            yield
            tt(pool, tmpB, tmpB[:], kk, kk[:], kk, kk[:], ALU.mult)
            P.I(dve, lambda e: e.tensor_reduce(out=s8[:, 0:4], in_=tmpB[:].rearrange("p (h n) -> p h n", h=4),
                                               axis=AX.X, op=ALU.add), R=[tmpB], W=[s8])
            ts(dve, s8, s8[:, 0:4], s8, s8[:, 0:4], 1e-24, None, ALU.max)
            yield
            rsqrt(s8, s8[:, 0:4], s8, s8[:, 0:4])
            tt(pool, kk, kk[:].rearrange("p (h n) -> p h n", h=4), kk, kk[:].rearrange("p (h n) -> p h n", h=4),
               s8, s8[:, 0:4].unsqueeze(2).to_broadcast([128, 4, 64]), ALU.mult)
            tt(dve, tmpA, tmpA[:], pw, pw[:, WG:2 * WG], w0a0, w0a0[:, 0, 1, cg], ALU.add)
            yield
            actf(a_t, a_t[:], tmpA, tmpA[:], AF.Sigmoid)
            stt(tmpB, tmpB[:], a_t, a_t[:], -1.0, vecs, vecs[:, 1, cg], ALU.add, ALU.mult)
            stt(kd, kd[:], tmpB, tmpB[:], 1.0, ksb, ksb[:], ALU.add, ALU.mult)
            yield
            if is_a:
                pg2 = nps()
                mm(pg2, pg2[:, 0:WG], lora_t[2], lora_t[2][:, :], g2l, g2l[:, cg])
                mm(pg2, pg2[:, WG:2 * WG], lora_t[1], lora_t[1][:, :], w2a2, w2a2[:, 1, 1, cg])
                cp(act, gt_t, gt_t[:], pg2, pg2[:, 0:WG])
                dma(sp, gate_d, gate_d[i * 128:(i + 1) * 128, cg], gt_t, gt_t[:])
                tt(dve, tmpA, tmpA[:], pg2, pg2[:, WG:2 * WG], w0a0, w0a0[:, 1, 1, cg], ALU.add)
                actf(a2_t, a2_t[:], tmpA, tmpA[:], AF.Sigmoid)
                tt(pool, a2_t, a2_t[:], a2_t, a2_t[:], a_t, a_t[:], ALU.add)
                ts(dve, a2_t, a2_t[:], a2_t, a2_t[:], 0.5, -1.0, ALU.mult, ALU.add)
                tt(pool, a2_t, a2_t[:], a2_t, a2_t[:], vecs, vecs[:, 1, cg], ALU.mult)
                stt(bon, bon[:], a2_t, a2_t[:], 1.0, ksb, ksb[:], ALU.add, ALU.mult)
                tt(dve, bon, bon[:], bon, bon[:], rsb, rsb[:], ALU.mult)
                tt(pool, bon, bon[:], bon, bon[:], rkrep, rkrep[:, cg], ALU.mult)
                P.I(dve, lambda e: e.tensor_reduce(out=s8[:, 4:8], in_=bon[:].rearrange("p (h n) -> p h n", h=4),
                                                   axis=AX.X, op=ALU.add), R=[bon], W=[s8])
                tt(dve, bon, bon[:].rearrange("p (h n) -> p h n", h=4), Vt, Vt[:].rearrange("p (h n) -> p h n", h=4),
                   s8, s8[:, 4:8].unsqueeze(2).to_broadcast([128, 4, 64]), ALU.mult)
                dma(sp, bon_d, bon_d[i * 128:(i + 1) * 128, cg], bon, bon[:])
            yield
            tt(dve, tmpA, tmpA[:], pw, pw[:, 0:WG], w0a0, w0a0[:, 0, 0, cg], ALU.add)
            actf(sg, sg[:], tmpA, tmpA[:], AF.Sigmoid)
            yield
            pL = nps()
            mm(pL, pL[:, 0:WG], cst, UTs, sg, sg[:])
            mm(pL, pL[:, WG:2 * WG], cst, SLs, sg, sg[:])
            yield
            actf(enL, enL[:], pL, pL[:, 0:WG], AF.Exp, scale=-1.0)
            actf(eh, eh[:], pL, pL[:, WG:2 * WG], AF.Exp)
            stt(tmpA, tmpA[:], sg, sg[:], -LOGC, pL, pL[:, 0:WG], ALU.mult, ALU.add)
            yield
            actf(eLx, eLx[:], tmpA, tmpA[:], AF.Exp)
            stt(TMA, TMA[:], kk, kk[:], -1.0, eLx, eLx[:], ALU.mult, ALU.mult)
            tt(pool, tmpB, tmpB[:], kk, kk[:], a_t, a_t[:], ALU.mult)
            yield
            tt(pool, TMB, TMB[:], tmpB, tmpB[:], enL, enL[:], ALU.mult)
            tt(pool, Bh, Bh[:], tmpB, tmpB[:], eh, eh[:], ALU.mult)
            tt(dve, TMK, TMK[:], kd, kd[:], enL, enL[:], ALU.mult)
            yield
            tt(dve, Kh, Kh[:], kd, kd[:], eh, eh[:], ALU.mult)
            if need_y:
                actf(eL, eL[:], pL, pL[:, 0:WG], AF.Exp)
                tt(dve, TMR, TMR[:], rsb, rsb[:], eL, eL[:], ALU.mult)
            yield
            for c in range(2):
                pwc = nps()
                for hl in range(4):
                    mm(pwc, pwc[0:64, hl:hl + 1], sg, sg[c * 64:(c + 1) * 64, hl * 64:(hl + 1) * 64],
                       cst, negc[c * 64:(c + 1) * 64, :])
                actf(WCt, WCt[:, c * 4:(c + 1) * 4], pwc, pwc[0:64, 0:4], AF.Exp)
            yield
            yield
            nq = 4 if need_y else 3
            for hpl in range(2):
                for q_ in range(nq):
                    s_ = hpl * 4 + q_
                    tr(tpb, tpb[:, s_ * 128:(s_ + 1) * 128], TMq[q_], TMq[q_][:, hpl * 128:(hpl + 1) * 128], L["identb_t"], identb)
            if need_y:
                cp(act, FM, FM[:], tpb, tpb[:, :].rearrange("p (s n) -> p s n", s=8))
            else:
                for hpl in range(2):
                    cp(act, FM, FM[:, hpl * 4:hpl * 4 + 3, :], tpb,
                       tpb[:, hpl * 512:hpl * 512 + 384].rearrange("p (s n) -> p s n", s=3))

        def Mpart(i, pump):
            nps = npsB
            sets = psets4[i % 2]
            for hl in range(8):
                TMA, TMR, Bh, Kh, Vt, FM, WCt = sets[hl // 4]
                hq = hl % 4
                e_, hpl = hq % 2, hq // 2
                pr = slice(e_ * 64, (e_ + 1) * 64)
                fB, fK, fA = FM[pr, hpl * 4 + 0, :], FM[pr, hpl * 4 + 1, :], FM[pr, hpl * 4 + 2, :]
                p1 = nps(); p2 = nps()
                if need_y:
                    fAR = FM[pr, hpl * 4 + 2:hpl * 4 + 4, :]
                    mm(p1, p1[:, 0:256], FM, fB, FM, fAR)
                    mm(p2, p2[:, 0:256], FM, fK, FM, fAR)
                else:
                    mm(p1, p1[:, 0:128], FM, fB, FM, fA)
                    mm(p2, p2[:, 0:128], FM, fK, FM, fA)
                mm(p1, p1[:, 256:384], FM, fA, FM, fB)
                if need_y:
                    tt(dve, SC1, SC1[:, hl, :], p1, p1[:, 0:384], cst, MASKA, ALU.mult)
                    tt(dve, SC2, SC2[:, hl, :], p2, p2[:, 0:256], cst, MASKB, ALU.mult)
                else:
                    tt(dve, SC1, SC1[:, hl, 0:128], p1, p1[:, 0:128], cst, MASKA[:, 0:128], ALU.mult)
                    tt(dve, SC1, SC1[:, hl, 256:384], p1, p1[:, 256:384], cst, MASKA[:, 256:384], ALU.mult)
                    tt(dve, SC2, SC2[:, hl, 0:128], p2, p2[:, 0:128], cst, MASKB[:, 0:128], ALU.mult)
                pX = nps()
                mm(pX, pX[:, 0:64], SC2, SC2[:, hl, 0:128], Vt, Vt[:, hq * 64:(hq + 1) * 64])
                cp(act, Xb[0], Xb[0][:, hl, 0:64], pX, pX[:, 0:64])
                cp(pool, Xb[0], Xb[0][:, hl, 64:128], TMA, TMA[:, hq * 64:(hq + 1) * 64])
                if hl % 2 == 1:
                    pump()
            for lev in range(6):
                Xc, Xn_ = Xb[lev % 2], Xb[(lev + 1) % 2]
                for hl in range(8):
                    if lev == 0:
                        LTb, LTc, Lmb, Lmc = SC1, SC1[:, hl, 0:128], SC1, SC1[:, hl, 256:384]
                    else:
                        src = LmT[(lev - 1) % 2]
                        LTb, LTc, Lmb, Lmc = src, src[:, hl, 128:256], src, src[:, hl, 0:128]
                    pa = nps()
                    mm(pa, pa[:, 0:128], LTb, LTc, Xc, Xc[:, hl, :])
                    if lev < 5:
                        mm(pa, pa[:, 128:256], LTb, LTc, Lmb, Lmc)
                        mm(pa, pa[:, 256:384], Lmb, Lmc, LTb, LTc)
                    tt(dve, Xn_, Xn_[:, hl, :], Xc, Xc[:, hl, :], pa, pa[:, 0:128], ALU.add)
                    if lev < 5:
                        cp(act, LmT[lev % 2], LmT[lev % 2][:, hl, :], pa, pa[:, 128:384])
                    if hl % 4 == 3:
                        pump()
            Xf = Xb[0]
            for hl in range(8):
                TMA, TMR, Bh, Kh, Vt, FM, WCt = sets[hl // 4]
                hq = hl % 4
                hs_ = slice(hq * 64, (hq + 1) * 64)
                for c in range(2):
                    pg = nps()
                    cs = slice(c * 64, (c + 1) * 64)
                    mm(pg, pg[0:64, 0:64], Xf, Xf[cs, hl, 64:128], Bh, Bh[cs, hs_])
                    mm(pg, pg[0:64, 64:128], Bh, Bh[cs, hs_], Xf, Xf[cs, hl, 0:64], start=True, stop=False)
                    mm(pg, pg[0:64, 64:128], Kh, Kh[cs, hs_], Vt, Vt[cs, hs_], start=False, stop=True)
                    cp(act, GD, GD[:, hl * 2 + c, :], pg, pg[0:64, 0:128])
                if need_y:
                    for c in range(2):
                        pq2 = nps()
                        cs = slice(c * 64, (c + 1) * 64)
                        mm(pq2, pq2[0:64, 0:64], Xf, Xf[cs, hl, 64:128], SC1, SC1[cs, hl, 128 + c * 64:128 + (c + 1) * 64],
                           start=True, stop=False)
                        mm(pq2, pq2[0:64, 0:64], TMR, TMR[cs, hs_], L["identb_t"], identb[cs, c * 64:(c + 1) * 64],
                           start=False, stop=True)
                        cp(act, QmT, QmT[:, hl * 2 + c, :], pq2, pq2[0:64, 0:64])
                    py = nps()
                    mm(py, py[:, 0:64], SC1, SC1[:, hl, 128:256], Xf, Xf[:, hl, 0:64], start=True, stop=False)
                    mm(py, py[:, 0:64], SC2, SC2[:, hl, 128:256], Vt, Vt[:, hs_], start=False, stop=True)
                    cp(act, Y0, Y0[:, hl, :], py, py[:, 0:64])
                if hl % 2 == 1:
                    pump()
            for c in range(2):
                cs = slice(c * 64, (c + 1) * 64)
                Hc_b, Hn_b = Hs[c], Hs[1 - c]
                ph = nps()
                pyc = nps() if need_y else None
                for hl in range(8):
                    WCt = sets[hl // 4][6]
                    hq = hl % 4
                    hs_ = slice(hl * 64, (hl + 1) * 64)
                    if need_y:
                        mm(pyc, pyc[cs, hs_], QmT, QmT[:, hl * 2 + c, :], Hc_b, Hc_b[:, hl, :])
                    mm(ph, ph[0:64, hs_], GD, GD[:, hl * 2 + c, 0:64], Hc_b, Hc_b[:, hl, :], start=True, stop=False)
                    mm(ph, ph[0:64, hs_], cst, identf[0:64, 0:64], GD, GD[:, hl * 2 + c, 64:128], start=False, stop=True)
                    stt(Hn_b, Hn_b[:, hl, :], Hc_b, Hc_b[:, hl, :], WCt[:, c * 4 + hq:c * 4 + hq + 1], ph, ph[0:64, hs_],
                        ALU.mult, ALU.add, sb=[WCt])
                if need_y:
                    tt(dve, ytile, ytile[cs, :], Y0, Y0[cs, :, :].rearrange("p h n -> p (h n)"), pyc, pyc[cs, 0:512], ALU.add)
                pump()

        def Gchain(i):
            yield from front(i)
            yield from Pgen(i, 0)
            yield from Pgen(i, 1)

        def tile_post(i):
            if need_y:
                dma(sp, yscr[0 if is_a else 1], yscr[0 if is_a else 1][i * 128:(i + 1) * 128, :], ytile, ytile[:])
                if ("ytile_" + sname) in dbg_t and i < 2:
                    dma(sp, Buf(None), dbg_t["ytile_" + sname][i * 128:(i + 1) * 128, :], ytile, ytile[:])


        def run_all(gen):
            for _ in gen:
                pass

        def mkpump(gen, k):
            def pump():
                for _ in range(k):
                    try:
                        next(gen)
                    except StopIteration:
                        return
            return pump
        ntl = L.get("nt_limit", NT)
        L["npool"][0] = 3
        run_all(Gchain(0))
        for i in range(ntl):
            g2 = Gchain(i + 1) if i + 1 < ntl else iter(())
            Mpart(i, mkpump(g2, L.get("spump_k2", 4)))
            run_all(g2)
            tile_post(i)
        L["npool"][0] = 7
        if sname == "c":
            cp(dve, Hctx, Hctx[:], Hs[0], Hs[0][:])
            if "hctx" in dbg_t:
                dma(sp, Buf(None), dbg_t["hctx"][:, :], Hctx, Hctx[:].rearrange("p h n -> p (h n)"))
        P.pop()
    return L


def _emit_attention(P, nc, L):
    pe, dve, act, pool, sp = P.pe, P.dve, P.act, P.pool, P.sp
    mm, tr, tt, ts, stt, actf, cp, dma, dump, rsqrt = (L[k] for k in ("mm", "tr", "tt", "ts", "stt", "actf", "cp", "dma", "dump", "rsqrt"))
    nps, cst, identf, flags, pps, npool = (L[k] for k in ("nps", "cst", "identf", "flags", "pps", "npool"))
    dbg_t = L["dbg_t"]
    sb = P.sb
    P.push()
    qT = sb("qT", [128, 4, T_OWN], BF16)
    dma(sp, qT, qT[:], L["qT_d"], L["qT_d"][:])
    kT = sb("kT2", [128, 2, 2 * T_OWN], BF16)
    dma(sp, kT, kT[:], L["kT_d"], L["kT_d"][:])
    Vx = sb("Vx", [128, 64, 130], BF16)
    dma(sp, Vx, Vx[:], L["vx_d"], L["vx_d"][:])
    outg = sb("outg", [128, 512])
    dma(sp, outg, outg[:], None, L["vecs_d"][5:6].rearrange("k n -> (k n)").partition_broadcast(128))
    pT = [sb(f"pT{i}", [128, 512], BF16) for i in range(3)]
    oTs = sb("oTs", [65, 512])
    osb = sb("osb", [128, 4, 65])
    o2 = sb("o2", [128, 4, 64])
    ssa = sb("ssa", [128, 8])
    yat = sb("yat", [128, 4, 64])
    yattn_d = P.dram("yattn_d", [T_OWN, 512])
    L["yattn_d"] = yattn_d
    npool[0] = 5
    acc = [pps[5], pps[6]]
    it = 0
    uv_bf = P.dram("uv_bf", [L["NEXP"], 2 * D], BF16)
    L["uv_bf"] = uv_bf
    cbf = [sb(f"cbf{i}", [128, 4096]) for i in range(2)]
    cbb = [sb(f"cbb{i}", [128, 4096], BF16) for i in range(2)]
    nchunk = L["NEXP"] // 512
    cast_jobs = [(tb, c) for tb in range(2) for c in range(nchunk)]

    def cast_job(n):
        if n >= len(cast_jobs):
            return
        tb, c = cast_jobs[n]
        src = (L["u_tab"], L["v_tab"])[tb]
        f_, b_ = cbf[n % 2], cbb[n % 2]
        P.D(pool, lambda e: e.dma_start(out=f_[:], in_=src[c * 512:(c + 1) * 512, :].rearrange("(p j) n -> p (j n)", j=4)), W=[f_])
        cp(dve, b_, b_[:], f_, f_[:])
        P.D(pool, lambda e: e.dma_start(out=uv_bf[c * 512:(c + 1) * 512, tb * D:(tb + 1) * D].rearrange("(p j) n -> p j n", j=4),
                                        in_=b_[:].rearrange("p (j n) -> p j n", j=4)), R=[b_], W=[uv_bf])
    njob = [0]
    nh_lim = L.get("attn_heads", 8)
    nqb_lim = L.get("attn_qb", 8)
    pT6 = pT + [sb(f"pTx{i}", [128, 512], BF16) for i in range(3)]

    def head_tail(h, qb, po):
        cp(act, oTs, oTs[:, :], po, po[0:65, :])
        ptp = nps()
        for t in range(4):
            tr(ptp, ptp[:, t * 65:(t + 1) * 65], oTs, oTs[0:65, t * 128:(t + 1) * 128], cst, identf[0:65, 0:65])
        cp(dve, osb, osb[:], ptp, ptp[:, 0:260].rearrange("p (t n) -> p t n", t=4))
        P.I(dve, lambda e: e.reciprocal(out=ssa[:, 0:4], in_=osb[:, :, 64]), R=[osb], W=[ssa])
        tt(dve, o2, o2[:], osb, osb[:, :, 0:64], ssa, ssa[:, 0:4].unsqueeze(2).to_broadcast([128, 4, 64]), ALU.mult)
        tt(pool, yat, yat[:], o2, o2[:], o2, o2[:], ALU.mult)
        P.I(dve, lambda e: e.tensor_reduce(out=ssa[:, 4:8], in_=yat[:], axis=AX.X, op=ALU.add), R=[yat], W=[ssa])
        ts(dve, ssa, ssa[:, 4:8], ssa, ssa[:, 4:8], 1.0 / 64, 1e-6, ALU.mult, ALU.add)
        rsqrt(ssa, ssa[:, 4:8], ssa, ssa[:, 4:8])
        tt(dve, o2, o2[:], o2, o2[:], ssa, ssa[:, 4:8].unsqueeze(2).to_broadcast([128, 4, 64]), ALU.mult)
        tt(pool, yat, yat[:], o2, o2[:], outg, outg[:, h * 64:(h + 1) * 64].unsqueeze(1).to_broadcast([128, 4, 64]), ALU.mult)
        dma(sp, yattn_d, yattn_d[qb * 512:(qb + 1) * 512, h * 64:(h + 1) * 64].rearrange("(t p) n -> p t n", p=128), yat, yat[:])
        if "yattn" in dbg_t and qb == 0:
            dma(sp, Buf(None), dbg_t["yattn"][:, h * 64:(h + 1) * 64].rearrange("(t p) n -> p t n", p=128), yat, yat[:])

    for hp in range(nh_lim // 2):
        kvh = hp // 2
        for qb in range(nqb_lim):
            cast_job(njob[0]); njob[0] += 1
            cast_job(njob[0]); njob[0] += 1
            for kt in range(64):
                pss = []
                for e_ in range(2):
                    pr = slice(e_ * 64, (e_ + 1) * 64)
                    ps_ = nps()
                    mm(ps_, ps_[:, :], kT, kT[pr, kvh, kt * 128:(kt + 1) * 128], qT, qT[pr, hp, qb * 512:(qb + 1) * 512])
                    pss.append(ps_)
                for e_ in range(2):
                    pt = pT6[(kt * 2 + e_) % 6]
                    if kt >= 32:
                        actf(pt, pt[:], pss[e_], pss[e_][:, :], AF.Exp, bias=flags[:, 8:9], scale=0.125, sb=[flags])
                    else:
                        actf(pt, pt[:], pss[e_], pss[e_][:, :], AF.Exp, scale=0.125)
                    mm(acc[e_], acc[e_][0:65, :], Vx, Vx[:, kt, kvh * 65:(kvh + 1) * 65], pt, pt[:], start=(kt == 0), stop=(kt == 63))
            for e_ in range(2):
                head_tail(2 * hp + e_, qb, acc[e_])
    while njob[0] < len(cast_jobs):
        cast_job(njob[0]); njob[0] += 1
    npool[0] = 7
    P.pop()
    return L


def _emit_final(P, nc, L):
    pe, dve, act, pool, sp = P.pe, P.dve, P.act, P.pool, P.sp
    mm, tr, tt, ts, stt, actf, cp, dma, dump, rsqrt = (L[k] for k in ("mm", "tr", "tt", "ts", "stt", "actf", "cp", "dma", "dump", "rsqrt"))
    nps, tpb, cst, identb, identf, flags = (L[k] for k in ("nps", "tpb", "cst", "identb", "identf", "flags"))
    dbg_t, xs, y_out = L["dbg_t"], L["xs"], L["y_out"]
    u_tab, v_tab = L["u_tab"], L["v_tab"]
    sb = P.sb
    yo = Buf(None, "y_out")
    L["yo"] = yo
    P.push()
    Jf = cst[:, 128:256]
    wo = sb("wo", [128, 8, D], BF16)
    wqb = sb("wqb", [128, 8, 2048], BF16)
    skb = sb("skb", [128, 2, 128], BF16)
    P.push()
    wstg = [sb(f"fstg{i}", [128, 2048]) for i in range(2)]
    wov = L["w_out_d"].rearrange("(j p) n -> p j n", p=128)
    wqv = L["wq_d"].rearrange("(j p) n -> p j n", p=128)
    k = 0
    for j in range(8):
        b = wstg[k % 2]; k += 1
        dma(sp, b, b[:, 0:D], None, wov[:, j, :])
        cp(dve, wo, wo[:, j, :], b, b[:, 0:D])
        b = wstg[k % 2]; k += 1
        dma(sp, b, b[:, :], None, wqv[:, j, :])
        cp(pool, wqb, wqb[:, j, :], b, b[:, :])
    b = wstg[0]
    dma(sp, b, b[:, 0:256].rearrange("p (k n) -> p k n", k=2), None, L["skT_d"].rearrange("k p n -> p k n"))
    cp(dve, skb, skb[:], b, b[:, 0:256].rearrange("p (k n) -> p k n", k=2))
    P.pop()
    reps = sb("reps", [128, 4, D])
    dma(sp, reps, reps[:], L["rep_d"], L["rep_d"][:].rearrange("r p n -> p r n"))
    lnr = sb("lnr", [128, 2, 512])
    dma(sp, lnr, lnr[:].rearrange("p k n -> p (k n)"), None, L["vecs_d"][2:4].rearrange("k n -> (k n)").partition_broadcast(128))
    yf, yb_, ysb, ysq, bon, gat, yat = (sb(n, [128, 512]) for n in ("yf", "yb", "ysb", "ysq", "bonf", "gatf", "yatf"))
    st8 = sb("st8", [128, 40])
    mixin = sb("mixin", [128, D], BF16)
    mT = sb("mT", [128, 8, 128], BF16)
    x1 = sb("x1", [128, D]); h2 = sb("h2", [128, D])
    tmpx = h2
    h2b = sb("h2b", [128, D], BF16)
    h2T = sb("h2T", [128, 8, 128], BF16)
    qpT = sb("qpT", [128, 16, 128], BF16)
    sc = sb("sc", [128, 16, 128]); scw = sb("scw", [128, 16, 128])
    sv = sb("sv", [128, 16, 16]); si = sb("si", [128, 16, 16], U32); sif = sb("sif", [128, 16, 16])
    cand = sc; candw = scw
    tv = sb("tv", [128, 8, 16]); ti = sb("ti", [128, 8, 16], U32)
    thi = sb("thi", [128, 8, 16], U32); tlo = sb("tlo", [128, 8, 16], U32)
    thif = sb("thif", [128, 8, 16]); tlof = sb("tlof", [128, 8, 16])
    ge = sb("ge", [128, 8, 16]); gate = sb("gate", [128, 128])
    eq = scw
    sel = sb("sel", [128, 2, 128])
    eidx = sb("eidx", [128, 128], I32)
    GS = 4
    NG = 3
    uvs = [[sb(f"uv{g}_{k}", [128, 2 * D], BF16) for k in range(GS)] for g in range(NG)]
    NPB = 3
    prods = [sb(f"prod{i}", [128, D], BF16) for i in range(NPB)]
    acc = h2
    dgs = [sb(f"dg{i}", [128, 128], BF16) for i in range(4)]
    pacc = [L["pps"][5], L["pps"][6]]
    L["npool"][0] = 5
    avs = [sb(f"av{i}", [128, GS]) for i in range(2)]
    gls = [sb(f"gl{i}", [128, GS]) for i in range(2)]
    wws = [sb(f"ww{i}", [128, GS]) for i in range(2)]
    iota16 = cst[:, 1280:1296]
    nt_lim = L.get("final_tiles", NT)
    x1s = [x1, sb("x1b", [128, D])]
    h2bs = [h2b, sb("h2bb", [128, D], BF16)]
    gates = [gate, sb("gateb", [128, 128])]
    eidxs = [eidx, sb("eidxb", [128, 128], I32)]

    def prefix(i, par):
        x1, h2b, gate, eidx = x1s[par], h2bs[par], gates[par], eidxs[par]
        rows = slice(i * 128, (i + 1) * 128)
        dma(sp, yf, yf[:], L["yscr"][0], L["yscr"][0][rows, :])
        dma(sp, yb_, yb_[:], L["yscr"][1], L["yscr"][1][(NT - 1 - i) * 128:(NT - i) * 128, :])
        yield
        dma(sp, bon, bon[:], L["bon_d"], L["bon_d"][rows, :])
        dma(sp, gat, gat[:], L["gate_d"], L["gate_d"][rows, :])
        dma(sp, yat, yat[:], L["yattn_d"], L["yattn_d"][rows, :])
        yield
        dma(sp, x1, x1[:], None, xs[0, rows, :])
        ps = nps()
        mm(ps, ps[:, :], cst, identf, yf, yf[:], start=True, stop=False)
        yield
        mm(ps, ps[:, :], cst, Jf, yb_, yb_[:], start=False, stop=True)
        cp(act, ysb, ysb[:], ps, ps[:, :])
        y3 = ysb[:].rearrange("p (h n) -> p h n", h=8)
        yield
        P.I(dve, lambda e: e.tensor_reduce(out=st8[:, 0:8], in_=y3, axis=AX.X, op=ALU.add), R=[ysb], W=[st8])
        tt(pool, ysq, ysq[:], ysb, ysb[:], ysb, ysb[:], ALU.mult)
        P.I(dve, lambda e: e.tensor_reduce(out=st8[:, 8:16], in_=ysq[:].rearrange("p (h n) -> p h n", h=8), axis=AX.X, op=ALU.add),
            R=[ysq], W=[st8])
        yield
        ts(dve, st8, st8[:, 0:8], st8, st8[:, 0:8], 1.0 / 64, None, ALU.mult)
        tt(dve, st8, st8[:, 16:24], st8, st8[:, 0:8], st8, st8[:, 0:8], ALU.mult)
        stt(st8, st8[:, 24:32], st8, st8[:, 8:16], 1.0 / 64, st8, st8[:, 16:24], ALU.mult, ALU.subtract)
        yield
        ts(dve, st8, st8[:, 24:32], st8, st8[:, 24:32], 64e-5, None, ALU.add)
        rsqrt(st8, st8[:, 24:32], st8, st8[:, 24:32])
        tt(dve, ysb, y3, ysb, y3, st8, st8[:, 0:8].unsqueeze(2).to_broadcast([128, 8, 64]), ALU.subtract)
        yield
        tt(dve, ysb, y3, ysb, y3, st8, st8[:, 24:32].unsqueeze(2).to_broadcast([128, 8, 64]), ALU.mult)
        tt(pool, ysb, ysb[:], ysb, ysb[:], lnr, lnr[:, 0, :], ALU.mult)
        tt(pool, ysb, ysb[:], ysb, ysb[:], lnr, lnr[:, 1, :], ALU.add)
        yield
        tt(pool, ysb, ysb[:], ysb, ysb[:], bon, bon[:], ALU.add)
        tt(dve, mixin, mixin[:, 0:512], ysb, ysb[:], gat, gat[:], ALU.mult)
        cp(pool, mixin, mixin[:, 512:1024], yat, yat[:])
        yield
        if "yrwkv" in dbg_t and i < 2:
            tt(pool, ysb, ysb[:], ysb, ysb[:], gat, gat[:], ALU.mult)
            dma(sp, Buf(None), dbg_t["yrwkv"][rows, :], ysb, ysb[:])
        for j in range(8):
            tr(tpb, tpb[:, j * 128:(j + 1) * 128], mixin, mixin[:, j * 128:(j + 1) * 128], L["identb_t"], identb)
        cp(act, mT, mT[:], tpb, tpb[:, :].rearrange("p (j n) -> p j n", j=8))
        yield
        for hf in range(2):
            hs_ = slice(hf * 512, (hf + 1) * 512)
            ps = nps()
            for j in range(8):
                mm(ps, ps[:, :], mT, mT[:, j, :], wo, wo[:, j, hs_], start=(j == 0), stop=(j == 7))
            tt(dve, tmpx, tmpx[:, hs_], ps, ps[:, :], reps, reps[:, 0, hs_], ALU.mult)
            tt(pool, x1, x1[:, hs_], tmpx, tmpx[:, hs_], x1, x1[:, hs_], ALU.add)
        if "x1" in dbg_t and i < 2:
            dma(sp, Buf(None), dbg_t["x1"][rows, :], x1, x1[:])
        P.I(pool, lambda e: e.memset(st8[:, 32:33], 0.0), W=[st8])
        yield
        actf(h2b, h2b[:], x1, x1[:], AF.Square, accum=st8[:, 32:33], accb=[st8])
        ts(dve, st8, st8[:, 33:34], st8, st8[:, 32:33], 1.0 / D, 1e-6, ALU.mult, ALU.add)
        rsqrt(st8, st8[:, 33:34], st8, st8[:, 33:34])
        yield
        stt(h2, h2[:], x1, x1[:], st8[:, 33:34], reps, reps[:, 2, :], ALU.mult, ALU.mult, sb=[st8])
        tt(pool, h2, h2[:], h2, h2[:], reps, reps[:, 3, :], ALU.add)
        cp(act, h2b, h2b[:], h2, h2[:])
        yield
        for j in range(8):
            tr(tpb, tpb[:, j * 128:(j + 1) * 128], h2b, h2b[:, j * 128:(j + 1) * 128], L["identb_t"], identb)
        cp(act, h2T, h2T[:], tpb, tpb[:, :].rearrange("p (j n) -> p j n", j=8))
        for g4 in range(4):
            ps = nps()
            for gg in range(4):
                g = g4 * 4 + gg
                for j in range(8):
                    mm(ps, ps[:, gg * 128:(gg + 1) * 128], wqb, wqb[:, j, g * 128:(g + 1) * 128], h2T, h2T[:, j, :],
                       start=(j == 0), stop=(j == 7))
            cp(act, qpT, qpT[:, g4 * 4:(g4 + 1) * 4, :], ps, ps[:, :].rearrange("p (g n) -> p g n", g=4))
            yield
        yield
        for g4 in range(4):
            ps = nps()
            for gg in range(4):
                g = g4 * 4 + gg
                mm(ps, ps[:, gg * 128:(gg + 1) * 128], qpT, qpT[:, g, :], skb, skb[:, g % 2, :])
            cp(dve, sc, sc[:, g4 * 4:(g4 + 1) * 4, :], ps, ps[:, :].rearrange("p (g n) -> p g n", g=4))

        def top16(vals_b, vals, work_b, work, ov_b, ov, oi_b, oi):
            P.I(dve, lambda e: e.max(out=ov[:, 0:8], in_=vals), R=[vals_b], W=[ov_b])
            P.I(dve, lambda e: e.match_replace(out=work, in_to_replace=ov[:, 0:8], in_values=vals, imm_value=-1e30),
                R=[vals_b, ov_b], W=[work_b])
            P.I(dve, lambda e: e.max(out=ov[:, 8:16], in_=work), R=[work_b], W=[ov_b])
            P.I(dve, lambda e: e.max_index(out=oi[:, 0:8], in_max=ov[:, 0:8], in_values=vals), R=[vals_b, ov_b], W=[oi_b])
            P.I(dve, lambda e: e.max_index(out=oi[:, 8:16], in_max=ov[:, 8:16], in_values=vals), R=[vals_b, ov_b], W=[oi_b])
        for g in range(16):
            top16(sc, sc[:, g, :], scw, scw[:, g, :], sv, sv[:, g, :], si, si[:, g, :])
            if g % 2 == 1:
                yield
        yield
        sv4 = sv[:].rearrange("p (h k) n -> p h k n", k=2)
        candv = cand[:].rearrange("p g n -> p (g n)").rearrange("p (h n) -> p h n", h=8)
        candwv = candw[:].rearrange("p g n -> p (g n)").rearrange("p (h n) -> p h n", h=8)
        yield
        eqv = eq[:].rearrange("p g n -> p (g n)").rearrange("p (h a b) -> p h a b", h=8, a=16)
        tt(dve, cand, candv.rearrange("p h (a b) -> p h a b", a=16), sv, sv4[:, :, 0, :].unsqueeze(3).to_broadcast([128, 8, 16, 16]),
           sv, sv4[:, :, 1, :].unsqueeze(2).to_broadcast([128, 8, 16, 16]), ALU.add)
        for h in range(8):
            top16(cand, candv[:, h, :], candw, candwv[:, h, :], tv, tv[:, h, :], ti, ti[:, h, :])
            if h % 2 == 1:
                yield
        yield
        ts(dve, st8, st8[:, 0:8], tv, tv[:, :, 0], -1.0, None, ALU.mult)
        P.I(pool, lambda e: e.memset(st8[:, 8:16], 0.0), W=[st8])
        for h in range(8):
            actf(ge, ge[:, h, :], tv, tv[:, h, :], AF.Exp, bias=st8[:, h:h + 1], sb=[st8], accum=st8[:, 8 + h:9 + h], accb=[st8])
        yield
        P.I(dve, lambda e: e.reciprocal(out=st8[:, 16:24], in_=st8[:, 8:16]), R=[st8], W=[st8])
        tt(dve, gate, gate[:].rearrange("p (h n) -> p h n", h=8), ge, ge[:], st8, st8[:, 16:24].unsqueeze(2).to_broadcast([128, 8, 16]), ALU.mult)
        ts(dve, thi, thi[:], ti, ti[:], 4, None, ALU.logical_shift_right)
        yield
        ts(dve, tlo, tlo[:], ti, ti[:], 15, None, ALU.bitwise_and)
        cp(dve, thif, thif[:], thi, thi[:])
        cp(dve, tlof, tlof[:], tlo, tlo[:])
        yield
        cp(dve, sif, sif[:], si, si[:])
        sif4 = sif[:].rearrange("p (h k) n -> p h k n", k=2)
        io4 = iota16.unsqueeze(1).unsqueeze(1).to_broadcast([128, 8, 16, 16])
        yield
        for q_, (tf_b, kk_) in enumerate(((thif, 0), (tlof, 1))):
            tt(dve, eq, eqv, tf_b, tf_b[:].unsqueeze(3).to_broadcast([128, 8, 16, 16]), cst, io4, ALU.is_equal)
            tt(dve, eq, eqv, eq, eqv, sif, sif4[:, :, kk_, :].unsqueeze(2).to_broadcast([128, 8, 16, 16]), ALU.mult)
            P.I(dve, lambda e, q_=q_: e.tensor_reduce(out=sel[:, q_, :].rearrange("p (h n) -> p h n", h=8), in_=eqv, axis=AX.X, op=ALU.add),
                R=[eq], W=[sel])
        stt(sel, sel[:, 0, :], sel, sel[:, 0, :], 128.0, sel, sel[:, 1, :], ALU.mult, ALU.add)
        cp(dve, eidx, eidx[:], sel, sel[:, 0, :])
        yield
        if "eidx" in dbg_t and i < 1:
            dma(sp, Buf(None), dbg_t["eidx"][:, :], sel, sel[:, 0, :])
            dma(sp, Buf(None), dbg_t["gate"][:, :], gate, gate[:])
    def tailp(i, par, pump):
        x1, h2b, gate, eidx = x1s[par], h2bs[par], gates[par], eidxs[par]
        rows = slice(i * 128, (i + 1) * 128)
        ngrp = 128 // GS

        def gath(g):
            for k_ in range(GS):
                m = g * GS + k_
                t_ = uvs[g % NG][k_]
                P.D(pool, lambda e, t_=t_, m=m: e.indirect_dma_start(
                    out=t_[:], out_offset=None, in_=L["uv_bf"][:, :],
                    in_offset=bass.IndirectOffsetOnAxis(ap=eidx[:, m:m + 1].bitcast(U32), axis=0)), R=[eidx], W=[t_])

        def udots(g):
            av_, gl_ = avs[g % 2], gls[g % 2]
            P.I(pool, lambda e, av_=av_: e.memset(av_[:], 0.0), W=[av_], cost=120.0)
            for k_ in range(GS):
                t_ = uvs[g % NG][k_]
                pj = prods[(g * GS + k_) % NPB]
                tt(dve, pj, pj[:], t_, t_[:, 0:D], h2b, h2b[:], ALU.mult)
                actf(pj, pj[:], pj, pj[:], AF.Copy, accum=av_[:, k_:k_ + 1], accb=[av_])
            actf(gl_, gl_[:], av_, av_[:], AF.Gelu)

        def vaxpy(g):
            ms = slice(g * GS, (g + 1) * GS)
            gl_, ww_ = gls[g % 2], wws[g % 2]
            tt(dve, ww_, ww_[:], gl_, gl_[:], gate, gate[:, ms], ALU.mult)
            for k_ in range(GS):
                m = g * GS + k_
                t_ = uvs[g % NG][k_]
                dg = dgs[m % 4]
                ts(dve, dg, dg[:], L["identb_t"], identb, ww_[:, k_:k_ + 1], None, ALU.mult, sb=[ww_])
                for hf in range(2):
                    mm(pacc[hf], pacc[hf][:, :], dg, dg[:], t_, t_[:, D + hf * 512:D + (hf + 1) * 512], start=(m == 0), stop=(m == 127))
        gath(0); gath(1)
        udots(0)
        for g in range(ngrp):
            if g + 2 < ngrp:
                gath(g + 2)
            if g + 1 < ngrp:
                udots(g + 1)
            vaxpy(g)
            pump()
        for hf in range(2):
            hs_ = slice(hf * 512, (hf + 1) * 512)
            if "peer" in dbg_t and i < 2:
                cp(act, acc, acc[:, hs_], pacc[hf], pacc[hf][:, :])
                dma(sp, Buf(None), dbg_t["peer"][rows, hs_], acc, acc[:, hs_])
            tt(dve, acc, acc[:, hs_], pacc[hf], pacc[hf][:, :], reps, reps[:, 1, hs_], ALU.mult)
        tt(pool, acc, acc[:], acc, acc[:], x1, x1[:], ALU.add)
        dma(sp, yo, y_out[rows, :], acc, acc[:])

    def run_all(gen):
        for _ in gen:
            pass
    run_all(prefix(0, 0))
    for i in range(nt_lim):
        nxt = prefix(i + 1, (i + 1) % 2) if i + 1 < nt_lim else None

        def pump(nxt=nxt, k=L.get("pump_k", 2)):
            if nxt is None:
                return
            for _ in range(k):
                try:
                    next(nxt)
                except StopIteration:
                    return
        tailp(i, i % 2, pump)
        if nxt is not None:
            run_all(nxt)
    L["npool"][0] = 7
    P.pop()
    return L
```

```python
import numpy as np
from contextlib import ExitStack
import concourse.bass as bass
import concourse.mybir as mybir
from concourse.bass_utils import run_bass_kernel_spmd

F32 = mybir.dt.float32
BF16 = mybir.dt.bfloat16
I32 = mybir.dt.int32
U32 = mybir.dt.uint32
ALU = mybir.AluOpType
AF = mybir.ActivationFunctionType
AX = mybir.AxisListType

class Buf:
    __slots__ = ("t", "lw", "rd", "name", "psum")

    def __init__(self, t, name=""):
        self.t = t
        self.lw = None
        self.rd = []
        self.name = name
        self.psum = False

    def __getitem__(self, k):
        return self.t[k]


class Eng:
    def __init__(self, P, name, eng, kind):
        self.P = P
        self.name = name
        self.eng = eng
        self.kind = kind
        self.seen = {}
        self.sem = P.newsem(name)
        self.cnt = 0
        self.pool = []
        self.ndma = 0


class Op:
    __slots__ = ("id", "eng", "fn", "kind", "cost", "preds", "tag")


class Prog:
    K = 12
    HOP = 2000.0
    SELF = 150.0

    def __init__(self, nc, ctx):
        self.nc = nc
        self.ctx = ctx
        self.sems = {}
        self.pe = Eng(self, "pe", nc.tensor, "c")
        self.dve = Eng(self, "dve", nc.vector, "c")
        self.act = Eng(self, "act", nc.scalar, "c")
        self.pool = Eng(self, "pool", nc.gpsimd, "c")
        self.sp = Eng(self, "sp", nc.sync, "c")
        self.engs = (self.pe, self.dve, self.act, self.pool, self.sp)
        for e in (self.sp, self.act, self.pool):
            e.pool = [self.newsem(f"{e.name}_d{i}") for i in range(self.K)]
        self.nins = 0
        self.ops = []
        self.nid = 0
        self.base = 0
        self.reorder = REORDER

    def newsem(self, name):
        s = self.ctx.enter_context(self.nc.semaphore(name))
        self.sems[name] = s
        return name

    def sb(self, name, shape, dt=F32):
        self._uid = getattr(self, "_uid", 0) + 1
        t = self.ctx.enter_context(self.nc.sbuf_tensor(f"s{self._uid}_" + name, list(shape), dt))
        return Buf(t, name)

    def ps(self, name, shape, dt=F32):
        t = self.ctx.enter_context(self.nc.psum_tensor("p_" + name, list(shape), dt))
        b = Buf(t, name)
        b.psum = True
        return b

    def dram(self, name, shape, dt=F32, kind="Internal"):
        t = self.nc.dram_tensor(name, list(shape), dt, kind=kind)
        return Buf(t, name)

    def _record(self, E, fn, R, W, kind, cost):
        W = list(W) + [b for b in R if b.psum]
        R = [b for b in R if not b.psum]
        op = Op()
        op.id = self.nid
        self.nid += 1
        op.eng, op.fn, op.kind, op.cost, op.tag = E, fn, kind, cost, None
        preds = set()
        base = self.base
        for b in R:
            if b.lw is not None and b.lw >= base:
                preds.add(b.lw)
        for b in W:
            if b.lw is not None and b.lw >= base:
                preds.add(b.lw)
            for r in b.rd:
                if r >= base:
                    preds.add(r)
        op.preds = preds
        for b in W:
            b.lw = op.id
            b.rd = []
        for b in R:
            b.rd.append(op.id)
        self.ops.append(op)
        self.nins += 1

    def I(self, E, fn, R=(), W=(), cost=250.0):
        self._record(E, fn, R, W, "I", cost)

    def D(self, Q, fn, R=(), W=(), cost=3000.0):
        self._record(Q, fn, R, W, "D", cost)

    def _schedule(self, ops):
        import heapq
        n = len(ops)
        base = self.base
        succs = [[] for _ in range(n)]
        indeg = [0] * n
        for k, op in enumerate(ops):
            for p in op.preds:
                succs[p - base].append(k)
                indeg[k] += 1
        if not self.reorder:
            return list(range(n))
        future = {e.name: [] for e in self.engs}
        avail = {e.name: [] for e in self.engs}
        free = {e.name: 0.0 for e in self.engs}
        finish = [0.0] * n
        rtime = [0.0] * n
        bl = [0.0] * n
        for k in range(n - 1, -1, -1):
            op = ops[k]
            m_ = 0.0
            for s_ in succs[k]:
                lat = self.SELF if ops[s_].eng is op.eng else self.HOP
                v = lat + bl[s_]
                if v > m_:
                    m_ = v
            bl[k] = m_ + (120.0 if op.kind == "D" else op.cost)
        PRI = PRIORITY
        for k in range(n):
            if indeg[k] == 0:
                heapq.heappush(avail[ops[k].eng.name], ((-bl[k], k) if PRI else (k, k)))
        order = []
        while len(order) < n:
            best = None
            for e in self.engs:
                nm = e.name
                fu, av = future[nm], avail[nm]
                while fu and fu[0][0] <= free[nm]:
                    k_ = heapq.heappop(fu)[1]
                    heapq.heappush(av, ((-bl[k_], k_) if PRI else (k_, k_)))
                if av:
                    cand = (free[nm], av[0][1], nm, True)
                elif fu:
                    cand = (fu[0][0], fu[0][1], nm, False)
                else:
                    continue
                if best is None or cand[:2] < best[:2]:
                    best = cand
            start, k, nm, from_av = best
            if from_av:
                heapq.heappop(avail[nm])
            else:
                heapq.heappop(future[nm])
            op = ops[k]
            if op.kind == "D":
                free[nm] = start + 120.0
                finish[k] = start + op.cost
            else:
                free[nm] = start + op.cost
                finish[k] = free[nm]
            order.append(k)
            for s_ in succs[k]:
                lat = self.SELF if ops[s_].eng is op.eng else self.HOP
                t_ = finish[k] + lat
                if t_ > rtime[s_]:
                    rtime[s_] = t_
                indeg[s_] -= 1
                if indeg[s_] == 0:
                    heapq.heappush(future[ops[s_].eng.name], (rtime[s_], s_))
        return order

    def flush(self):
        ops = self.ops
        if not ops:
            return
        order = self._schedule(ops)
        base = self.base
        for k in order:
            op = ops[k]
            E = op.eng
            need = {}
            for p in op.preds:
                po = ops[p - base]
                if po.eng is E and E is self.pe:
                    continue
                key, val = po.tag
                if need.get(key, 0) < val:
                    need[key] = val
            for key, val in need.items():
                if E.seen.get(key, 0) < val:
                    E.eng.wait_ge(self.sems[key], val)
                    E.seen[key] = val
            if op.kind == "I":
                ins = op.fn(E.eng)
                E.cnt += 1
                ins.then_inc(self.sems[E.sem], 1)
                op.tag = (E.sem, E.cnt)
            else:
                j = E.ndma
                sname = E.pool[j % self.K]
                prev = 16 * (j // self.K)
                if prev > 0 and E.seen.get(sname, 0) < prev:
                    E.eng.wait_ge(self.sems[sname], prev)
                    E.seen[sname] = prev
                ins = op.fn(E.eng)
                ins.then_inc(self.sems[sname], 16)
                E.ndma += 1
                op.tag = (sname, prev + 16)
            op.fn = None
        self.base = self.nid
        self.ops = []

    def push(self):
        self._saved = getattr(self, "_saved", [])
        self._saved.append(self.ctx)
        self.ctx = ExitStack()
        self.ctx.__enter__()

    def barrier(self):
        self.flush()
        engs = self.engs
        for E in engs:
            for X in engs:
                if X is not E and X.cnt > 0 and E.seen.get(X.sem, 0) < X.cnt:
                    E.eng.wait_ge(self.sems[X.sem], X.cnt)
                    E.seen[X.sem] = X.cnt
            for Q in (self.sp, self.act, self.pool):
                for i, sname in enumerate(Q.pool):
                    n = (Q.ndma - i + self.K - 1) // self.K if Q.ndma > i else 0
                    if n > 0 and E.seen.get(sname, 0) < 16 * n:
                        E.eng.wait_ge(self.sems[sname], 16 * n)
                        E.seen[sname] = 16 * n

    def pop(self):
        self.barrier()
        self.ctx.__exit__(None, None, None)
        self.ctx = self._saved.pop()

    def finish(self, bufs):
        self.barrier()


T_OWN = 4096
REORDER = True
PRIORITY = True
NT = 32
D = 1024
LOGC = -0.6065306597126334
NEG = -100.0


def build_program(dbg=(), stages=(), small=False):
    nc = bass.Bass("TRN2", target_bir_lowering=False)

    def din(name, shape, dt=F32):
        return nc.dram_tensor(name, list(shape), dt, kind="ExternalInput").ap()

    xs = din("xs", [3, 33 * 128, D])
    flags_d = din("flags", [128, 16])
    ccol_d = din("ccol", [128, 8])
    ada_w = din("ada_w", [D, 6 * D])
    ada_b = din("ada_b", [1, 6 * D])
    g1col_d = din("g1col", [128, 8])
    g2col_d = din("g2col", [128, 8])
    g2row_d = din("g2row", [1, D])
    NCOL = [2688, 1664, 1408]
    wst_d = [din(f"wst{s}", [D, NCOL[s]]) for s in range(3)]
    mu_d = [din(f"mu{s}", [1, NCOL[s]]) for s in range(3)]
    w2a2_d = din("w2a2", [4, 2, 128, 512])
    w0a0_d = din("w0a0", [4, 2, 512])
    g2lora_d = din("g2lora", [128, 512])
    vecs_d = din("vecs", [8, 512])
    rope_d = din("rope", [3, 4096, 64])
    w_out_d = din("w_out", [D, D])
    wq_d = din("wq", [D, 2048])
    skT_d = din("skT", [2, 128, 128])
    NEXP = 512 if small else 16384
    u_tab = din("u_tab", [NEXP, D])
    v_tab = din("v_tab", [NEXP, D])
    consts_d = din("consts", [128, 2048])

    y_out = nc.dram_tensor("y_out", [T_OWN, D], F32, kind="ExternalOutput").ap()
    dbg_t = {}
    for name, shape in dbg:
        dbg_t[name] = nc.dram_tensor("dbg_" + name, list(shape), F32, kind="ExternalOutput").ap()

    ctx = ExitStack()
    with ctx:
        P = Prog(nc, ctx)
        L2 = _emit(P, nc, locals())
        for kv in stages:
            if isinstance(kv, tuple):
                L2[kv[0]] = kv[1]
        if "stop0" not in stages:
            _emit_streams(P, nc, L2)
        if "attn" in stages or not stages:
            _emit_attention(P, nc, L2)
        if "final" in stages or not stages:
            _emit_final(P, nc, L2)
            P.finish([L2["yo"]])
        else:
            P.finish([])
    return nc


def _emit(P, nc, L):
    xs, flags_d, ccol_d, ada_w, ada_b = L["xs"], L["flags_d"], L["ccol_d"], L["ada_w"], L["ada_b"]
    dbg_t = L["dbg_t"]
    pe, dve, act, pool, sp = P.pe, P.dve, P.act, P.pool, P.sp

    def fsz(ap):
        try:
            return float(ap.free_size())
        except Exception:
            return 256.0

    def mm(ob, o, lb, l, rb, r, start=True, stop=True):
        c = 64.0 + fsz(r) / 1.4
        if l.dtype == F32:
            c *= 4
        return P.I(pe, lambda e: e.matmul(o, lhsT=l, rhs=r, start=start, stop=stop), R=[lb, rb], W=[ob], cost=c)

    def tr(ob, o, ib, i, idb, idap):
        c = 64.0 + fsz(i) / 1.4
        if i.dtype == F32:
            c *= 4
        return P.I(pe, lambda e: e.transpose(o, i, idap), R=[ib, idb], W=[ob], cost=c)

    def vcost(E, o):
        if E is pool:
            return 150.0 + fsz(o) / 0.7
        return 80.0 + fsz(o) / 0.96

    def tt(E, ob, o, ab, a, bb, b, op):
        return P.I(E, lambda e: e.tensor_tensor(out=o, in0=a, in1=b, op=op), R=[ab, bb], W=[ob], cost=vcost(E, o))

    def ts(E, ob, o, ab, a, s1, s2, op0, op1=None, sb=()):
        if op1 is None:
            return P.I(E, lambda e: e.tensor_scalar(out=o, in0=a, scalar1=s1, scalar2=None, op0=op0), R=[ab, *sb], W=[ob], cost=vcost(E, o))
        return P.I(E, lambda e: e.tensor_scalar(out=o, in0=a, scalar1=s1, scalar2=s2, op0=op0, op1=op1), R=[ab, *sb], W=[ob], cost=vcost(E, o))

    def stt(ob, o, ab, a, sc, bb, b, op0, op1, sb=()):
        return P.I(dve, lambda e: e.scalar_tensor_tensor(out=o, in0=a, scalar=sc, in1=b, op0=op0, op1=op1), R=[ab, bb, *sb], W=[ob],
                   cost=vcost(dve, o))

    def actf(ob, o, ib, i, func, bias=0.0, scale=1.0, sb=(), accum=None, accb=()):
        def f(e):
            kw = dict(out=o, in_=i, func=func, bias=bias, scale=scale)
            if accum is not None:
                kw["accum_out"] = accum
            return e.activation(**kw)
        return P.I(act, f, R=[ib, *sb], W=[ob, *accb], cost=220.0 + fsz(o) / 1.4)

    def rsqrt(ob, o, ib, i):
        P.I(act, lambda e: e.activation(out=o, in_=i, func=AF.Sqrt), R=[ib], W=[ob], cost=220.0 + fsz(o) / 1.4)
        P.I(dve, lambda e: e.reciprocal(out=o, in_=o), R=[ob], W=[ob], cost=vcost(dve, o))

    def cp(E, ob, o, ib, i):
        if E is act:
            return P.I(E, lambda e: e.copy(out=o, in_=i), R=[ib], W=[ob], cost=220.0 + fsz(o) / 1.4)
        return P.I(E, lambda e: e.tensor_copy(out=o, in_=i), R=[ib], W=[ob], cost=vcost(E, o))

    def dma(Q, ob, o, ib, i):
        return P.D(Q, lambda e: e.dma_start(out=o, in_=i), R=[ib] if ib is not None else [], W=[ob] if ob is not None else [],
                   cost=2500.0 + fsz(o) * 128 * 4 / 150.0)

    DR = Buf(None, "dram_in")

    def dump(name, buf, ap):
        if name in dbg_t:
            dma(sp, Buf(None), dbg_t[name], buf, ap)

    cst = P.sb("cst", [128, 2048])
    dma(sp, cst, cst[:], None, L["consts_d"])
    identf = cst[:, 0:128]
    identb_t = P.sb("identb", [128, 256], BF16)
    cp(dve, identb_t, identb_t[:], cst, cst[:, 0:256])
    identb = identb_t[:, 0:128]
    flags = P.sb("flags", [128, 16])
    dma(sp, flags, flags[:], None, flags_d)
    ones1 = P.sb("ones1", [1, 128])
    P.I(dve, lambda e: e.memset(ones1[:], 1.0), W=[ones1])

    pps = [P.ps(f"pp{i}", [128, 512]) for i in range(7)]
    ppi = [0]

    npool = [7]

    def nps():
        b = pps[ppi[0] % npool[0]]
        ppi[0] += 1
        return b
    tpb = P.ps("tpb", [128, 1024], BF16)

    gs = P.sb("gs", [128, 16])
    modcol = P.sb("modcol", [128, 32])
    rep_d = P.dram("rep_d", [4, 128, D])
    P.push()
    cT = P.sb("cT", [128, 8])
    dma(sp, cT, cT[:], None, ccol_d)
    sT = P.sb("sT", [128, 8])
    actf(sT, sT[:], cT, cT[:], AF.Silu)
    adab = P.sb("adab", [1, 6 * D])
    dma(sp, adab, adab[:], None, ada_b)
    modrow = P.sb("modrow", [1, 6 * D])
    awv = ada_w.rearrange("(j p) n -> p j n", p=128)
    stg = [P.sb(f"stg{i}", [128, 4096]) for i in range(2)]
    for g in range(12):
        b = stg[g % 2]
        bv = b[:].rearrange("p (j n) -> p j n", j=8)
        dma(sp, b, bv, None, awv[:, :, g * 512:(g + 1) * 512])
        ps = nps()
        for j in range(8):
            mm(ps, ps[0:1, :], sT, sT[:, j:j + 1], b, bv[:, j, :], start=(j == 0), stop=(j == 7))
        tt(dve, modrow, modrow[0:1, g * 512:(g + 1) * 512], ps, ps[0:1, :], adab, adab[0:1, g * 512:(g + 1) * 512], ALU.add)
    dump("modrow", modrow, modrow[:])
    ps = nps()
    for pi, off in enumerate((0, 1024, 3072, 4096)):
        for j in range(8):
            mm(ps, ps[:, pi * 8 + j:pi * 8 + j + 1], modrow, modrow[0:1, off + j * 128:off + (j + 1) * 128],
               ones1, ones1[0:1, 0:1])
    cp(dve, modcol, modcol[:], ps, ps[:, 0:32])
    gcol = P.sb("gcol", [128, 16])
    dma(sp, gcol, gcol[:, 0:8], None, L["g1col_d"])
    dma(sp, gcol, gcol[:, 8:16], None, L["g2col_d"])
    stt(gs, gs[:, 0:8], modcol, modcol[:, 8:16], 1.0, gcol, gcol[:, 0:8], ALU.add, ALU.mult)
    stt(gs, gs[:, 8:16], modcol, modcol[:, 24:32], 1.0, gcol, gcol[:, 8:16], ALU.add, ALU.mult)
    g2rep = P.sb("g2rep", [128, D])
    dma(sp, g2rep, g2rep[:], None, L["g2row_d"].partition_broadcast(128))
    for ri, (name, off) in enumerate((("gt1", 2048), ("gt2", 5120), ("sc2", 4096), ("sh2", 3072))):
        t = stg[ri % 2]
        for hf in range(2):
            ps = nps()
            mm(ps, ps[:, :], ones1, ones1[0:1, :], modrow, modrow[0:1, off + hf * 512:off + (hf + 1) * 512])
            cp(act, t, t[:, hf * 512:(hf + 1) * 512], ps, ps[:, :])
        if name == "sc2":
            stt(t, t[:, 0:D], t, t[:, 0:D], 1.0, g2rep, g2rep[:], ALU.add, ALU.mult)
        dma(sp, rep_d, rep_d[ri], t, t[:, 0:D])
    dump("gs", gs, gs[:])
    P.pop()
    L2 = dict(L)
    L2.update(locals())
    return L2


def _consts():
    c = np.zeros((128, 2048), np.float32)
    i = np.arange(128)
    c[:, 0:128] = np.eye(128)
    c[:, 128:256] = np.eye(128)[::-1]
    same = (i[:, None] // 64) == (i[None, :] // 64)
    s_le_t = same & (i[:, None] <= i[None, :])
    s_lt_t = same & (i[:, None] < i[None, :])
    s_gt_t = same & (i[:, None] > i[None, :])
    c[:, 256:384] = LOGC * s_le_t
    c[:, 384:512] = LOGC * s_gt_t
    c[:, 512:640] = s_lt_t
    c[:, 640:768] = s_le_t
    c[:, 768:896] = s_gt_t
    c[:, 896:1024] = s_lt_t
    c[:, 1024:1152] = s_le_t
    c[:, 1152] = LOGC
    c[:, 1153] = 1.0
    c[:, 1280:1296] = np.arange(16)[None, :]
    return c


def host_layout(inp):
    f = np.float32
    w_in = inp["w_in"][0]
    mu = inp["mu_shift"][0]
    sl = lambda a, b: list(range(a, b))
    r_, k_, v_ = sl(0, 512), sl(512, 1024), sl(1024, 1536)
    wl = [sl(1536, 1600), sl(1600, 1664)]
    al = [sl(1664, 1728), sl(1728, 1792)]
    gl, q_, ak, av = sl(1792, 1920), sl(1920, 2432), sl(2432, 2560), sl(2560, 2688)
    mu_ext = np.concatenate([mu, np.zeros(768, f)])
    cols_a = r_ + k_ + v_ + wl[0] + al[0] + wl[1] + al[1] + gl + q_ + ak + av
    cols_b = r_ + k_ + v_ + wl[1] + al[1]
    w2, a2, w0, a0 = inp["w2"][0], inp["a2"][0], inp["w0"][0], inp["a0"][0]
    vecs = np.zeros((8, 512), f)
    vecs[0] = inp["k_k"][0]; vecs[1] = inp["k_a"][0]; vecs[2] = inp["ln_x_w"][0]; vecs[3] = inp["ln_x_b"][0]
    vecs[4] = inp["r_k"][0].reshape(-1); vecs[5] = inp["attn_out_g"][0]
    vecs[6] = np.tile(inp["q_norm_g"][0], 8); vecs[7] = np.tile(inp["k_norm_g"][0], 8)
    inv_freq = (10000.0 ** (-np.arange(0, 32, 2, dtype=f) / 32.0)).astype(f)

    def rope_tab(pos):
        rows = (pos // 64).astype(f); colsp = (pos % 64).astype(f)
        ar = rows[:, None] * inv_freq[None, :]; ac = colsp[:, None] * inv_freq[None, :]
        return np.concatenate([np.cos(ar), np.cos(ac), np.sin(ar), np.sin(ac)], axis=1).astype(f)

    consts = _consts()
    shared = dict(
        ada_w=inp["ada_w"][0], ada_b=inp["ada_b"], g1col=inp["norm1_g"][0].reshape(8, 128).T.copy(),
        g2col=inp["norm2_g"][0].reshape(8, 128).T.copy(), g2row=inp["norm2_g"],
        g2lora=inp["g2"][0], vecs=vecs, w_out=inp["w_out"][0], wq=inp["peer_wq"][0],
        skT=np.ascontiguousarray(inp["peer_sub_keys"][0].transpose(0, 2, 1)),
        u_tab=inp["peer_u"][0], v_tab=inp["peer_v"][0], consts=consts,
        wst0=np.ascontiguousarray(w_in[:, cols_a]), mu0=mu_ext[cols_a][None, :].copy(),
        wst1=np.ascontiguousarray(w_in[:, cols_b]), mu1=mu_ext[cols_b][None, :].copy(),
    )
    wc_cache = {}
    maps = []
    for core in range(8):
        xs = np.zeros((3, 33 * 128, D), f)
        fl = np.zeros((128, 16), f)
        if core < 4:
            xfull = inp["x_prompt"][core]; base = 0; c = inp["c_prompt"][core]; dc = 0
            own = xfull
            fl[:, 8] = NEG
            pos_c = np.zeros(4096, np.int64)
            xs[2, :4096] = xfull
        else:
            s, half = (core - 4) // 2, (core - 4) % 2
            xfull = inp["x_sample"][s]; base = half * 4096; c = inp["c_sample"][s]
            own = xfull[base:base + 4096]
            if half == 0:
                dc = 1
                ctxs = xfull[4096:8192][::-1]; pos_c = np.arange(8191, 4095, -1)
                xs[0, 4097] = xfull[4096]; fl[:, 1] = 1
                xs[1, 4096] = xfull[4096]; fl[:, 2] = 1
                xs[2, 4097] = xfull[4095]; fl[:, 5] = 1
                fl[:, 7] = 1
            else:
                dc = 0
                ctxs = xfull[0:4096]; pos_c = np.arange(0, 4096)
                xs[0, 4096] = xfull[4095]; fl[:, 0] = 1
                xs[1, 4097] = xfull[4095]; fl[:, 3] = 1
                xs[2, 4097] = xfull[4096]; fl[:, 5] = 1
                fl[:, 6] = 1
            xs[2, :4096] = ctxs
        xs[0, :4096] = own
        xs[1, :4096] = own[::-1]
        if dc not in wc_cache:
            cols_c = k_ + v_ + wl[dc] + al[dc] + ak + av
            wc_cache[dc] = (np.ascontiguousarray(w_in[:, cols_c]), mu_ext[cols_c][None, :].copy())
        w2a2 = np.zeros((4, 2, 128, 512), f); w0a0 = np.zeros((4, 2, 512), f)
        for slot, d in enumerate((0, 1, 1, dc)):
            w2a2[slot, 0, :64] = w2[d]; w2a2[slot, 1, 64:] = a2[d]; w0a0[slot, 0] = w0[d]; w0a0[slot, 1] = a0[d]
        rope = np.stack([rope_tab(base + np.arange(4096)), rope_tab(base + np.arange(4096)), rope_tab(pos_c)])
        m = dict(shared)
        m.update(xs=xs, flags=fl, ccol=c.reshape(8, 128).T.copy(), wst2=wc_cache[dc][0], mu2=wc_cache[dc][1],
                 w2a2=w2a2, w0a0=w0a0, rope=rope)
        maps.append(m)
    return maps


def kernel(**inputs):
    inp = {k: np.asarray(v) for k, v in inputs.items()}
    maps = host_layout(inp)
    nc = build_program()
    res = run_bass_kernel_spmd(nc, maps, core_ids=list(range(8)))
    outs = [np.asarray(r["y_out"], dtype=np.float32) for r in res.results]
    y_p = np.stack(outs[0:4])
    y_s = np.stack([np.concatenate([outs[4], outs[5]]), np.concatenate([outs[6], outs[7]])])
    return (y_p, y_s)


def _emit_streams(P, nc, L):
    pe, dve, act, pool, sp = P.pe, P.dve, P.act, P.pool, P.sp
    mm, tr, tt, ts, stt, actf, cp, dma, dump, rsqrt = (L[k] for k in ("mm", "tr", "tt", "ts", "stt", "actf", "cp", "dma", "dump", "rsqrt"))
    nps, tpb, cst, identb, identf, flags, gs, modcol = (L[k] for k in (
        "nps", "tpb", "cst", "identb", "identf", "flags", "gs", "modcol"))
    xs, NCOL, dbg_t = L["xs"], L["NCOL"], L["dbg_t"]
    sb = P.sb
    MASKA, MASKB = cst[:, 512:896], cst[:, 896:1152]
    UTs, SLs, negc = cst[:, 256:384], cst[:, 384:512], cst[:, 1152:1153]
    WG = 256

    Hctx = sb("Hctx", [64, 8, 64])
    yscr = [P.dram(f"yscr{i}", [T_OWN, 512]) for i in range(2)]
    qT_d = P.dram("qT_d", [128, 4, T_OWN], BF16)
    kT_d = P.dram("kT_d", [128, 2, 2 * T_OWN], BF16)
    vx_d = P.dram("vx_d", [128, 64, 130], BF16)
    gate_d = P.dram("gate_d", [T_OWN, 512])
    bon_d = P.dram("bon_d", [T_OWN, 512])
    L.update(yscr=yscr, qT_d=qT_d, kT_d=kT_d, vx_d=vx_d, gate_d=gate_d, bon_d=bon_d)

    cfgs = {
        "c": dict(si=2, slot=3, need_y=False, nshift=1152, offs=dict(k=0, v=512, lora=1024, akv=1152)),
        "a": dict(si=0, slot=0, need_y=True, nshift=1920,
                  offs=dict(r=0, k=512, v=1024, lora=1536, lora2=1664, gl=1792, q=1920, akv=2432)),
        "b": dict(si=1, slot=2, need_y=True, nshift=1664, offs=dict(r=0, k=512, v=1024, lora=1536)),
    }
    order = L.get("stream_order", ("c", "a", "b"))
    for sname in order:
        cf = cfgs[sname]
        si, offs, ncol, nsh, need_y, slot = cf["si"], cf["offs"], NCOL[cf["si"]], cf["nshift"], cf["need_y"], cf["slot"]
        is_a = sname == "a"
        P.push()
        W1 = sb("W1", [128, 8, ncol], BF16)
        W2 = sb("W2", [128, 8, nsh], BF16)
        P.push()
        murep = sb("murep", [128, ncol])
        dma(sp, murep, murep[:], None, L["mu_d"][si].partition_broadcast(128))
        wstg = [sb(f"wstg{i}", [128, 1024]) for i in range(2)]
        wtmp = sb("wtmp", [128, 1024])
        wv = L["wst_d"][si].rearrange("(j p) n -> p j n", p=128)
        k = 0
        for j in range(8):
            for p0 in range(0, ncol, 1024):
                n = min(1024, ncol - p0)
                b = wstg[k % 2]; k += 1
                dma(sp, b, b[:, 0:n], None, wv[:, j, p0:p0 + n])
                tt(dve, wtmp, wtmp[:, 0:n], b, b[:, 0:n], murep, murep[:, p0:p0 + n], ALU.mult)
                tt(pool, W1, W1[:, j, p0:p0 + n], b, b[:, 0:n], wtmp, wtmp[:, 0:n], ALU.subtract)
                if p0 < nsh:
                    m = min(n, nsh - p0)
                    ts(dve, W2, W2[:, j, p0:p0 + m], wtmp, wtmp[:, 0:m], 0.5, None, ALU.mult)
        P.pop()
        if L.get("stop_at") == 1:
            P.pop(); return L
        hring = sb("hring", [128, 4, 8, 128], BF16)
        haloT = sb("haloT", [128, 8, 2], BF16)
        xbuf = [sb("xbuf0", [128, D])]
        xn = sb("xn", [128, D], BF16)
        ssq = sb("ssq", [128, 4])
        hsT = sb("hsT", [128, 8, 128], BF16)
        lora_t = [sb(f"lora{i}", [128, 128], BF16) for i in range(3)]
        w2a2 = sb("w2a2s", [128, 2, 2, 512], BF16)
        w0a0 = sb("w0a0s", [128, 2, 2, 512])
        vecs = sb("vecs", [128, 2, 512])
        qkg = sb("qkg", [128, 2, 512])
        rkrep = sb("rkrep", [128, 512])
        g2l = sb("g2l", [128, 512], BF16)
        P.push()
        wst = sb("w2a2f", [128, 2, 2, 512])
        slots = (slot, 1) if is_a else (slot, slot)
        for q_, sl_ in enumerate(slots):
            dma(sp, wst, wst[:, q_, :, :], None, L["w2a2_d"][sl_].rearrange("k p n -> p k n"))
            dma(sp, w0a0, w0a0[:, q_, :, :].rearrange("p k n -> p (k n)"), None,
                L["w0a0_d"][sl_].rearrange("k n -> (k n)").partition_broadcast(128))
        cp(dve, w2a2, w2a2[:], wst, wst[:])
        dma(sp, vecs, vecs[:].rearrange("p k n -> p (k n)"), None,
            L["vecs_d"][0:2].rearrange("k n -> (k n)").partition_broadcast(128))
        dma(sp, qkg, qkg[:].rearrange("p k n -> p (k n)"), None,
            L["vecs_d"][6:8].rearrange("k n -> (k n)").partition_broadcast(128))
        dma(sp, rkrep, rkrep[:], None, L["vecs_d"][4:5].rearrange("k n -> (k n)").partition_broadcast(128))
        wst2 = sb("g2lf", [128, 512])
        dma(sp, wst2, wst2[:], None, L["g2lora_d"])
        cp(dve, g2l, g2l[:], wst2, wst2[:])
        P.pop()
        if L.get("stop_at") == 2:
            P.pop(); return L

        def f32t(n):
            return sb(n, [128, WG])

        def bft(n):
            return sb(n, [128, WG], BF16)
        kk, a_t, kd, sg, eL, enL, eLx, eh, tmpA, tmpB = (f32t(n) for n in (
            "kk", "a_t", "kd", "sg", "eL", "enL", "eLx", "eh", "tmpA", "tmpB"))
        TMB, TMK, TMA, TMR, Bh, Kh, Vt = (bft(n) for n in ("TMB", "TMK", "TMA", "TMR", "Bh", "Kh", "Vt"))
        TMq = [TMB, TMK, TMA, TMR]
        FM = sb("FM", [128, 8, 128], BF16)
        s8 = sb("s8", [128, 8])
        WCt = sb("WCt", [64, 8])
        SC1 = sb("SC1", [128, 8, 384], BF16)
        SC2 = sb("SC2", [128, 8, 256], BF16)
        LmT = [sb(f"LmT{i}", [128, 8, 256], BF16) for i in range(2)]
        Xb = [sb(f"Xb{i}", [128, 8, 128], BF16) for i in range(2)]
        GD = sb("GD", [64, 16, 128])
        QmT = sb("QmT", [64, 16, 64])
        Y0 = sb("Y0", [128, 8, 64])
        Hs = [sb(f"H{i}", [64, 8, 64]) for i in range(2)]
        ytile = sb("ytile", [128, 512])
        qn = sb("qn", [128, 512]); qr = sb("qr", [128, 512], BF16)
        qsq = qn
        rt = [tmpA, tmpB]
        ksb = sb("ksb", [128, WG]); rsb = sb("rsb", [128, WG])
        ropet = sb("ropet", [128, 64])
        vx = sb("vx", [128, 2, 65], BF16)
        P.I(pool, lambda e: e.memset(vx[:], 1.0), W=[vx])
        qTs = sb("qTs", [128, 5, 128], BF16)
        qd = sb("qd", [128, 256], BF16)
        a2_t = f32t("a2_t"); gt_t = f32t("gt_t"); bon = f32t("bon")

        if sname == "c":
            P.I(pool, lambda e: e.memset(Hs[0][:], 0.0), W=[Hs[0]])
        else:
            fc = 6 if sname == "a" else 7
            ts(dve, Hs[0], Hs[0][:], Hctx, Hctx[:], flags[0:64, fc:fc + 1], None, ALU.mult, sb=[flags])

        xk = [0]

        def build_h(ti, dest_b, dest):
            xt = xbuf[0]
            dma(sp, xt, xt[:], None, xs[si, ti * 128:(ti + 1) * 128, :])
            P.I(pool, lambda e: e.memset(ssq[:, 0:1], 0.0), W=[ssq])
            actf(xn, xn[:], xt, xt[:], AF.Square, accum=ssq[:, 0:1], accb=[ssq])
            ts(dve, ssq, ssq[:, 1:2], ssq, ssq[:, 0:1], 1.0 / D, 1e-6, ALU.mult, ALU.add)
            rsqrt(ssq, ssq[:, 2:3], ssq, ssq[:, 1:2])
            actf(xn, xn[:], xt, xt[:], AF.Copy, scale=ssq[:, 2:3], sb=[ssq])
            for j in range(8):
                tr(tpb, tpb[:, j * 128:(j + 1) * 128], xn, xn[:, j * 128:(j + 1) * 128], L["identb_t"], identb)
            for j in range(8):
                ts(dve, dest_b, dest[:, j, :], tpb, tpb[:, j * 128:(j + 1) * 128], gs[:, j:j + 1], modcol[:, j:j + 1],
                   ALU.mult, ALU.add, sb=[gs, modcol])

        hh = hring
        build_h(32, hring, hring[:, 3])
        fo = {"a": 0, "b": 2, "c": 4}[sname]
        ts(dve, haloT, haloT[:, :, 0:1], hring, hring[:, 3, :, 0:1], flags[:, fo:fo + 1], None, ALU.mult, sb=[flags])
        ts(dve, haloT, haloT[:, :, 1:2], hring, hring[:, 3, :, 1:2], flags[:, fo + 1:fo + 2], None, ALU.mult, sb=[flags])
        if L.get("stop_at") == 3:
            P.pop(); return L
        build_h(0, hring, hring[:, 0])

        hsTs = [hsT, sb("hsTb", [128, 8, 128], BF16)]
        loras = [lora_t, [sb(f"lorab{i}", [128, 128], BF16) for i in range(3)]]

        def mkproj(cur, hsT):
            def proj(ps_ap, ps_b, off, n, shift=True, fm=False):
                nmm = 16 if shift else 8
                c = 0
                for (hb, hap, Wt) in ((hring, cur, W1), (hsT, hsT[:], W2)):
                    if Wt is W2 and not shift:
                        continue
                    for j in range(8):
                        if fm:
                            mm(ps_b, ps_ap, Wt, Wt[:, j, off:off + n], hb, hap[:, j, :], start=(c == 0), stop=(c == nmm - 1))
                        else:
                            mm(ps_b, ps_ap, hb, hap[:, j, :], Wt, Wt[:, j, off:off + n], start=(c == 0), stop=(c == nmm - 1))
                        c += 1

            return proj

        fpi = [0]

        def npsf():
            b_ = L["pps"][5 + fpi[0] % 2]
            fpi[0] += 1
            return b_

        def front(i):
            nps = L["nps"]
            hsT, lora_t = hsTs[i % 2], loras[i % 2]
            if i + 1 < NT:
                build_h(i + 1, hring, hring[:, (i + 1) % 4])
            cur = hring[:, i % 4]
            yield
            prevcol = haloT[:, :, 0:1] if i == 0 else hring[:, (i - 1) % 4, :, 127:128]
            nextcol = haloT[:, :, 1:2] if i == NT - 1 else hring[:, (i + 1) % 4, :, 0:1]
            yield
            proj = mkproj(cur, hsT)
            tt(pool, hsT, hsT[:, :, 1:127], hring, cur[:, :, 0:126], hring, cur[:, :, 2:128], ALU.add)
            tt(pool, hsT, hsT[:, :, 0:1], hring if i else haloT, prevcol, hring, cur[:, :, 1:2], ALU.add)
            yield
            tt(pool, hsT, hsT[:, :, 127:128], hring, cur[:, :, 126:127], haloT if i == NT - 1 else hring, nextcol, ALU.add)

            lgroups = [("lora", 0)] + ([("lora2", 1), ("gl", 2)] if is_a else [])
            yield
            for nm, li in lgroups:
                ps = nps()
                proj(ps[:, 0:128], ps, offs[nm], 128, fm=True)
                if nm == "gl":
                    actf(lora_t[li], lora_t[li][:], ps, ps[:, 0:128], AF.Sigmoid)
                else:
                    actf(lora_t[li], lora_t[li][0:64, :], ps, ps[0:64, 0:128], AF.Tanh)
                    cp(dve, lora_t[li], lora_t[li][64:128, :], ps, ps[64:128, 0:128])

            if sname in ("a", "c"):
                dma(sp, ropet, ropet[:], None, L["rope_d"][si, i * 128:(i + 1) * 128, :])
                jobs = []
                if is_a:
                    pq_ = nps(); proj(pq_[:, :], pq_, offs["q"], 512, shift=False)
                    jobs.append((pq_, pq_[:, 0:512], 8, 0))
                pk_ = nps(); proj(pk_[:, 0:256], pk_, offs["akv"], 256, shift=False)
                jobs.append((pk_, pk_[:, 0:128], 2, 1))
                cp(act, vx, vx[:, :, 0:64], pk_, pk_[:, 128:256].rearrange("p (h n) -> p h n", h=2))
                kt_ = (i if is_a else 32 + i)
                dma(sp, vx_d, vx_d[:, kt_, :], vx, vx[:].rearrange("p h n -> p (h n)"))
                for (pb, pap, nh, gi) in jobs:
                    yield
                    w = nh * 64
                    actf(qsq, qsq[:, 0:w], pb, pap, AF.Square)
                    P.I(dve, lambda e, nh=nh, w=w: e.tensor_reduce(out=s8[:, 0:nh], in_=qsq[:, 0:w].rearrange("p (h n) -> p h n", h=nh),
                                                       axis=AX.X, op=ALU.add), R=[qsq], W=[s8])
                    ts(dve, s8, s8[:, 0:nh], s8, s8[:, 0:nh], 1.0 / 64, 1e-6, ALU.mult, ALU.add)
                    rsqrt(s8, s8[:, 0:nh], s8, s8[:, 0:nh])
                    tt(dve, qn, qn[:, 0:w].rearrange("p (h n) -> p h n", h=nh), pb, pap.rearrange("p (h n) -> p h n", h=nh),
                       s8, s8[:, 0:nh].unsqueeze(2).to_broadcast([128, nh, 64]), ALU.mult)
                    tt(pool, qn, qn[:, 0:w], qn, qn[:, 0:w], qkg, qkg[:, gi, 0:w], ALU.mult)
                    v5 = qn[:, 0:w].rearrange("p (h a f n) -> p h a f n", h=nh, a=2, f=2)
                    o5 = qr[:, 0:w].rearrange("p (h a f n) -> p h a f n", h=nh, a=2, f=2)
                    x1, x2 = v5[:, :, :, 0, :], v5[:, :, :, 1, :]
                    cs_ = ropet[:, 0:32].rearrange("p (a n) -> p a n", a=2).unsqueeze(1).to_broadcast([128, nh, 2, 16])
                    sn_ = ropet[:, 32:64].rearrange("p (a n) -> p a n", a=2).unsqueeze(1).to_broadcast([128, nh, 2, 16])
                    r0 = rt[0][:, 0:w // 2].rearrange("p (h a n) -> p h a n", h=nh, a=2)
                    r1 = rt[1][:, 0:w // 2].rearrange("p (h a n) -> p h a n", h=nh, a=2)
                    tt(dve, rt[0], r0, qn, x1, ropet, cs_, ALU.mult)
                    tt(pool, rt[1], r1, qn, x2, ropet, sn_, ALU.mult)
                    tt(dve, qr, o5[:, :, :, 0, :], rt[0], r0, rt[1], r1, ALU.subtract)
                    tt(dve, rt[0], r0, qn, x1, ropet, sn_, ALU.mult)
                    tt(pool, rt[1], r1, qn, x2, ropet, cs_, ALU.mult)
                    tt(dve, qr, o5[:, :, :, 1, :], rt[0], r0, rt[1], r1, ALU.add)
                    if gi == 0:
                        src_b, src, nt_ = qr, qr, 4
                    else:
                        cp(pool, qd, qd[:].rearrange("p (k d n) -> p k d n", k=2, d=2),
                           qr, qr[:, 0:128].rearrange("p (k n) -> p k n", k=2).unsqueeze(2).to_broadcast([128, 2, 2, 64]))
                        src_b, src, nt_ = qd, qd, 2
                    for t_ in range(nt_):
                        tr(tpb, tpb[:, t_ * 128:(t_ + 1) * 128], src_b, src[:, t_ * 128:(t_ + 1) * 128], L["identb_t"], identb)
                    cp(act, qTs, qTs[:, 0:nt_, :], tpb, tpb[:, 0:nt_ * 128].rearrange("p (t n) -> p t n", t=nt_))
                    if gi == 0:
                        dma(sp, qT_d, qT_d[:, :, i * 128:(i + 1) * 128], qTs, qTs[:, 0:4, :])
                    else:
                        dma(sp, kT_d, kT_d[:, :, kt_ * 128:(kt_ + 1) * 128], qTs, qTs[:, 0:2, :])


        def mkset(tag):
            return (sb("TMA" + tag, [128, WG], BF16), sb("TMR" + tag, [128, WG], BF16), sb("Bh" + tag, [128, WG], BF16),
                    sb("Kh" + tag, [128, WG], BF16), sb("Vt" + tag, [128, WG], BF16), sb("FM" + tag, [128, 8, 128], BF16), sb("WCt" + tag, [64, 8]))
        psets4 = [[(TMA, TMR, Bh, Kh, Vt, FM, WCt), mkset("b")], [mkset("c"), mkset("d")]]
        bpi = [0]

        def npsB():
            b_ = L["pps"][2 + bpi[0] % 5]
            bpi[0] += 1
            return b_

        def Pgen(i, hg):
            nps = L["nps"]
            hsT, lora_t = hsTs[i % 2], loras[i % 2]
            cur = hring[:, i % 4]
            proj = mkproj(cur, hsT)
            TMA, TMR, Bh, Kh, Vt, FM, WCt = psets4[i % 2][hg]
            TMq = [TMB, TMK, TMA, TMR]
            cg = slice(hg * WG, (hg + 1) * WG)
            pk = nps(); proj(pk[:, 0:WG], pk, offs["k"] + hg * WG, WG)
            cp(act, ksb, ksb[:], pk, pk[:, 0:WG])
            pv = nps(); proj(pv[:, 0:WG], pv, offs["v"] + hg * WG, WG)
            yield
            cp(act, Vt, Vt[:], pv, pv[:, 0:WG])
            if need_y:
                pr_ = nps(); proj(pr_[:, 0:WG], pr_, offs["r"] + hg * WG, WG)
                cp(act, rsb, rsb[:], pr_, pr_[:, 0:WG])
            pw = nps()
            yield
            mm(pw, pw[:, 0:WG], lora_t[0], lora_t[0][:, :], w2a2, w2a2[:, 0, 0, cg])
            mm(pw, pw[:, WG:2 * WG], lora_t[0], lora_t[0][:, :], w2a2, w2a2[:, 0, 1, cg])
            yield
            tt(dve, kk, kk[:], ksb, ksb[:], vecs, vecs[:, 0, cg], ALU.mult)
            yield
            tt(pool, tmpB, tmpB[:], kk, kk[:], kk, kk[:], ALU.mult)
            P.I(dve, lambda e: e.tensor_reduce(out=s8[:, 0:4], in_=tmpB[:].rearrange("p (h n) -> p h n", h=4),
                                               axis=AX.X, op=ALU.add), R=[tmpB], W=[s8])
            ts(dve, s8, s8[:, 0:4], s8, s8[:, 0:4], 1e-24, None, ALU.max)
            yield
            rsqrt(s8, s8[:, 0:4], s8, s8[:, 0:4])
            tt(pool, kk, kk[:].rearrange("p (h n) -> p h n", h=4), kk, kk[:].rearrange("p (h n) -> p h n", h=4),
               s8, s8[:, 0:4].unsqueeze(2).to_broadcast([128, 4, 64]), ALU.mult)
            tt(dve, tmpA, tmpA[:], pw, pw[:, WG:2 * WG], w0a0, w0a0[:, 0, 1, cg], ALU.add)
            yield
            actf(a_t, a_t[:], tmpA, tmpA[:], AF.Sigmoid)
            stt(tmpB, tmpB[:], a_t, a_t[:], -1.0, vecs, vecs[:, 1, cg], ALU.add, ALU.mult)
            stt(kd, kd[:], tmpB, tmpB[:], 1.0, ksb, ksb[:], ALU.add, ALU.mult)
            yield
            if is_a:
                pg2 = nps()
                mm(pg2, pg2[:, 0:WG], lora_t[2], lora_t[2][:, :], g2l, g2l[:, cg])
                mm(pg2, pg2[:, WG:2 * WG], lora_t[1], lora_t[1][:, :], w2a2, w2a2[:, 1, 1, cg])
                cp(act, gt_t, gt_t[:], pg2, pg2[:, 0:WG])
                dma(sp, gate_d, gate_d[i * 128:(i + 1) * 128, cg], gt_t, gt_t[:])
                tt(dve, tmpA, tmpA[:], pg2, pg2[:, WG:2 * WG], w0a0, w0a0[:, 1, 1, cg], ALU.add)
                actf(a2_t, a2_t[:], tmpA, tmpA[:], AF.Sigmoid)
                tt(pool, a2_t, a2_t[:], a2_t, a2_t[:], a_t, a_t[:], ALU.add)
                ts(dve, a2_t, a2_t[:], a2_t, a2_t[:], 0.5, -1.0, ALU.mult, ALU.add)
                tt(pool, a2_t, a2_t[:], a2_t, a2_t[:], vecs, vecs[:, 1, cg], ALU.mult)
                stt(bon, bon[:], a2_t, a2_t[:], 1.0, ksb, ksb[:], ALU.add, ALU.mult)
                tt(dve, bon, bon[:], bon, bon[:], rsb, rsb[:], ALU.mult)
                tt(pool, bon, bon[:], bon, bon[:], rkrep, rkrep[:, cg], ALU.mult)
                P.I(dve, lambda e: e.tensor_reduce(out=s8[:, 4:8], in_=bon[:].rearrange("p (h n) -> p h n", h=4),
                                                   axis=AX.X, op=ALU.add), R=[bon], W=[s8])
                tt(dve, bon, bon[:].rearrange("p (h n) -> p h n", h=4), Vt, Vt[:].rearrange("p (h n) -> p h n", h=4),
                   s8, s8[:, 4:8].unsqueeze(2).to_broadcast([128, 4, 64]), ALU.mult)
                dma(sp, bon_d, bon_d[i * 128:(i + 1) * 128, cg], bon, bon[:])
            yield
            tt(dve, tmpA, tmpA[:], pw, pw[:, 0:WG], w0a0, w0a0[:, 0, 0, cg], ALU.add)
            actf(sg, sg[:], tmpA, tmpA[:], AF.Sigmoid)
            yield
            pL = nps()
            mm(pL, pL[:, 0:WG], cst, UTs, sg, sg[:])
            mm(pL, pL[:, WG:2 * WG], cst, SLs, sg, sg[:])
            yield
            actf(enL, enL[:], pL, pL[:, 0:WG], AF.Exp, scale=-1.0)
            actf(eh, eh[:], pL, pL[:, WG:2 * WG], AF.Exp)
            stt(tmpA, tmpA[:], sg, sg[:], -LOGC, pL, pL[:, 0:WG], ALU.mult, ALU.add)
            yield
            actf(eLx, eLx[:], tmpA, tmpA[:], AF.Exp)
            stt(TMA, TMA[:], kk, kk[:], -1.0, eLx, eLx[:], ALU.mult, ALU.mult)
            tt(pool, tmpB, tmpB[:], kk, kk[:], a_t, a_t[:], ALU.mult)
            yield
            tt(pool, TMB, TMB[:], tmpB, tmpB[:], enL, enL[:], ALU.mult)
            tt(pool, Bh, Bh[:], tmpB, tmpB[:], eh, eh[:], ALU.mult)
            tt(dve, TMK, TMK[:], kd, kd[:], enL, enL[:], ALU.mult)
            yield
            tt(dve, Kh, Kh[:], kd, kd[:], eh, eh[:], ALU.mult)
            if need_y:
                actf(eL, eL[:], pL, pL[:, 0:WG], AF.Exp)
                tt(dve, TMR, TMR[:], rsb, rsb[:], eL, eL[:], ALU.mult)
            yield
            for c in range(2):
                pwc = nps()
                for hl in range(4):
                    mm(pwc, pwc[0:64, hl:hl + 1], sg, sg[c * 64:(c + 1) * 64, hl * 64:(hl + 1) * 64],
                       cst, negc[c * 64:(c + 1) * 64, :])
                actf(WCt, WCt[:, c * 4:(c + 1) * 4], pwc, pwc[0:64, 0:4], AF.Exp)
            yield
            yield
            nq = 4 if need_y else 3
            for hpl in range(2):
                for q_ in range(nq):
                    s_ = hpl * 4 + q_
                    tr(tpb, tpb[:, s_ * 128:(s_ + 1) * 128], TMq[q_], TMq[q_][:, hpl * 128:(hpl + 1) * 128], L["identb_t"], identb)
            if need_y:
                cp(act, FM, FM[:], tpb, tpb[:, :].rearrange("p (s n) -> p s n", s=8))
            else:
                for hpl in range(2):
                    cp(act, FM, FM[:, hpl * 4:hpl * 4 + 3, :], tpb,
                       tpb[:, hpl * 512:hpl * 512 + 384].rearrange("p (s n) -> p s n", s=3))

        def Mpart(i, pump):
            nps = npsB
            sets = psets4[i % 2]
            for hl in range(8):
                TMA, TMR, Bh, Kh, Vt, FM, WCt = sets[hl // 4]
                hq = hl % 4
                e_, hpl = hq % 2, hq // 2
                pr = slice(e_ * 64, (e_ + 1) * 64)
                fB, fK, fA = FM[pr, hpl * 4 + 0, :], FM[pr, hpl * 4 + 1, :], FM[pr, hpl * 4 + 2, :]
                p1 = nps(); p2 = nps()
                if need_y:
                    fAR = FM[pr, hpl * 4 + 2:hpl * 4 + 4, :]
                    mm(p1, p1[:, 0:256], FM, fB, FM, fAR)
                    mm(p2, p2[:, 0:256], FM, fK, FM, fAR)
                else:
                    mm(p1, p1[:, 0:128], FM, fB, FM, fA)
                    mm(p2, p2[:, 0:128], FM, fK, FM, fA)
                mm(p1, p1[:, 256:384], FM, fA, FM, fB)
                if need_y:
                    tt(dve, SC1, SC1[:, hl, :], p1, p1[:, 0:384], cst, MASKA, ALU.mult)
                    tt(dve, SC2, SC2[:, hl, :], p2, p2[:, 0:256], cst, MASKB, ALU.mult)
                else:
                    tt(dve, SC1, SC1[:, hl, 0:128], p1, p1[:, 0:128], cst, MASKA[:, 0:128], ALU.mult)
                    tt(dve, SC1, SC1[:, hl, 256:384], p1, p1[:, 256:384], cst, MASKA[:, 256:384], ALU.mult)
                    tt(dve, SC2, SC2[:, hl, 0:128], p2, p2[:, 0:128], cst, MASKB[:, 0:128], ALU.mult)
                pX = nps()
                mm(pX, pX[:, 0:64], SC2, SC2[:, hl, 0:128], Vt, Vt[:, hq * 64:(hq + 1) * 64])
                cp(act, Xb[0], Xb[0][:, hl, 0:64], pX, pX[:, 0:64])
                cp(pool, Xb[0], Xb[0][:, hl, 64:128], TMA, TMA[:, hq * 64:(hq + 1) * 64])
                if hl % 2 == 1:
                    pump()
            for lev in range(6):
                Xc, Xn_ = Xb[lev % 2], Xb[(lev + 1) % 2]
                for hl in range(8):
                    if lev == 0:
                        LTb, LTc, Lmb, Lmc = SC1, SC1[:, hl, 0:128], SC1, SC1[:, hl, 256:384]
                    else:
                        src = LmT[(lev - 1) % 2]
                        LTb, LTc, Lmb, Lmc = src, src[:, hl, 128:256], src, src[:, hl, 0:128]
                    pa = nps()
                    mm(pa, pa[:, 0:128], LTb, LTc, Xc, Xc[:, hl, :])
                    if lev < 5:
                        mm(pa, pa[:, 128:256], LTb, LTc, Lmb, Lmc)
                        mm(pa, pa[:, 256:384], Lmb, Lmc, LTb, LTc)
                    tt(dve, Xn_, Xn_[:, hl, :], Xc, Xc[:, hl, :], pa, pa[:, 0:128], ALU.add)
                    if lev < 5:
                        cp(act, LmT[lev % 2], LmT[lev % 2][:, hl, :], pa, pa[:, 128:384])
                    if hl % 4 == 3:
                        pump()
            Xf = Xb[0]
            for hl in range(8):
                TMA, TMR, Bh, Kh, Vt, FM, WCt = sets[hl // 4]
                hq = hl % 4
                hs_ = slice(hq * 64, (hq + 1) * 64)
                for c in range(2):
                    pg = nps()
                    cs = slice(c * 64, (c + 1) * 64)
                    mm(pg, pg[0:64, 0:64], Xf, Xf[cs, hl, 64:128], Bh, Bh[cs, hs_])
                    mm(pg, pg[0:64, 64:128], Bh, Bh[cs, hs_], Xf, Xf[cs, hl, 0:64], start=True, stop=False)
                    mm(pg, pg[0:64, 64:128], Kh, Kh[cs, hs_], Vt, Vt[cs, hs_], start=False, stop=True)
                    cp(act, GD, GD[:, hl * 2 + c, :], pg, pg[0:64, 0:128])
                if need_y:
                    for c in range(2):
                        pq2 = nps()
                        cs = slice(c * 64, (c + 1) * 64)
                        mm(pq2, pq2[0:64, 0:64], Xf, Xf[cs, hl, 64:128], SC1, SC1[cs, hl, 128 + c * 64:128 + (c + 1) * 64],
                           start=True, stop=False)
                        mm(pq2, pq2[0:64, 0:64], TMR, TMR[cs, hs_], L["identb_t"], identb[cs, c * 64:(c + 1) * 64],
                           start=False, stop=True)
                        cp(act, QmT, QmT[:, hl * 2 + c, :], pq2, pq2[0:64, 0:64])
                    py = nps()
                    mm(py, py[:, 0:64], SC1, SC1[:, hl, 128:256], Xf, Xf[:, hl, 0:64], start=True, stop=False)
                    mm(py, py[:, 0:64], SC2, SC2[:, hl, 128:256], Vt, Vt[:, hs_], start=False, stop=True)
                    cp(act, Y0, Y0[:, hl, :], py, py[:, 0:64])
                if hl % 2 == 1:
                    pump()
            for c in range(2):
                cs = slice(c * 64, (c + 1) * 64)
                Hc_b, Hn_b = Hs[c], Hs[1 - c]
                ph = nps()
                pyc = nps() if need_y else None
                for hl in range(8):
                    WCt = sets[hl // 4][6]
                    hq = hl % 4
                    hs_ = slice(hl * 64, (hl + 1) * 64)
                    if need_y:
                        mm(pyc, pyc[cs, hs_], QmT, QmT[:, hl * 2 + c, :], Hc_b, Hc_b[:, hl, :])
                    mm(ph, ph[0:64, hs_], GD, GD[:, hl * 2 + c, 0:64], Hc_b, Hc_b[:, hl, :], start=True, stop=False)
                    mm(ph, ph[0:64, hs_], cst, identf[0:64, 0:64], GD, GD[:, hl * 2 + c, 64:128], start=False, stop=True)
                    stt(Hn_b, Hn_b[:, hl, :], Hc_b, Hc_b[:, hl, :], WCt[:, c * 4 + hq:c * 4 + hq + 1], ph, ph[0:64, hs_],
                        ALU.mult, ALU.add, sb=[WCt])
                if need_y:
                    tt(dve, ytile, ytile[cs, :], Y0, Y0[cs, :, :].rearrange("p h n -> p (h n)"), pyc, pyc[cs, 0:512], ALU.add)
                pump()

        def Gchain(i):
            yield from front(i)
            yield from Pgen(i, 0)
            yield from Pgen(i, 1)

        def tile_post(i):
            if need_y:
                dma(sp, yscr[0 if is_a else 1], yscr[0 if is_a else 1][i * 128:(i + 1) * 128, :], ytile, ytile[:])
                if ("ytile_" + sname) in dbg_t and i < 2:
                    dma(sp, Buf(None), dbg_t["ytile_" + sname][i * 128:(i + 1) * 128, :], ytile, ytile[:])


        def run_all(gen):
            for _ in gen:
                pass

        def mkpump(gen, k):
            def pump():
                for _ in range(k):
                    try:
                        next(gen)
                    except StopIteration:
                        return
            return pump
        ntl = L.get("nt_limit", NT)
        L["npool"][0] = 2
        run_all(Gchain(0))
        for i in range(ntl):
            g2 = Gchain(i + 1) if i + 1 < ntl else iter(())
            Mpart(i, mkpump(g2, L.get("spump_k2", 4)))
            run_all(g2)
            tile_post(i)
        L["npool"][0] = 7
        if sname == "c":
            cp(dve, Hctx, Hctx[:], Hs[0], Hs[0][:])
            if "hctx" in dbg_t:
                dma(sp, Buf(None), dbg_t["hctx"][:, :], Hctx, Hctx[:].rearrange("p h n -> p (h n)"))
        P.pop()
    return L


def _emit_attention(P, nc, L):
    pe, dve, act, pool, sp = P.pe, P.dve, P.act, P.pool, P.sp
    mm, tr, tt, ts, stt, actf, cp, dma, dump, rsqrt = (L[k] for k in ("mm", "tr", "tt", "ts", "stt", "actf", "cp", "dma", "dump", "rsqrt"))
    nps, cst, identf, flags, pps, npool = (L[k] for k in ("nps", "cst", "identf", "flags", "pps", "npool"))
    dbg_t = L["dbg_t"]
    sb = P.sb
    P.push()
    qT = sb("qT", [128, 4, T_OWN], BF16)
    dma(sp, qT, qT[:], L["qT_d"], L["qT_d"][:])
    kT = sb("kT2", [128, 2, 2 * T_OWN], BF16)
    dma(sp, kT, kT[:], L["kT_d"], L["kT_d"][:])
    Vx = sb("Vx", [128, 64, 130], BF16)
    dma(sp, Vx, Vx[:], L["vx_d"], L["vx_d"][:])
    outg = sb("outg", [128, 512])
    dma(sp, outg, outg[:], None, L["vecs_d"][5:6].rearrange("k n -> (k n)").partition_broadcast(128))
    pT = [sb(f"pT{i}", [128, 512], BF16) for i in range(3)]
    oTs = sb("oTs", [65, 512])
    osb = sb("osb", [128, 4, 65])
    o2 = sb("o2", [128, 4, 64])
    ssa = sb("ssa", [128, 8])
    yat = sb("yat", [128, 4, 64])
    yattn_d = P.dram("yattn_d", [T_OWN, 512])
    L["yattn_d"] = yattn_d
    npool[0] = 5
    acc = [pps[5], pps[6]]
    it = 0
    uv_bf = P.dram("uv_bf", [L["NEXP"], 2 * D], BF16)
    L["uv_bf"] = uv_bf
    cbf = [sb(f"cbf{i}", [128, 4096]) for i in range(2)]
    cbb = [sb(f"cbb{i}", [128, 4096], BF16) for i in range(2)]
    nchunk = L["NEXP"] // 512
    cast_jobs = [(tb, c) for tb in range(2) for c in range(nchunk)]

    def cast_job(n):
        if n >= len(cast_jobs):
            return
        tb, c = cast_jobs[n]
        src = (L["u_tab"], L["v_tab"])[tb]
        f_, b_ = cbf[n % 2], cbb[n % 2]
        P.D(pool, lambda e: e.dma_start(out=f_[:], in_=src[c * 512:(c + 1) * 512, :].rearrange("(p j) n -> p (j n)", j=4)), W=[f_])
        cp(dve, b_, b_[:], f_, f_[:])
        P.D(pool, lambda e: e.dma_start(out=uv_bf[c * 512:(c + 1) * 512, tb * D:(tb + 1) * D].rearrange("(p j) n -> p j n", j=4),
                                        in_=b_[:].rearrange("p (j n) -> p j n", j=4)), R=[b_], W=[uv_bf])
    njob = [0]
    nh_lim = L.get("attn_heads", 8)
    nqb_lim = L.get("attn_qb", 8)
    pT6 = pT + [sb(f"pTx{i}", [128, 512], BF16) for i in range(3)]

    def head_tail(h, qb, po):
        cp(act, oTs, oTs[:, :], po, po[0:65, :])
        ptp = nps()
        for t in range(4):
            tr(ptp, ptp[:, t * 65:(t + 1) * 65], oTs, oTs[0:65, t * 128:(t + 1) * 128], cst, identf[0:65, 0:65])
        cp(dve, osb, osb[:], ptp, ptp[:, 0:260].rearrange("p (t n) -> p t n", t=4))
        P.I(dve, lambda e: e.reciprocal(out=ssa[:, 0:4], in_=osb[:, :, 64]), R=[osb], W=[ssa])
        tt(dve, o2, o2[:], osb, osb[:, :, 0:64], ssa, ssa[:, 0:4].unsqueeze(2).to_broadcast([128, 4, 64]), ALU.mult)
        tt(pool, yat, yat[:], o2, o2[:], o2, o2[:], ALU.mult)
        P.I(dve, lambda e: e.tensor_reduce(out=ssa[:, 4:8], in_=yat[:], axis=AX.X, op=ALU.add), R=[yat], W=[ssa])
        ts(dve, ssa, ssa[:, 4:8], ssa, ssa[:, 4:8], 1.0 / 64, 1e-6, ALU.mult, ALU.add)
        rsqrt(ssa, ssa[:, 4:8], ssa, ssa[:, 4:8])
        tt(dve, o2, o2[:], o2, o2[:], ssa, ssa[:, 4:8].unsqueeze(2).to_broadcast([128, 4, 64]), ALU.mult)
        tt(pool, yat, yat[:], o2, o2[:], outg, outg[:, h * 64:(h + 1) * 64].unsqueeze(1).to_broadcast([128, 4, 64]), ALU.mult)
        dma(sp, yattn_d, yattn_d[qb * 512:(qb + 1) * 512, h * 64:(h + 1) * 64].rearrange("(t p) n -> p t n", p=128), yat, yat[:])
        if "yattn" in dbg_t and qb == 0:
            dma(sp, Buf(None), dbg_t["yattn"][:, h * 64:(h + 1) * 64].rearrange("(t p) n -> p t n", p=128), yat, yat[:])

    for hp in range(nh_lim // 2):
        kvh = hp // 2
        for qb in range(nqb_lim):
            cast_job(njob[0]); njob[0] += 1
            cast_job(njob[0]); njob[0] += 1
            for kt in range(64):
                pss = []
                for e_ in range(2):
                    pr = slice(e_ * 64, (e_ + 1) * 64)
                    ps_ = nps()
                    mm(ps_, ps_[:, :], kT, kT[pr, kvh, kt * 128:(kt + 1) * 128], qT, qT[pr, hp, qb * 512:(qb + 1) * 512])
                    pss.append(ps_)
                for e_ in range(2):
                    pt = pT6[(kt * 2 + e_) % 6]
                    if kt >= 32:
                        actf(pt, pt[:], pss[e_], pss[e_][:, :], AF.Exp, bias=flags[:, 8:9], scale=0.125, sb=[flags])
                    else:
                        actf(pt, pt[:], pss[e_], pss[e_][:, :], AF.Exp, scale=0.125)
                    mm(acc[e_], acc[e_][0:65, :], Vx, Vx[:, kt, kvh * 65:(kvh + 1) * 65], pt, pt[:], start=(kt == 0), stop=(kt == 63))
            for e_ in range(2):
                head_tail(2 * hp + e_, qb, acc[e_])
    while njob[0] < len(cast_jobs):
        cast_job(njob[0]); njob[0] += 1
    npool[0] = 7
    P.pop()
    return L


def _emit_final(P, nc, L):
    pe, dve, act, pool, sp = P.pe, P.dve, P.act, P.pool, P.sp
    mm, tr, tt, ts, stt, actf, cp, dma, dump, rsqrt = (L[k] for k in ("mm", "tr", "tt", "ts", "stt", "actf", "cp", "dma", "dump", "rsqrt"))
    nps, tpb, cst, identb, identf, flags = (L[k] for k in ("nps", "tpb", "cst", "identb", "identf", "flags"))
    dbg_t, xs, y_out = L["dbg_t"], L["xs"], L["y_out"]
    u_tab, v_tab = L["u_tab"], L["v_tab"]
    sb = P.sb
    yo = Buf(None, "y_out")
    L["yo"] = yo
    P.push()
    Jf = cst[:, 128:256]
    wo = sb("wo", [128, 8, D], BF16)
    wqb = sb("wqb", [128, 8, 2048], BF16)
    skb = sb("skb", [128, 2, 128], BF16)
    P.push()
    wstg = [sb(f"fstg{i}", [128, 2048]) for i in range(2)]
    wov = L["w_out_d"].rearrange("(j p) n -> p j n", p=128)
    wqv = L["wq_d"].rearrange("(j p) n -> p j n", p=128)
    k = 0
    for j in range(8):
        b = wstg[k % 2]; k += 1
        dma(sp, b, b[:, 0:D], None, wov[:, j, :])
        cp(dve, wo, wo[:, j, :], b, b[:, 0:D])
        b = wstg[k % 2]; k += 1
        dma(sp, b, b[:, :], None, wqv[:, j, :])
        cp(pool, wqb, wqb[:, j, :], b, b[:, :])
    b = wstg[0]
    dma(sp, b, b[:, 0:256].rearrange("p (k n) -> p k n", k=2), None, L["skT_d"].rearrange("k p n -> p k n"))
    cp(dve, skb, skb[:], b, b[:, 0:256].rearrange("p (k n) -> p k n", k=2))
    P.pop()
    reps = sb("reps", [128, 4, D])
    dma(sp, reps, reps[:], L["rep_d"], L["rep_d"][:].rearrange("r p n -> p r n"))
    lnr = sb("lnr", [128, 2, 512])
    dma(sp, lnr, lnr[:].rearrange("p k n -> p (k n)"), None, L["vecs_d"][2:4].rearrange("k n -> (k n)").partition_broadcast(128))
    yf, yb_, ysb, ysq, bon, gat, yat = (sb(n, [128, 512]) for n in ("yf", "yb", "ysb", "ysq", "bonf", "gatf", "yatf"))
    st8 = sb("st8", [128, 40])
    mixin = sb("mixin", [128, D], BF16)
    mT = sb("mT", [128, 8, 128], BF16)
    x1 = sb("x1", [128, D]); h2 = sb("h2", [128, D])
    tmpx = h2
    h2b = sb("h2b", [128, D], BF16)
    h2T = sb("h2T", [128, 8, 128], BF16)
    qpT = sb("qpT", [128, 16, 128], BF16)
    sc = sb("sc", [128, 16, 128]); scw = sb("scw", [128, 16, 128])
    sv = sb("sv", [128, 16, 16]); si = sb("si", [128, 16, 16], U32); sif = sb("sif", [128, 16, 16])
    cand = sc; candw = scw
    tv = sb("tv", [128, 8, 16]); ti = sb("ti", [128, 8, 16], U32)
    thi = sb("thi", [128, 8, 16], U32); tlo = sb("tlo", [128, 8, 16], U32)
    thif = sb("thif", [128, 8, 16]); tlof = sb("tlof", [128, 8, 16])
    ge = sb("ge", [128, 8, 16]); gate = sb("gate", [128, 128])
    eq = scw
    sel = sb("sel", [128, 2, 128])
    eidx = sb("eidx", [128, 128], I32)
    GS = 4
    NG = 3
    uvs = [[sb(f"uv{g}_{k}", [128, 2 * D], BF16) for k in range(GS)] for g in range(NG)]
    NPB = 3
    prods = [sb(f"prod{i}", [128, D], BF16) for i in range(NPB)]
    acc = h2
    dgs = [sb(f"dg{i}", [128, 128], BF16) for i in range(4)]
    pacc = [L["pps"][5], L["pps"][6]]
    L["npool"][0] = 5
    avs = [sb(f"av{i}", [128, GS]) for i in range(2)]
    gls = [sb(f"gl{i}", [128, GS]) for i in range(2)]
    wws = [sb(f"ww{i}", [128, GS]) for i in range(2)]
    iota16 = cst[:, 1280:1296]
    nt_lim = L.get("final_tiles", NT)
    x1s = [x1, sb("x1b", [128, D])]
    h2bs = [h2b, sb("h2bb", [128, D], BF16)]
    gates = [gate, sb("gateb", [128, 128])]
    eidxs = [eidx, sb("eidxb", [128, 128], I32)]

    def prefix(i, par):
        x1, h2b, gate, eidx = x1s[par], h2bs[par], gates[par], eidxs[par]
        rows = slice(i * 128, (i + 1) * 128)
        dma(sp, yf, yf[:], L["yscr"][0], L["yscr"][0][rows, :])
        dma(sp, yb_, yb_[:], L["yscr"][1], L["yscr"][1][(NT - 1 - i) * 128:(NT - i) * 128, :])
        yield
        dma(sp, bon, bon[:], L["bon_d"], L["bon_d"][rows, :])
        dma(sp, gat, gat[:], L["gate_d"], L["gate_d"][rows, :])
        dma(sp, yat, yat[:], L["yattn_d"], L["yattn_d"][rows, :])
        yield
        dma(sp, x1, x1[:], None, xs[0, rows, :])
        ps = nps()
        mm(ps, ps[:, :], cst, identf, yf, yf[:], start=True, stop=False)
        yield
        mm(ps, ps[:, :], cst, Jf, yb_, yb_[:], start=False, stop=True)
        cp(act, ysb, ysb[:], ps, ps[:, :])
        y3 = ysb[:].rearrange("p (h n) -> p h n", h=8)
        yield
        P.I(dve, lambda e: e.tensor_reduce(out=st8[:, 0:8], in_=y3, axis=AX.X, op=ALU.add), R=[ysb], W=[st8])
        tt(pool, ysq, ysq[:], ysb, ysb[:], ysb, ysb[:], ALU.mult)
        P.I(dve, lambda e: e.tensor_reduce(out=st8[:, 8:16], in_=ysq[:].rearrange("p (h n) -> p h n", h=8), axis=AX.X, op=ALU.add),
            R=[ysq], W=[st8])
        yield
        ts(dve, st8, st8[:, 0:8], st8, st8[:, 0:8], 1.0 / 64, None, ALU.mult)
        tt(dve, st8, st8[:, 16:24], st8, st8[:, 0:8], st8, st8[:, 0:8], ALU.mult)
        stt(st8, st8[:, 24:32], st8, st8[:, 8:16], 1.0 / 64, st8, st8[:, 16:24], ALU.mult, ALU.subtract)
        yield
        ts(dve, st8, st8[:, 24:32], st8, st8[:, 24:32], 64e-5, None, ALU.add)
        rsqrt(st8, st8[:, 24:32], st8, st8[:, 24:32])
        tt(dve, ysb, y3, ysb, y3, st8, st8[:, 0:8].unsqueeze(2).to_broadcast([128, 8, 64]), ALU.subtract)
        yield
        tt(dve, ysb, y3, ysb, y3, st8, st8[:, 24:32].unsqueeze(2).to_broadcast([128, 8, 64]), ALU.mult)
        tt(pool, ysb, ysb[:], ysb, ysb[:], lnr, lnr[:, 0, :], ALU.mult)
        tt(pool, ysb, ysb[:], ysb, ysb[:], lnr, lnr[:, 1, :], ALU.add)
        yield
        tt(pool, ysb, ysb[:], ysb, ysb[:], bon, bon[:], ALU.add)
        tt(dve, mixin, mixin[:, 0:512], ysb, ysb[:], gat, gat[:], ALU.mult)
        cp(pool, mixin, mixin[:, 512:1024], yat, yat[:])
        yield
        if "yrwkv" in dbg_t and i < 2:
            tt(pool, ysb, ysb[:], ysb, ysb[:], gat, gat[:], ALU.mult)
            dma(sp, Buf(None), dbg_t["yrwkv"][rows, :], ysb, ysb[:])
        for j in range(8):
            tr(tpb, tpb[:, j * 128:(j + 1) * 128], mixin, mixin[:, j * 128:(j + 1) * 128], L["identb_t"], identb)
        cp(act, mT, mT[:], tpb, tpb[:, :].rearrange("p (j n) -> p j n", j=8))
        yield
        for hf in range(2):
            hs_ = slice(hf * 512, (hf + 1) * 512)
            ps = nps()
            for j in range(8):
                mm(ps, ps[:, :], mT, mT[:, j, :], wo, wo[:, j, hs_], start=(j == 0), stop=(j == 7))
            tt(dve, tmpx, tmpx[:, hs_], ps, ps[:, :], reps, reps[:, 0, hs_], ALU.mult)
            tt(pool, x1, x1[:, hs_], tmpx, tmpx[:, hs_], x1, x1[:, hs_], ALU.add)
        if "x1" in dbg_t and i < 2:
            dma(sp, Buf(None), dbg_t["x1"][rows, :], x1, x1[:])
        P.I(pool, lambda e: e.memset(st8[:, 32:33], 0.0), W=[st8])
        yield
        actf(h2b, h2b[:], x1, x1[:], AF.Square, accum=st8[:, 32:33], accb=[st8])
        ts(dve, st8, st8[:, 33:34], st8, st8[:, 32:33], 1.0 / D, 1e-6, ALU.mult, ALU.add)
        rsqrt(st8, st8[:, 33:34], st8, st8[:, 33:34])
        yield
        stt(h2, h2[:], x1, x1[:], st8[:, 33:34], reps, reps[:, 2, :], ALU.mult, ALU.mult, sb=[st8])
        tt(pool, h2, h2[:], h2, h2[:], reps, reps[:, 3, :], ALU.add)
        cp(act, h2b, h2b[:], h2, h2[:])
        yield
        for j in range(8):
            tr(tpb, tpb[:, j * 128:(j + 1) * 128], h2b, h2b[:, j * 128:(j + 1) * 128], L["identb_t"], identb)
        cp(act, h2T, h2T[:], tpb, tpb[:, :].rearrange("p (j n) -> p j n", j=8))
        for g4 in range(4):
            ps = nps()
            for gg in range(4):
                g = g4 * 4 + gg
                for j in range(8):
                    mm(ps, ps[:, gg * 128:(gg + 1) * 128], wqb, wqb[:, j, g * 128:(g + 1) * 128], h2T, h2T[:, j, :],
                       start=(j == 0), stop=(j == 7))
            cp(act, qpT, qpT[:, g4 * 4:(g4 + 1) * 4, :], ps, ps[:, :].rearrange("p (g n) -> p g n", g=4))
            yield
        yield
        for g4 in range(4):
            ps = nps()
            for gg in range(4):
                g = g4 * 4 + gg
                mm(ps, ps[:, gg * 128:(gg + 1) * 128], qpT, qpT[:, g, :], skb, skb[:, g % 2, :])
            cp(dve, sc, sc[:, g4 * 4:(g4 + 1) * 4, :], ps, ps[:, :].rearrange("p (g n) -> p g n", g=4))

        def top16(vals_b, vals, work_b, work, ov_b, ov, oi_b, oi):
            P.I(dve, lambda e: e.max(out=ov[:, 0:8], in_=vals), R=[vals_b], W=[ov_b])
            P.I(dve, lambda e: e.match_replace(out=work, in_to_replace=ov[:, 0:8], in_values=vals, imm_value=-1e30),
                R=[vals_b, ov_b], W=[work_b])
            P.I(dve, lambda e: e.max(out=ov[:, 8:16], in_=work), R=[work_b], W=[ov_b])
            P.I(dve, lambda e: e.max_index(out=oi[:, 0:8], in_max=ov[:, 0:8], in_values=vals), R=[vals_b, ov_b], W=[oi_b])
            P.I(dve, lambda e: e.max_index(out=oi[:, 8:16], in_max=ov[:, 8:16], in_values=vals), R=[vals_b, ov_b], W=[oi_b])
        for g in range(16):
            top16(sc, sc[:, g, :], scw, scw[:, g, :], sv, sv[:, g, :], si, si[:, g, :])
            if g % 2 == 1:
                yield
        yield
        sv4 = sv[:].rearrange("p (h k) n -> p h k n", k=2)
        candv = cand[:].rearrange("p g n -> p (g n)").rearrange("p (h n) -> p h n", h=8)
        candwv = candw[:].rearrange("p g n -> p (g n)").rearrange("p (h n) -> p h n", h=8)
        yield
        eqv = eq[:].rearrange("p g n -> p (g n)").rearrange("p (h a b) -> p h a b", h=8, a=16)
        tt(dve, cand, candv.rearrange("p h (a b) -> p h a b", a=16), sv, sv4[:, :, 0, :].unsqueeze(3).to_broadcast([128, 8, 16, 16]),
           sv, sv4[:, :, 1, :].unsqueeze(2).to_broadcast([128, 8, 16, 16]), ALU.add)
        for h in range(8):
            top16(cand, candv[:, h, :], candw, candwv[:, h, :], tv, tv[:, h, :], ti, ti[:, h, :])
            if h % 2 == 1:
                yield
        yield
        ts(dve, st8, st8[:, 0:8], tv, tv[:, :, 0], -1.0, None, ALU.mult)
        P.I(pool, lambda e: e.memset(st8[:, 8:16], 0.0), W=[st8])
        for h in range(8):
            actf(ge, ge[:, h, :], tv, tv[:, h, :], AF.Exp, bias=st8[:, h:h + 1], sb=[st8], accum=st8[:, 8 + h:9 + h], accb=[st8])
        yield
        P.I(dve, lambda e: e.reciprocal(out=st8[:, 16:24], in_=st8[:, 8:16]), R=[st8], W=[st8])
        tt(dve, gate, gate[:].rearrange("p (h n) -> p h n", h=8), ge, ge[:], st8, st8[:, 16:24].unsqueeze(2).to_broadcast([128, 8, 16]), ALU.mult)
        ts(dve, thi, thi[:], ti, ti[:], 4, None, ALU.logical_shift_right)
        yield
        ts(dve, tlo, tlo[:], ti, ti[:], 15, None, ALU.bitwise_and)
        cp(dve, thif, thif[:], thi, thi[:])
        cp(dve, tlof, tlof[:], tlo, tlo[:])
        yield
        cp(dve, sif, sif[:], si, si[:])
        sif4 = sif[:].rearrange("p (h k) n -> p h k n", k=2)
        io4 = iota16.unsqueeze(1).unsqueeze(1).to_broadcast([128, 8, 16, 16])
        yield
        for q_, (tf_b, kk_) in enumerate(((thif, 0), (tlof, 1))):
            tt(dve, eq, eqv, tf_b, tf_b[:].unsqueeze(3).to_broadcast([128, 8, 16, 16]), cst, io4, ALU.is_equal)
            tt(dve, eq, eqv, eq, eqv, sif, sif4[:, :, kk_, :].unsqueeze(2).to_broadcast([128, 8, 16, 16]), ALU.mult)
            P.I(dve, lambda e, q_=q_: e.tensor_reduce(out=sel[:, q_, :].rearrange("p (h n) -> p h n", h=8), in_=eqv, axis=AX.X, op=ALU.add),
                R=[eq], W=[sel])
        stt(sel, sel[:, 0, :], sel, sel[:, 0, :], 128.0, sel, sel[:, 1, :], ALU.mult, ALU.add)
        cp(dve, eidx, eidx[:], sel, sel[:, 0, :])
        yield
        if "eidx" in dbg_t and i < 1:
            dma(sp, Buf(None), dbg_t["eidx"][:, :], sel, sel[:, 0, :])
            dma(sp, Buf(None), dbg_t["gate"][:, :], gate, gate[:])
    def tailp(i, par, pump):
        x1, h2b, gate, eidx = x1s[par], h2bs[par], gates[par], eidxs[par]
        rows = slice(i * 128, (i + 1) * 128)
        ngrp = 128 // GS

        def gath(g):
            for k_ in range(GS):
                m = g * GS + k_
                t_ = uvs[g % NG][k_]
                P.D(pool, lambda e, t_=t_, m=m: e.indirect_dma_start(
                    out=t_[:], out_offset=None, in_=L["uv_bf"][:, :],
                    in_offset=bass.IndirectOffsetOnAxis(ap=eidx[:, m:m + 1].bitcast(U32), axis=0)), R=[eidx], W=[t_])

        def udots(g):
            av_, gl_ = avs[g % 2], gls[g % 2]
            P.I(pool, lambda e, av_=av_: e.memset(av_[:], 0.0), W=[av_], cost=120.0)
            for k_ in range(GS):
                t_ = uvs[g % NG][k_]
                pj = prods[(g * GS + k_) % NPB]
                tt(dve, pj, pj[:], t_, t_[:, 0:D], h2b, h2b[:], ALU.mult)
                actf(pj, pj[:], pj, pj[:], AF.Copy, accum=av_[:, k_:k_ + 1], accb=[av_])
            actf(gl_, gl_[:], av_, av_[:], AF.Gelu)

        def vaxpy(g):
            ms = slice(g * GS, (g + 1) * GS)
            gl_, ww_ = gls[g % 2], wws[g % 2]
            tt(dve, ww_, ww_[:], gl_, gl_[:], gate, gate[:, ms], ALU.mult)
            for k_ in range(GS):
                m = g * GS + k_
                t_ = uvs[g % NG][k_]
                dg = dgs[m % 4]
                ts(dve, dg, dg[:], L["identb_t"], identb, ww_[:, k_:k_ + 1], None, ALU.mult, sb=[ww_])
                for hf in range(2):
                    mm(pacc[hf], pacc[hf][:, :], dg, dg[:], t_, t_[:, D + hf * 512:D + (hf + 1) * 512], start=(m == 0), stop=(m == 127))
        gath(0); gath(1)
        udots(0)
        for g in range(ngrp):
            if g + 2 < ngrp:
                gath(g + 2)
            if g + 1 < ngrp:
                udots(g + 1)
            vaxpy(g)
            pump()
        for hf in range(2):
            hs_ = slice(hf * 512, (hf + 1) * 512)
            if "peer" in dbg_t and i < 2:
                cp(act, acc, acc[:, hs_], pacc[hf], pacc[hf][:, :])
                dma(sp, Buf(None), dbg_t["peer"][rows, hs_], acc, acc[:, hs_])
            tt(dve, acc, acc[:, hs_], pacc[hf], pacc[hf][:, :], reps, reps[:, 1, hs_], ALU.mult)
        tt(pool, acc, acc[:], acc, acc[:], x1, x1[:], ALU.add)
        dma(sp, yo, y_out[rows, :], acc, acc[:])

    def run_all(gen):
        for _ in gen:
            pass
    run_all(prefix(0, 0))
    for i in range(nt_lim):
        nxt = prefix(i + 1, (i + 1) % 2) if i + 1 < nt_lim else None

        def pump(nxt=nxt, k=L.get("pump_k", 2)):
            if nxt is None:
                return
            for _ in range(k):
                try:
                    next(nxt)
                except StopIteration:
                    return
        tailp(i, i % 2, pump)
        if nxt is not None:
            run_all(nxt)
    L["npool"][0] = 7
    P.pop()
    return L
```
